# Optimizing a Trainium2 kernel written in Bass

```python
import jax, jax.numpy as jnp
from jax import lax
import numpy as np


D_MODEL = 2048
BATCH = 8
SEQ = 2048
DEPTH = 2

GRID_W = 64
CTX_LEN = 256

HD = 64
H_A = 8
D_A = H_A * HD
H_B = 12
HKV_B = 4
D_B = H_B * HD
KV_B = HKV_B * HD
H_C = 6
NOPE_C = 128
ROPE_DIM = 64
QK_C = NOPE_C + ROPE_DIM
V_C = 128
D_C = H_C * V_C
Q_LORA = 768
KV_LORA = 512
D_MIX = D_A + D_B + D_C
N_IN = 3 * D_A + D_B + 2 * KV_B + Q_LORA + KV_LORA + ROPE_DIM + D_MIX

NA_ROWS = 8
NA_COLS = 16
SW_WINDOW = 128
SW_BLOCK = 128
Q_BLOCK = 128

ROPE_BASE = 10000.0
ADA_SCALE = 0.5
EPS = 1e-6
NEG_INF = -1e30

kernel_name = "hymba_grid_hybrid_dit_block"


def rms(x, w):
    xf = x.astype(jnp.float32)
    y = xf * lax.rsqrt(jnp.mean(xf * xf, axis=-1, keepdims=True) + EPS)
    return (y * w.astype(jnp.float32)).astype(x.dtype)


def axial_rope(L, dim, dtype):
    t = jnp.arange(L)
    row = (t // GRID_W).astype(jnp.float32)
    col = (t % GRID_W).astype(jnp.float32)
    n_freq = dim // 4
    inv = ROPE_BASE ** (-jnp.arange(n_freq, dtype=jnp.float32) / n_freq)
    ar = row[:, None] * inv
    ac = col[:, None] * inv
    ang = jnp.concatenate([ar, ar, ac, ac], axis=-1)
    return jnp.cos(ang).astype(dtype), jnp.sin(ang).astype(dtype)


def apply_rope2d(x, cos, sin):
    x1, x2, x3, x4 = jnp.split(x, 4, axis=-1)
    rot = jnp.concatenate([-x2, x1, -x4, x3], axis=-1)
    return x * cos[None, :, None, :] + rot * sin[None, :, None, :]


def split_points():
    sizes = (D_A, D_A, D_A, D_B, KV_B, KV_B, Q_LORA, KV_LORA, ROPE_DIM, D_MIX)
    return [int(p) for p in np.cumsum(sizes)[:-1]]


def project_stream(h, w_in, qn_a, kn_a, qn_b, kn_b, qa_norm, kva_norm, w_qb, w_kvb, qn_c, kn_c, rope):
    B, L, _ = h.shape
    u = h @ w_in
    qa, ka, va, qb, kb, vb, cq, ckv, kpe, z = jnp.split(u, split_points(), axis=-1)

    def heads(t, n):
        return t.reshape(B, L, n, -1)

    qa = rms(heads(qa, H_A), qn_a)
    ka = rms(heads(ka, H_A), kn_a)
    va = heads(va, H_A)
    qb = rms(heads(qb, H_B), qn_b)
    kb = rms(heads(kb, HKV_B), kn_b)
    vb = heads(vb, HKV_B)
    qc = rms(heads(rms(cq, qa_norm) @ w_qb, H_C), qn_c)
    kvc = heads(rms(ckv, kva_norm) @ w_kvb, H_C)
    kc = jnp.concatenate(
        [kvc[..., :NOPE_C], jnp.broadcast_to(kpe[:, :, None, :], (B, L, H_C, ROPE_DIM))], axis=-1)
    kc = rms(kc, kn_c)
    vc = kvc[..., NOPE_C:]
    if rope is not None:
        cos, sin = rope
        qb = apply_rope2d(qb, cos, sin)
        kb = apply_rope2d(kb, cos, sin)
        qc = jnp.concatenate([qc[..., :NOPE_C], apply_rope2d(qc[..., NOPE_C:], cos, sin)], axis=-1)
        kc = jnp.concatenate([kc[..., :NOPE_C], apply_rope2d(kc[..., NOPE_C:], cos, sin)], axis=-1)
    return (qa, ka, va, qb, kb, vb, qc, kc, vc, z)


def natten_latent(q, k, v, kc, vc, rpb):
    B, L, H, D = q.shape
    rows = L // GRID_W
    wr = min(NA_ROWS, rows)
    ncb = GRID_W // NA_COLS
    span = 2 * NA_COLS
    cb = np.arange(ncb)
    kcol = np.clip(cb * NA_COLS - NA_COLS // 2, 0, GRID_W - span)[:, None] + np.arange(span)
    qcol = cb[:, None] * NA_COLS + np.arange(NA_COLS)
    qstart = np.clip(qcol - NA_COLS // 2, 0, GRID_W - NA_COLS)
    col_ok = (kcol[:, None, :] >= qstart[:, :, None]) & (kcol[:, None, :] < qstart[:, :, None] + NA_COLS)
    dc_idx = (np.clip(kcol[:, None, :] - qcol[:, :, None], -(NA_COLS - 1), NA_COLS - 1)
              + NA_COLS - 1).astype(np.int32)
    qg = q.reshape(B, rows, ncb, NA_COLS, H, D)
    kg = k.reshape(B, rows, GRID_W, H, D)[:, :, kcol]
    vg = v.reshape(B, rows, GRID_W, H, D)[:, :, kcol]
    scale = D ** -0.5
    n_loc = wr * span

    def row_fn(r):
        rs = jnp.clip(r - wr // 2, 0, rows - wr)
        qr = lax.dynamic_index_in_dim(qg, r, axis=1, keepdims=False)
        kr = lax.dynamic_slice_in_dim(kg, rs, wr, axis=1)
        vr = lax.dynamic_slice_in_dim(vg, rs, wr, axis=1)
        dr_idx = rs + jnp.arange(wr) - r + NA_ROWS - 1
        bias = rpb[:, dr_idx[None, None, :, None], dc_idx[:, :, None, :]]
        s_loc = jnp.einsum('bnqhd,brnkhd->bhnqrk', qr, kr,
                           preferred_element_type=jnp.float32) * scale + bias
        s_loc = jnp.where(col_ok[:, :, None, :], s_loc, NEG_INF)
        s_ctx = jnp.einsum('bnqhd,bchd->bhnqc', qr, kc, preferred_element_type=jnp.float32) * scale
        s = jnp.concatenate([s_loc.reshape(B, H, ncb, NA_COLS, n_loc), s_ctx], axis=-1)
        p = jax.nn.softmax(s, axis=-1).astype(v.dtype)
        p_loc = p[..., :n_loc].reshape(B, H, ncb, NA_COLS, wr, span)
        o = (jnp.einsum('bhnqrk,brnkhd->bnqhd', p_loc, vr)
             + jnp.einsum('bhnqc,bchd->bnqhd', p[..., n_loc:], vc))
        return o.reshape(B, GRID_W, H, D)

    o = lax.map(row_fn, jnp.arange(rows))
    return o.transpose(1, 0, 2, 3, 4).reshape(B, L, H, D)


def swa_latent(q, k, v, kc, vc, sink):
    B, L, H, D = q.shape
    Hkv = k.shape[2]
    G = H // Hkv
    Lc = kc.shape[1]
    nb = L // SW_BLOCK
    span = SW_BLOCK + 2 * SW_WINDOW
    pad = ((0, 0), (SW_WINDOW, SW_WINDOW), (0, 0), (0, 0))
    kp = jnp.pad(k, pad)
    vp = jnp.pad(v, pad)
    qb = q.reshape(B, nb, SW_BLOCK, Hkv, G, D).transpose(1, 0, 2, 3, 4, 5)
    snk = sink.reshape(Hkv, G).astype(jnp.float32)[None, :, :, None, None]
    scale = D ** -0.5

    def blk(args):
        i, qi = args
        k0 = i * SW_BLOCK
        ki = lax.dynamic_slice_in_dim(kp, k0, span, axis=1)
        vi = lax.dynamic_slice_in_dim(vp, k0, span, axis=1)
        qpos = k0 + jnp.arange(SW_BLOCK)
        kpos = k0 - SW_WINDOW + jnp.arange(span)
        ok = (kpos >= 0) & (kpos < L) & (jnp.abs(qpos[:, None] - kpos[None, :]) <= SW_WINDOW)
        s_loc = jnp.einsum('bqkgd,bjkd->bkgqj', qi, ki, preferred_element_type=jnp.float32) * scale
        s_loc = jnp.where(ok, s_loc, NEG_INF)
        s_ctx = jnp.einsum('bqkgd,bckd->bkgqc', qi, kc, preferred_element_type=jnp.float32) * scale
        s = jnp.concatenate([s_loc, s_ctx, jnp.broadcast_to(snk, s_loc.shape[:-1] + (1,))], axis=-1)
        p = jax.nn.softmax(s, axis=-1).astype(v.dtype)
        o = (jnp.einsum('bkgqj,bjkd->bqkgd', p[..., :span], vi)
             + jnp.einsum('bkgqc,bckd->bqkgd', p[..., span:span + Lc], vc))
        return o.reshape(B, SW_BLOCK, H, D)

    o = lax.map(blk, (jnp.arange(nb), qb))
    return o.transpose(1, 0, 2, 3, 4).reshape(B, L, H, D)


def mla_latent(q, k, v, kc, vc):
    B, L, H, D = q.shape
    Dv = v.shape[-1]
    nb = L // Q_BLOCK
    k_all = jnp.concatenate([kc, k], axis=1)
    v_all = jnp.concatenate([vc, v], axis=1)
    qb = q.reshape(B, nb, Q_BLOCK, H, D).transpose(1, 0, 2, 3, 4)
    scale = D ** -0.5

    def blk(qi):
        s = jnp.einsum('bqhd,bkhd->bhqk', qi, k_all, preferred_element_type=jnp.float32) * scale
        p = jax.nn.softmax(s, axis=-1).astype(v.dtype)
        return jnp.einsum('bhqk,bkhd->bqhd', p, v_all)

    o = lax.map(blk, qb)
    return o.transpose(1, 0, 2, 3, 4).reshape(B, L, H, Dv)


def ctx_attn(q, k, v, sink=None):
    B, Lc, H, D = q.shape
    Hkv = k.shape[2]
    G = H // Hkv
    qg = q.reshape(B, Lc, Hkv, G, D)
    s = jnp.einsum('bqkgd,bckd->bkgqc', qg, k, preferred_element_type=jnp.float32) * (D ** -0.5)
    if sink is not None:
        snk = sink.reshape(Hkv, G).astype(jnp.float32)[None, :, :, None, None]
        s = jnp.concatenate([s, jnp.broadcast_to(snk, s.shape[:-1] + (1,))], axis=-1)
    p = jax.nn.softmax(s, axis=-1)[..., :Lc].astype(v.dtype)
    return jnp.einsum('bkgqc,bckd->bqkgd', p, v).reshape(B, Lc, H, v.shape[-1])


def merge_out(o_a, o_b, o_c, z, w_out):
    B, L = o_a.shape[:2]
    y = jnp.concatenate([o_a.reshape(B, L, -1), o_b.reshape(B, L, -1), o_c.reshape(B, L, -1)], axis=-1)
    return (y * jax.nn.silu(z)) @ w_out


def setup_inputs(seed: int = 0) -> dict:
    key = jax.random.key(seed)
    ks = jax.random.split(key, 21)
    f32 = jnp.float32

    def nrm(k, shape, s):
        return jax.random.normal(k, shape, f32) * s

    def gain(k, shape):
        return 1.0 + 0.02 * jax.random.normal(k, shape, f32)

    return {
        "x": nrm(ks[0], (BATCH, SEQ, D_MODEL), 1.0),
        "c": nrm(ks[1], (BATCH, D_MODEL), 1.0),
        "ctx": nrm(ks[2], (BATCH, CTX_LEN, D_MODEL), 1.0),
        "c_ctx": nrm(ks[3], (D_MODEL,), 1.0),
        "norm_w": gain(ks[4], (DEPTH, D_MODEL)),
        "w_ada": nrm(ks[5], (DEPTH, D_MODEL, 3 * D_MODEL), ADA_SCALE * D_MODEL ** -0.5),
        "b_ada": nrm(ks[6], (DEPTH, 3 * D_MODEL), 0.01),
        "w_in": nrm(ks[7], (DEPTH, D_MODEL, N_IN), D_MODEL ** -0.5),
        "qn_a": gain(ks[8], (DEPTH, HD)),
        "kn_a": gain(ks[9], (DEPTH, HD)),
        "rpb_a": nrm(ks[10], (DEPTH, H_A, 2 * NA_ROWS - 1, 2 * NA_COLS - 1), 0.1),
        "qn_b": gain(ks[11], (DEPTH, HD)),
        "kn_b": gain(ks[12], (DEPTH, HD)),
        "sink_b": nrm(ks[13], (DEPTH, H_B), 0.5),
        "qa_norm": gain(ks[14], (DEPTH, Q_LORA)),
        "kva_norm": gain(ks[15], (DEPTH, KV_LORA)),
        "w_qb": nrm(ks[16], (DEPTH, Q_LORA, H_C * QK_C), Q_LORA ** -0.5),
        "w_kvb": nrm(ks[17], (DEPTH, KV_LORA, H_C * (NOPE_C + V_C)), KV_LORA ** -0.5),
        "qn_c": gain(ks[18], (DEPTH, QK_C)),
        "kn_c": gain(ks[19], (DEPTH, QK_C)),
        "w_out": nrm(ks[20], (DEPTH, D_MIX, D_MODEL), D_MIX ** -0.5),
    }


def reference(x, c, ctx, c_ctx, norm_w, w_ada, b_ada, w_in, qn_a, kn_a, rpb_a, qn_b, kn_b, sink_b,
              qa_norm, kva_norm, w_qb, w_kvb, qn_c, kn_c, w_out):
    B, L, _ = x.shape
    cos, sin = axial_rope(L, ROPE_DIM, x.dtype)
    sc = jax.nn.silu(c)
    scc = jax.nn.silu(c_ctx)
    xc = ctx
    for l in range(DEPTH):
        last = l == DEPTH - 1
        shift, scale, gate = jnp.split(sc @ w_ada[l] + b_ada[l], 3, axis=-1)
        shift_c, scale_c, gate_c = jnp.split(scc @ w_ada[l] + b_ada[l], 3, axis=-1)
        h = rms(x, norm_w[l]) * (1.0 + scale[:, None]) + shift[:, None]
        hc = rms(xc, norm_w[l]) * (1.0 + scale_c) + shift_c
        weights = (w_in[l], qn_a[l], kn_a[l], qn_b[l], kn_b[l], qa_norm[l], kva_norm[l],
                   w_qb[l], w_kvb[l], qn_c[l], kn_c[l])
        qa, ka, va, qb, kb, vb, qc, kc, vc, z = project_stream(h, *weights, rope=(cos, sin))
        qa_c, ka_c, va_c, qb_c, kb_c, vb_c, qc_c, kc_c, vc_c, z_c = project_stream(hc, *weights, rope=None)
        o_a = natten_latent(qa, ka, va, ka_c, va_c, rpb_a[l])
        o_b = swa_latent(qb, kb, vb, kb_c, vb_c, sink_b[l])
        o_c = mla_latent(qc, kc, vc, kc_c, vc_c)
        x = x + gate[:, None] * merge_out(o_a, o_b, o_c, z, w_out[l])
        if not last:
            oa_c = ctx_attn(qa_c, ka_c, va_c)
            ob_c = ctx_attn(qb_c, kb_c, vb_c, sink_b[l])
            oc_c = ctx_attn(qc_c, kc_c, vc_c)
            xc = xc + gate_c * merge_out(oa_c, ob_c, oc_c, z_c, w_out[l])
    return x
```

```python
import contextlib
import numpy as np
import concourse.bass as bass
import concourse.mybir as mybir
from concourse.bass_utils import run_bass_kernel_spmd

F32 = mybir.dt.float32
BF16 = mybir.dt.bfloat16
AF = mybir.ActivationFunctionType
ALU = mybir.AluOpType

CH = 30000
DCH = 1800
D = 2048
T = 2304
EPS = 1e-6
N_IN = 6208
TB = [(0, 256), (256, 512), (768, 512), (1280, 512), (1792, 512)]


class Buf:
    def __init__(self, t, name, is_dram=False, is_psum=False):
        self.t = t
        self.name = name
        self.is_dram = is_dram
        self.is_psum = is_psum
        self.writers = {}
        self.readers = {}

    def __getitem__(self, k):
        return self.t[k]


class Prog:
    ENG = ["pe", "act", "dve", "pool", "sp"]

    def __init__(self, nc):
        self.nc = nc
        self.ops = {e: [] for e in self.ENG}
        self.cnt = {e: 0 for e in self.ENG}
        self.seen = {e: {} for e in self.ENG}
        self.sems = {}
        self.ndma = {}
        self.latest = {}
        self.stack = None
        self.uid = 0

    def sb(self, name, shape, dtype):
        self.uid += 1
        t = self.stack.enter_context(self.nc.sbuf_tensor("%s_u%d" % (name, self.uid), list(shape), dtype))
        return Buf(t, name)

    def gsb(self, name, shape, dtype):
        return Buf(self.nc.alloc_sbuf_tensor(name, list(shape), dtype), name)

    def ps(self, name, shape, dtype=F32):
        return Buf(self.nc.alloc_psum_tensor(name, list(shape), dtype), name, is_psum=True)

    def dram(self, name, shape, dtype, kind="Internal"):
        return Buf(self.nc.dram_tensor(name, list(shape), dtype, kind=kind), name, is_dram=True)

    def sem(self, key):
        if key not in self.sems:
            self.sems[key] = self.nc.alloc_semaphore("s%d" % len(self.sems))
        return self.sems[key]

    def _collect(self, E, reads, writes, own_keys):
        deps = {}

        def add(k, v):
            if deps.get(k, 0) < v:
                deps[k] = v
        for r in reads:
            for k, v in r.writers.items():
                if E == "pe" and k[:2] == ("eng", "pe"):
                    continue
                add(k, v)
            if r.is_psum:
                for k, v in r.readers.items():
                    if k[:2] not in own_keys:
                        add(k, v)
        for w in writes:
            for k, v in w.writers.items():
                if k[:2] not in own_keys:
                    add(k, v)
            for k, v in w.readers.items():
                if k[:2] not in own_keys:
                    add(k, v)
        waits = []
        for k, v in deps.items():
            if self.seen[E].get(k, 0) >= v:
                continue
            self.seen[E][k] = v
            waits.append((self.sem(k), v))
        return waits

    def _commit(self, ev, reads, writes):
        k, v = ev
        self.latest[k] = v
        for r in reads:
            if r.readers.get(k, 0) < v:
                r.readers[k] = v
        for w in writes:
            if w.writers.get(k, 0) < v:
                w.writers[k] = v
            w.readers = {}

    def op(self, E, fn, reads=(), writes=()):
        waits = self._collect(E, reads, writes, (("eng", E),))
        idx = self.cnt[E]
        self.cnt[E] += 1
        key = ("eng", E, idx // CH)
        val = idx % CH + 1
        self.ops[E].append((waits, fn, self.sem(key), 1))
        self._commit((key, val), reads, writes)

    def dma(self, Q, fn, src, dst):
        side = dst if src.is_dram else src
        n = self.ndma.get(side.name, 0)
        key = ("dma", side.name, n // DCH)
        waits = self._collect(Q, [src], [dst], (("dma", side.name),))
        val = 16 * (n % DCH + 1)
        self.ndma[side.name] = n + 1
        self.ops[Q].append((waits, fn, self.sem(key), 16))
        self._commit((key, val), [src], [dst])

    def barrier(self):
        for E in self.ENG:
            waits = []
            for k, v in self.latest.items():
                if self.seen[E].get(k, 0) >= v:
                    continue
                self.seen[E][k] = v
                waits.append((self.sem(k), v))
            if waits:
                self.ops[E].append((waits, None, None, 0))

    @contextlib.contextmanager
    def scope(self):
        old = self.stack
        with contextlib.ExitStack() as st:
            self.stack = st
            yield
            self.barrier()
        self.stack = old

    def emit(self):
        nc = self.nc
        ops = self.ops

        def run(eng, lst):
            for waits, fn, sem, inc in lst:
                for s, v in waits:
                    eng.wait_ge(s, v)
                if fn is not None:
                    fn(eng).then_inc(sem, inc)

        with nc.Block() as block:
            @block.sync
            def _(e):
                run(e, ops["sp"])

            @block.tensor
            def _(e):
                run(e, ops["pe"])

            @block.scalar
            def _(e):
                run(e, ops["act"])

            @block.vector
            def _(e):
                run(e, ops["dve"])

            @block.gpsimd
            def _(e):
                run(e, ops["pool"])


class Rot:
    def __init__(self, bufs):
        self.bufs = bufs
        self.i = 0

    def next(self):
        b = self.bufs[self.i % len(self.bufs)]
        self.i += 1
        return b


def build(nlayers=2, dbg=(), stop=None):
    nc = bass.Bass("TRN2", target_bir_lowering=False)
    P = Prog(nc)
    EI = "ExternalInput"

    def scr(name, shape, dtype):
        return P.dram(name, shape, dtype, kind=("ExternalOutput" if name in dbg else "Internal"))

    x_d = P.dram("x", [2048, D], F32, EI)
    ctx_d = P.dram("ctx", [256, D], F32, EI)
    cc_d = P.dram("cc", [128, 32], F32, EI)
    w_ada_d = P.dram("w_ada", [2, D, 6144], F32, EI)
    b_ada_d = P.dram("b_ada", [2, 6144], F32, EI)
    w_in_d = P.dram("w_in", [2, D, N_IN], F32, EI)
    w_qb_d = P.dram("w_qb", [2, 768, 1152], F32, EI)
    w_kvb_d = P.dram("w_kvb", [2, 512, 1536], F32, EI)
    w_out_d = P.dram("w_out", [2, D, D], F32, EI)
    cols_d = P.dram("cols", [2, 128, 34], F32, EI)
    sink_d = P.dram("sink", [2, 64, 12], F32, EI)
    gb_d = P.dram("gb", [2, 128, 8 * 896], F32, EI)
    cmat_d = P.dram("cmat", [128, 4, 128], F32, EI)
    cossin_d = P.dram("cossin", [128, 2, 2048], F32, EI)
    namask_d = P.dram("namask", [128, 64], F32, EI)
    swamask_d = P.dram("swamask", [128, 2, 384], F32, EI)
    out_d = P.dram("out", [2048, D], F32, "ExternalOutput")

    WINs = [scr("WIN%d" % i, [D, N_IN], BF16) for i in range(2)]
    WQBs = [scr("WQB%d" % i, [768, 1152], BF16) for i in range(2)]
    WKVBs = [scr("WKVB%d" % i, [512, 1536], BF16) for i in range(2)]
    WOUTs = [scr("WOUT%d" % i, [D, D], BF16) for i in range(2)]
    MODs = [scr("MOD" if i == 0 else "MOD1", [2, 6144], F32) for i in range(2)]
    QAT = scr("QAT", [512, T], BF16)
    KAT = scr("KAT", [512, T], BF16)
    VA = scr("VA", [T, 512], BF16)
    QBT = scr("QBT", [768, T], BF16)
    KBT = scr("KBT", [256, T], BF16)
    VB = scr("VB", [T, 256], BF16)
    CQN = scr("CQN", [768, T], BF16)
    CKVN = scr("CKVN", [512, T], BF16)
    KPE = scr("KPE", [64, T], F32)
    SZ = scr("SZ", [D, T], BF16)
    QCN = scr("QCN", [768, T], BF16)
    QCR = scr("QCR", [384, T], BF16)
    KCN = scr("KCN", [768, T], BF16)
    KCR = scr("KCR", [384, T], BF16)
    VC = scr("VC", [T, 768], BF16)
    YT = scr("YT", [D, T], BF16)
    X1 = scr("X1", [2048, D], F32)
    XC1 = scr("XC1", [256, D], F32)

    psb = [P.ps("ps%d" % i, [128, 512], F32) for i in range(8)]

    ident = P.gsb("ident", [128, 128], F32)
    bd64 = P.gsb("bd64", [128, 128], BF16)
    ones = P.gsb("ones", [128, 128], BF16)
    rotm = P.gsb("rotm", [128, 128], BF16)
    cos_t = P.gsb("cos_t", [128, 2048], F32)
    sin_t = P.gsb("sin_t", [128, 2048], F32)
    namask = P.gsb("namask_s", [128, 64], F32)
    swamask = P.gsb("swamask_s", [128, 2, 384], BF16)
    scT = P.gsb("scT", [128, 32], F32)
    cols = P.gsb("cols_s", [128, 34], F32)
    modcol = P.gsb("modcol", [128, 2, 32], F32)
    gcol = P.gsb("gcol", [128, 2, 16], F32)
    esink = P.gsb("esink", [64, 12], F32)
    ones2 = P.gsb("ones2", [1, 2], F32)
    epsc = P.gsb("epsc", [128, 1], F32)
    esr = P.gsb("esr", [33, 12, 128], BF16)
    selr = P.gsb("selr", [33, 128], BF16)
    onesf = P.gsb("onesf", [64, 128], F32)
    eshi = P.gsb("eshi", [64, 12], BF16)
    eslo = P.gsb("eslo", [64, 12], F32)

    with P.scope():
        cm = P.sb("cm", [128, 4, 128], F32)
        swf = P.sb("swf", [128, 2, 384], F32)
        cct = P.sb("cct", [128, 32], F32)
        P.dma("sp", lambda e: e.dma_start(out=cm[:], in_=cmat_d[:]), cmat_d, cm)
        P.dma("sp", lambda e: e.dma_start(out=swf[:], in_=swamask_d[:]), swamask_d, swf)
        P.dma("sp", lambda e: e.dma_start(out=cct[:], in_=cc_d[:]), cc_d, cct)
        P.dma("sp", lambda e: e.dma_start(out=cos_t[:], in_=cossin_d[:, 0, :]), cossin_d, cos_t)
        P.dma("sp", lambda e: e.dma_start(out=sin_t[:], in_=cossin_d[:, 1, :]), cossin_d, sin_t)
        P.dma("sp", lambda e: e.dma_start(out=namask[:], in_=namask_d[:]), namask_d, namask)
        P.op("dve", lambda e: e.tensor_copy(out=ident[:], in_=cm[:, 0, :]), [cm], [ident])
        P.op("dve", lambda e: e.tensor_copy(out=bd64[:], in_=cm[:, 1, :]), [cm], [bd64])
        P.op("dve", lambda e: e.tensor_copy(out=ones[:], in_=cm[:, 2, :]), [cm], [ones])
        P.op("dve", lambda e: e.tensor_copy(out=rotm[:], in_=cm[:, 3, :]), [cm], [rotm])
        P.op("dve", lambda e: e.tensor_copy(out=swamask[:], in_=swf[:]), [swf], [swamask])
        P.op("dve", lambda e: e.memset(ones2[:], 1.0), [], [ones2])
        P.op("dve", lambda e: e.memset(epsc[:], EPS), [], [epsc])
        P.op("dve", lambda e: e.memset(onesf[:], 1.0), [], [onesf])
        P.op("dve", lambda e: e.memset(selr[:], 0.0), [], [selr])
        P.op("dve", lambda e: e.memset(selr[0:1, 64:128], 1.0), [], [selr])
        P.op("dve", lambda e: e.memset(selr[32:33, 64:128], 1.0), [], [selr])
        P.op("dve", lambda e: e.memset(esr[:], 0.0), [], [esr])
        P.op("act", lambda e: e.activation(out=scT[:], in_=cct[:], func=AF.Silu), [cct], [scT])

    def make_bg(l, ps_bank, CW=3104, store_q="pool", which=("qb", "kvb", "out"), ada=True, extra=(), load_q="sp"):
        stg = [P.sb("stg%d" % i, [128, CW], F32) for i in range(2)]
        bft = [P.sb("bft%d" % i, [128, CW], BF16) for i in range(2)]
        wts = [P.sb("wada%d" % i, [128, 16, 256], F32) for i in range(2)]
        badas = [P.sb("bada%d" % i, [1, 256], F32) for i in range(2)]
        mods = [P.sb("modsb%d" % i, [2, 256], F32) for i in range(2)]
        cast = []
        for (ll, wh) in list(extra) + [(l, which)]:
            for (nm, src, dst, R, C) in [("in", w_in_d, WINs[ll], D, N_IN), ("qb", w_qb_d, WQBs[ll], 768, 1152),
                                         ("kvb", w_kvb_d, WKVBs[ll], 512, 1536), ("out", w_out_d, WOUTs[ll], D, D)]:
                if nm not in wh:
                    continue
                for rc in range(R // 128):
                    for c0 in range(0, C, CW):
                        cast.append((ll, src, dst, rc, c0, min(C, c0 + CW) - c0))

        def cast_stages(k, ll, src, dst, rc, c0, cw):
            st = stg[k % 2]
            bt = bft[k % 2]
            h = cw // 2

            def s0():
                P.dma(load_q if load_q != "alt" else "sp", lambda e: e.dma_start(out=st[:, :cw], in_=src[ll, rc * 128:(rc + 1) * 128, c0:c0 + cw]), src, st)

            def s1():
                P.op("pool", lambda e: e.tensor_copy(out=bt[:, :h], in_=st[:, :h]), [st], [bt])
                P.op("dve", lambda e: e.tensor_copy(out=bt[:, h:cw], in_=st[:, h:cw]), [st], [bt])

            def s2():
                P.dma(store_q, lambda e: e.dma_start(out=dst[rc * 128:(rc + 1) * 128, c0:c0 + cw], in_=bt[:, :cw]), bt, dst)
            return (s0, s1, s2)

        def ada_stages(k):
            w = wts[k % 2]
            bada = badas[k % 2]
            md = mods[k % 2]
            c0 = k * 256

            def s0():
                lq = load_q if load_q != "alt" else ("sp" if k % 2 == 0 else "act")
                P.dma(lq, lambda e: e.dma_start(out=w[:], in_=w_ada_d[l, :, c0:c0 + 256].rearrange("(c p) n -> p c n", p=128)), w_ada_d, w)
                P.dma(lq, lambda e: e.dma_start(out=bada[:], in_=b_ada_d[l:l + 1, c0:c0 + 256]), b_ada_d, bada)

            def s1():
                pm = ps_bank
                for kc in range(16):
                    P.op("pe", lambda e, kc=kc: e.matmul(pm[0:2, 0:256], lhsT=scT[:, 2 * kc:2 * kc + 2], rhs=w[:, kc, :], start=(kc == 0), stop=False), [scT, w], [pm])
                P.op("pe", lambda e: e.matmul(pm[0:2, 0:256], lhsT=ones2[0:1, 0:2], rhs=bada[0:1, :], start=False, stop=True), [ones2, bada], [pm])
                P.op("act", lambda e: e.activation(out=md[0:2, :], in_=pm[0:2, 0:256], func=AF.Copy), [pm], [md])

            def s2():
                P.dma("pool", lambda e: e.dma_start(out=MODs[l][:, c0:c0 + 256], in_=md[0:2, :]), md, MODs[l])
            return (s0, s1, s2)

        def lagged(stages):
            ticks = []
            n = len(stages)
            for t in range(n + 2):
                def tick(t=t):
                    if t < n:
                        stages[t][0]()
                    if 0 <= t - 1 < n:
                        stages[t - 1][1]()
                    if 0 <= t - 2 < n:
                        stages[t - 2][2]()
                ticks.append(tick)
            return ticks
        ct = lagged([cast_stages(k, *c) for k, c in enumerate(cast)])
        at = lagged([ada_stages(k) for k in range(24)]) if ada else []
        jobs = []
        while ct or at:
            for _ in range(3):
                if ct:
                    jobs.append(ct.pop(0))
            if at:
                jobs.append(at.pop(0))
        return jobs

    def ada_tail(l):
        with P.scope():
            nwt = P.sb("nwt", [128, 2, 16], F32)
            snk = P.sb("snk", [64, 12], F32)
            P.dma("sp", lambda e: e.dma_start(out=cols[:], in_=cols_d[l]), cols_d, cols)
            P.dma("sp", lambda e: e.dma_start(out=snk[:], in_=sink_d[l]), sink_d, snk)
            P.op("act", lambda e: e.activation(out=esink[:], in_=snk[:], func=AF.Exp), [snk], [esink])
            P.op("dve", lambda e: e.tensor_copy(out=eshi[:], in_=esink[:]), [esink], [eshi])
            P.op("dve", lambda e: e.tensor_tensor(out=eslo[:], in0=esink[:], in1=eshi[:], op=ALU.subtract), [esink, eshi], [eslo])
            for hh in range(12):
                P.op("dve", lambda e, hh=hh: e.tensor_scalar(out=esr[0:1, hh, :], in0=onesf[0:1, :], scalar1=eshi[0:1, hh:hh + 1], scalar2=None, op0=ALU.mult), [onesf, eshi], [esr])
                P.op("dve", lambda e, hh=hh: e.tensor_scalar(out=esr[32:33, hh, :], in0=onesf[32:33, :], scalar1=eslo[32:33, hh:hh + 1], scalar2=None, op0=ALU.mult), [onesf, eslo], [esr])
            for s_ in range(2):
                P.dma("sp", lambda e, s_=s_: e.dma_start(out=modcol[:, s_, :], in_=MODs[l][s_, 0:4096].rearrange("(c p) -> p c", p=128), allow_slow_non_contiguous=True), MODs[l], modcol)
            for s_ in range(2):
                P.op("dve", lambda e, s_=s_: e.tensor_scalar(out=nwt[:, s_, :], in0=modcol[:, s_, 16:32], scalar1=1.0, scalar2=None, op0=ALU.add), [modcol], [nwt])
                P.op("dve", lambda e, s_=s_: e.tensor_tensor(out=gcol[:, s_, :], in0=nwt[:, s_, :], in1=cols[:, 18:34], op=ALU.mult), [nwt, cols], [gcol])

    def phase_cast_ada(l):
        with P.scope():
            for j in make_bg(l, psb[7], CW=N_IN, store_q="act", load_q="alt", which=(("qb", "kvb") if (l == 0 and bg_next) else ("qb", "kvb", "out"))):
                j()
        ada_tail(l)

    def rope_tail(P_, qn, M, t0, n, obf, f32p, psC):
        pr = psC.next()
        P.op("pe", lambda e: e.matmul(pr[0:M, :n], lhsT=rotm[0:M, 0:M], rhs=qn[0:M, :n], start=True, stop=True), [rotm, qn], [pr])
        t1 = f32p.next()
        t2 = f32p.next()
        c0 = t0 - 256
        e1 = "dve" if M == 128 else "pool"
        P.op(e1, lambda e: e.tensor_tensor(out=t1[0:M, :n], in0=qn[0:M, :n], in1=cos_t[0:M, c0:c0 + n], op=ALU.mult), [qn, cos_t], [t1])
        P.op("dve", lambda e: e.tensor_tensor(out=t2[0:M, :n], in0=pr[0:M, :n], in1=sin_t[0:M, c0:c0 + n], op=ALU.mult), [pr, sin_t], [t2])
        o = obf.next()
        P.op(e1, lambda e: e.tensor_tensor(out=o[0:M, :n], in0=t1[0:M, :n], in1=t2[0:M, :n], op=ALU.add), [t1, t2], [o])
        return o

    def rstd_from(pss, M, n, scale, rsp):
        rs = rsp.next()
        P.op("act", lambda e: e.activation(out=rs[0:M, :n], in_=pss[0:M, :n], func=AF.Ln, scale=scale, bias=epsc[0:M, :]), [pss, epsc], [rs])
        P.op("act", lambda e: e.activation(out=rs[0:M, :n], in_=rs[0:M, :n], func=AF.Exp, scale=-0.5), [rs], [rs])
        return rs

    def phase_B(l):
        with P.scope():
            hT = P.sb("hT", [128, 16, T], BF16)
            hT2 = Buf(hT.t, "hT2")
            with P.scope():
                xts = Rot([P.sb("xt%d" % i, [128, D], F32) for i in range(2)])
                xns = Rot([P.sb("xn%d" % i, [128, D], F32) for i in range(2)])
                junk = P.sb("junk", [128, D], BF16)
                sss = Rot([P.sb("ss%d" % i, [128, 1], F32) for i in range(2)])
                pT = Rot(psb[0:4])
                for tt in range(18):
                    s = 1 if tt < 2 else 0
                    if l == 0:
                        src = ctx_d if tt < 2 else x_d
                    else:
                        src = XC1 if tt < 2 else X1
                    r0 = tt * 128 if tt < 2 else (tt - 2) * 128
                    xt = xts.next()
                    xn = xns.next()
                    ss = sss.next()
                    P.dma("sp", lambda e, xt=xt, src=src, r0=r0: e.dma_start(out=xt[:], in_=src[r0:r0 + 128, :]), src, xt)
                    P.op("act", lambda e, xt=xt, ss=ss: e.activation(out=junk[:], in_=xt[:], func=AF.Square, accum_out=ss[:]), [xt], [junk, ss])
                    P.op("act", lambda e, ss=ss: e.activation(out=ss[:], in_=ss[:], func=AF.Sqrt, scale=1.0 / D, bias=EPS), [ss], [ss])
                    P.op("dve", lambda e, ss=ss: e.reciprocal(out=ss[:], in_=ss[:]), [ss], [ss])
                    P.op("dve", lambda e, xt=xt, xn=xn, ss=ss: e.tensor_scalar(out=xn[:], in0=xt[:], scalar1=ss[:], scalar2=None, op0=ALU.mult), [xt, ss], [xn])
                    for g4 in range(4):
                        pb = pT.next()
                        for c4 in range(4):
                            c = g4 * 4 + c4
                            P.op("pe", lambda e, pb=pb, xn=xn, c=c, c4=c4: e.transpose(out=pb[:, c4 * 128:(c4 + 1) * 128], in_=xn[:, c * 128:(c + 1) * 128], identity=ident[:]), [xn, ident], [pb])
                        for c4 in range(4):
                            c = g4 * 4 + c4
                            if g4 % 2 == 0:
                                P.op("act", lambda e, pb=pb, c=c, c4=c4, s=s, tt=tt: e.activation(out=hT[:, c, tt * 128:(tt + 1) * 128], in_=pb[:, c4 * 128:(c4 + 1) * 128], func=AF.Identity, scale=gcol[:, s, c:c + 1], bias=modcol[:, s, c:c + 1]), [pb, gcol, modcol], [hT])
                            else:
                                P.op("dve", lambda e, pb=pb, c=c, c4=c4, s=s, tt=tt: e.tensor_scalar(out=hT[:, c, tt * 128:(tt + 1) * 128], in0=pb[:, c4 * 128:(c4 + 1) * 128], scalar1=gcol[:, s, c:c + 1], scalar2=modcol[:, s, c:c + 1], op0=ALU.mult, op1=ALU.add), [pb, gcol, modcol], [hT2])
            if stop == "B1":
                return
            with P.scope():
                wts = Rot([P.sb("wt%d" % i, [128, 16, 768], BF16) for i in range(2)])
                sqp = Rot([P.sb("sq%d" % i, [128, 512], BF16) for i in range(3)])
                rsp = Rot([P.sb("rs%d" % i, [128, 512], F32) for i in range(3)])
                obf = Rot([P.sb("ob%d" % i, [128, 512], BF16) for i in range(6)])
                f32p = Rot([P.sb("f32_%d" % i, [128, 512], F32) for i in range(4)])
                raw = P.sb("raw", [128, 6, 512], F32)
                psA = Rot(psb[0:3])
                psB = Rot(psb[3:5])
                psC = Rot(psb[5:7])

                wstg = Rot([P.sb("wstg%d" % i, [128, 16, 128], F32) for i in range(3)])
                wgroups = [(0, 512), (512, 512), (1024, 512), (1536, 768), (2304, 256), (2560, 256), (2816, 768),
                           (3584, 512), (4096, 64)] + [(4160 + i * 512, 512) for i in range(4)]
                wcache = {}
                wticks = []

                def issue_w(idx, spread):
                    col0, ncols = wgroups[idx]
                    wt = wts.next()
                    subs = [(c, min(128, ncols - c)) for c in range(0, ncols, 128)]
                    sts = {}

                    def dma(k):
                        c, cw = subs[k]
                        st = wstg.next()
                        sts[k] = st
                        P.dma("sp", lambda e: e.dma_start(out=st[:, :, :cw], in_=w_in_d[l, :, col0 + c:col0 + c + cw].rearrange("(c p) n -> p c n", p=128)), w_in_d, st)

                    def cast(k):
                        c, cw = subs[k]
                        st = sts[k]
                        P.op("dve", lambda e: e.tensor_copy(out=wt[:, :, c:c + cw], in_=st[:, :, :cw]), [st], [wt])

                    n = len(subs)
                    for t in range(n + 3):
                        def tick(t=t):
                            if t - 3 >= 0:
                                cast(t - 3)
                            if t < n:
                                dma(t)
                        if spread:
                            wticks.append(tick)
                        else:
                            tick()
                    wcache[idx] = wt

                def wtick():
                    if wticks:
                        wticks.pop(0)()

                def load_w(col0, ncols):
                    idx = [g[0] for g in wgroups].index(col0)
                    assert wgroups[idx][1] == ncols
                    while wticks:
                        wticks.pop(0)()
                    if idx not in wcache:
                        issue_w(idx, False)
                    wt = wcache[idx]
                    if idx + 1 < len(wgroups) and (idx + 1) not in wcache:
                        issue_w(idx + 1, True)
                    return wt

                def main_mm(wt, j, M, t0, n):
                    wtick()
                    pu = psA.next()
                    for kc in range(16):
                        P.op("pe", lambda e, kc=kc: e.matmul(pu[0:M, :n], lhsT=wt[:, kc, j * 128:j * 128 + M], rhs=hT[:, kc, t0:t0 + n], start=(kc == 0), stop=(kc == 15)), [wt, hT, hT2], [pu])
                    return pu

                def headnorm_group(col0, nch, gi, dst, do_rope, tbs=TB):
                    wt = load_w(col0, nch * 128)
                    q1 = []
                    q2 = []

                    def stage1(pu, sq, j, t0, n):
                        pss = psB.next()
                        P.op("pe", lambda e: e.matmul(pss[:, :n], lhsT=bd64[:], rhs=sq[:, :n], start=True, stop=True), [bd64, sq], [pss])
                        rs = rstd_from(pss, 128, n, 1.0 / 64, rsp)
                        o1 = obf.next()
                        P.op("dve", lambda e: e.scalar_tensor_tensor(out=o1[:, :n], in0=pu[:, :n], scalar=cols[:, gi:gi + 1], in1=rs[:, :n], op0=ALU.mult, op1=ALU.mult), [pu, cols, rs], [o1])
                        q2.append((o1, j, t0, n))

                    def stage2(o1, j, t0, n):
                        if do_rope and t0 >= 256:
                            o2 = rope_tail(P, o1, 128, t0, n, obf, f32p, psC)
                        else:
                            o2 = o1
                        P.dma("pool", lambda e: e.dma_start(out=dst[j * 128:(j + 1) * 128, t0:t0 + n], in_=o2[:, :n]), o2, dst)

                    for j in range(nch):
                        for (t0, n) in tbs:
                            pu = main_mm(wt, j, 128, t0, n)
                            sq = sqp.next()
                            P.op("act", lambda e, pu=pu, sq=sq, n=n: e.activation(out=sq[:, :n], in_=pu[:, :n], func=AF.Square), [pu], [sq])
                            if q2:
                                stage2(*q2.pop(0))
                            if q1:
                                stage1(*q1.pop(0))
                            q1.append((pu, sq, j, t0, n))
                    while q1 or q2:
                        if q2:
                            stage2(*q2.pop(0))
                        if q1:
                            stage1(*q1.pop(0))

                def allnorm_group(col0, nch, gi, dst, tbs=TB):
                    wt = load_w(col0, nch * 128)
                    for (t0, n) in tbs:
                        pss = psB.next()
                        pend = None
                        for j in range(nch):
                            pu = main_mm(wt, j, 128, t0, n)
                            sq = sqp.next()
                            P.op("act", lambda e, pu=pu, j=j, n=n: e.activation(out=raw[:, j, :n], in_=pu[:, :n], func=AF.Copy), [pu], [raw])
                            P.op("act", lambda e, pu=pu, sq=sq, n=n: e.activation(out=sq[:, :n], in_=pu[:, :n], func=AF.Square), [pu], [sq])
                            if pend is not None:
                                pj, psq = pend
                                P.op("pe", lambda e, pj=pj, psq=psq, n=n, pss=pss: e.matmul(pss[:, :n], lhsT=ones[:], rhs=psq[:, :n], start=(pj == 0), stop=False), [ones, psq], [pss])
                            pend = (j, sq)
                        pj, psq = pend
                        P.op("pe", lambda e, pj=pj, psq=psq, n=n, pss=pss: e.matmul(pss[:, :n], lhsT=ones[:], rhs=psq[:, :n], start=(pj == 0), stop=True), [ones, psq], [pss])
                        rs = rstd_from(pss, 128, n, 1.0 / (nch * 128), rsp)
                        for j in range(nch):
                            o = obf.next()
                            P.op("dve", lambda e, o=o, j=j, n=n, rs=rs: e.scalar_tensor_tensor(out=o[:, :n], in0=raw[:, j, :n], scalar=cols[:, gi + j:gi + j + 1], in1=rs[:, :n], op0=ALU.mult, op1=ALU.mult), [raw, cols, rs], [o])
                            P.dma("pool", lambda e, o=o, j=j, t0=t0, n=n: e.dma_start(out=dst[j * 128:(j + 1) * 128, t0:t0 + n], in_=o[:, :n]), o, dst)

                def v_group(col0, ncols, dst):
                    wt = load_w(col0, ncols)
                    for tt in range(18):
                        wtick()
                        pv = psA.next()
                        for kc in range(16):
                            P.op("pe", lambda e, kc=kc, tt=tt, pv=pv: e.matmul(pv[:, :ncols], lhsT=hT[:, kc, tt * 128:(tt + 1) * 128], rhs=wt[:, kc, :ncols], start=(kc == 0), stop=(kc == 15)), [hT, hT2, wt], [pv])
                        o = obf.next()
                        if tt % 2 == 0:
                            P.op("act", lambda e, o=o, pv=pv: e.activation(out=o[:, :ncols], in_=pv[:, :ncols], func=AF.Copy), [pv], [o])
                        else:
                            P.op("dve", lambda e, o=o, pv=pv: e.tensor_copy(out=o[:, :ncols], in_=pv[:, :ncols]), [pv], [o])
                        P.dma("pool", lambda e, o=o, tt=tt: e.dma_start(out=dst[tt * 128:(tt + 1) * 128, :], in_=o[:, :ncols]), o, dst)

                def kpe_group():
                    wt = load_w(4096, 64)
                    for (t0, n) in TB:
                        pu = main_mm(wt, 0, 64, t0, n)
                        o = f32p.next()
                        P.op("act", lambda e, o=o, pu=pu, n=n: e.activation(out=o[0:64, :n], in_=pu[0:64, :n], func=AF.Copy), [pu], [o])
                        P.dma("pool", lambda e, o=o, t0=t0, n=n: e.dma_start(out=KPE[:, t0:t0 + n], in_=o[0:64, :n]), o, KPE)

                def z_group():
                    for half in range(4):
                        wt = load_w(4160 + half * 512, 512)
                        for j in range(4):
                            for (t0, n) in qtb:
                                pu = main_mm(wt, j, 128, t0, n)
                                o = obf.next()
                                P.op("act", lambda e, o=o, pu=pu, n=n: e.activation(out=o[:, :n], in_=pu[:, :n], func=AF.Silu), [pu], [o])
                                r = (half * 4 + j) * 128
                                P.dma("pool", lambda e, o=o, r=r, t0=t0, n=n: e.dma_start(out=SZ[r:r + 128, t0:t0 + n], in_=o[:, :n]), o, SZ)

                qtb = TB[1:] if (l == nlayers - 1 and nlayers == 2) else TB
                headnorm_group(0, 4, 0, QAT, False, qtb)
                headnorm_group(512, 4, 1, KAT, False)
                v_group(1024, 512, VA)
                headnorm_group(1536, 6, 2, QBT, True, qtb)
                headnorm_group(2304, 2, 3, KBT, True)
                v_group(2560, 256, VB)
                allnorm_group(2816, 6, 4, CQN, qtb)
                allnorm_group(3584, 4, 10, CKVN)
                kpe_group()
                z_group()

    def phase_B3(l):
        with P.scope():
            wqb = P.sb("wqb", [128, 6, 1152], BF16)
            wkvb = P.sb("wkvb", [128, 4, 1536], BF16)
            P.dma("sp", lambda e: e.dma_start(out=wqb[:], in_=WQBs[l][:].rearrange("(c p) n -> p c n", p=128)), WQBs[l], wqb)
            P.dma("sp", lambda e: e.dma_start(out=wkvb[:], in_=WKVBs[l][:].rearrange("(c p) n -> p c n", p=128)), WKVBs[l], wkvb)
            cqs = Rot([P.sb("cq%d" % i, [128, 6, 512], BF16) for i in range(2)])
            ckvs = Rot([P.sb("ckv%d" % i, [128, 4, 512], BF16) for i in range(2)])
            kpes = Rot([P.sb("kpe%d" % i, [64, 512], F32) for i in range(2)])
            sqks = Rot([P.sb("sqk%d" % i, [64, 512], BF16) for i in range(2)])
            sqp = Rot([P.sb("sq%d" % i, [128, 512], BF16) for i in range(9)])
            rawp = Rot([P.sb("raw%d" % i, [128, 512], F32) for i in range(9)])
            rsp = Rot([P.sb("rs%d" % i, [128, 512], F32) for i in range(4)])
            obf = Rot([P.sb("ob%d" % i, [128, 512], BF16) for i in range(12)])
            f32p = Rot([P.sb("f32_%d" % i, [128, 512], F32) for i in range(4)])
            ovs = Rot([P.sb("ov%d" % i, [128, 768], BF16) for i in range(2)])
            psA = Rot(psb[0:4])
            psB = Rot(psb[4:6])
            psC = Rot(psb[6:8])
            qB = []
            qC = []

            def stageB(h, t0, n, kpe, sqk, rN, rR, rK, sqN, sqR, sqK):
                pq = psB.next()
                P.op("pe", lambda e: e.matmul(pq[:, :n], lhsT=ones[:], rhs=sqN[:, :n], start=True, stop=False), [ones, sqN], [pq])
                P.op("pe", lambda e: e.matmul(pq[:, :n], lhsT=ones[0:64, :], rhs=sqR[0:64, :n], start=False, stop=True), [ones, sqR], [pq])
                pk = psB.next()
                P.op("pe", lambda e: e.matmul(pk[:, :n], lhsT=ones[:], rhs=sqK[:, :n], start=True, stop=False), [ones, sqK], [pk])
                P.op("pe", lambda e: e.matmul(pk[:, :n], lhsT=ones[0:64, :], rhs=sqk[0:64, :n], start=False, stop=True), [ones, sqk], [pk])
                rq = rstd_from(pq, 128, n, 1.0 / 192, rsp)
                rk = rstd_from(pk, 128, n, 1.0 / 192, rsp)
                oN = obf.next(); oR = obf.next(); oK = obf.next(); oKR = obf.next()
                P.op("dve", lambda e: e.scalar_tensor_tensor(out=oN[:, :n], in0=rN[:, :n], scalar=cols[:, 14:15], in1=rq[:, :n], op0=ALU.mult, op1=ALU.mult), [rN, cols, rq], [oN])
                P.op("dve", lambda e: e.scalar_tensor_tensor(out=oR[0:64, :n], in0=rR[0:64, :n], scalar=cols[0:64, 15:16], in1=rq[0:64, :n], op0=ALU.mult, op1=ALU.mult), [rR, cols, rq], [oR])
                P.op("dve", lambda e: e.scalar_tensor_tensor(out=oK[:, :n], in0=rK[:, :n], scalar=cols[:, 16:17], in1=rk[:, :n], op0=ALU.mult, op1=ALU.mult), [rK, cols, rk], [oK])
                P.op("dve", lambda e: e.scalar_tensor_tensor(out=oKR[0:64, :n], in0=kpe[0:64, :n], scalar=cols[0:64, 17:18], in1=rk[0:64, :n], op0=ALU.mult, op1=ALU.mult), [kpe, cols, rk], [oKR])
                P.dma("act", lambda e: e.dma_start(out=QCN[h * 128:(h + 1) * 128, t0:t0 + n], in_=oN[:, :n]), oN, QCN)
                P.dma("act", lambda e: e.dma_start(out=KCN[h * 128:(h + 1) * 128, t0:t0 + n], in_=oK[:, :n]), oK, KCN)
                qC.append((h, t0, n, oR, oKR))

            def stageC(h, t0, n, oR, oKR):
                if t0 >= 256:
                    oR = rope_tail(P, oR, 64, t0, n, obf, f32p, psC)
                    oKR = rope_tail(P, oKR, 64, t0, n, obf, f32p, psC)
                P.dma("sp", lambda e: e.dma_start(out=QCR[h * 64:(h + 1) * 64, t0:t0 + n], in_=oR[0:64, :n]), oR, QCR)
                P.dma("sp", lambda e: e.dma_start(out=KCR[h * 64:(h + 1) * 64, t0:t0 + n], in_=oKR[0:64, :n]), oKR, KCR)

            def drain_one():
                if qC:
                    stageC(*qC.pop(0))
                if qB:
                    stageB(*qB.pop(0))

            for (t0, n) in TB:
                cq = cqs.next()
                ckv = ckvs.next()
                kpe = kpes.next()
                sqk = sqks.next()
                P.dma("sp", lambda e, cq=cq, t0=t0, n=n: e.dma_start(out=cq[:, :, :n], in_=CQN[:, t0:t0 + n].rearrange("(c p) t -> p c t", p=128)), CQN, cq)
                P.dma("sp", lambda e, ckv=ckv, t0=t0, n=n: e.dma_start(out=ckv[:, :, :n], in_=CKVN[:, t0:t0 + n].rearrange("(c p) t -> p c t", p=128)), CKVN, ckv)
                P.dma("sp", lambda e, kpe=kpe, t0=t0, n=n: e.dma_start(out=kpe[:, :n], in_=KPE[:, t0:t0 + n]), KPE, kpe)
                P.op("act", lambda e, kpe=kpe, sqk=sqk, n=n: e.activation(out=sqk[:, :n], in_=kpe[:, :n], func=AF.Square), [kpe], [sqk])
                for h in range(6):
                    pN = psA.next()
                    for kc in range(6):
                        P.op("pe", lambda e, kc=kc, pN=pN, h=h, cq=cq, n=n: e.matmul(pN[:, :n], lhsT=wqb[:, kc, h * 192:h * 192 + 128], rhs=cq[:, kc, :n], start=(kc == 0), stop=(kc == 5)), [wqb, cq], [pN])
                    pR = psA.next()
                    for kc in range(6):
                        P.op("pe", lambda e, kc=kc, pR=pR, h=h, cq=cq, n=n: e.matmul(pR[0:64, :n], lhsT=wqb[:, kc, h * 192 + 128:h * 192 + 192], rhs=cq[:, kc, :n], start=(kc == 0), stop=(kc == 5)), [wqb, cq], [pR])
                    pK = psA.next()
                    for kc in range(4):
                        P.op("pe", lambda e, kc=kc, pK=pK, h=h, ckv=ckv, n=n: e.matmul(pK[:, :n], lhsT=wkvb[:, kc, h * 256:h * 256 + 128], rhs=ckv[:, kc, :n], start=(kc == 0), stop=(kc == 3)), [wkvb, ckv], [pK])
                    sqN = sqp.next(); sqR = sqp.next(); sqK = sqp.next()
                    rN = rawp.next(); rR = rawp.next(); rK = rawp.next()
                    P.op("act", lambda e, sqN=sqN, pN=pN, n=n: e.activation(out=sqN[:, :n], in_=pN[:, :n], func=AF.Square), [pN], [sqN])
                    P.op("act", lambda e, rN=rN, pN=pN, n=n: e.activation(out=rN[:, :n], in_=pN[:, :n], func=AF.Copy), [pN], [rN])
                    P.op("act", lambda e, sqR=sqR, pR=pR, n=n: e.activation(out=sqR[0:64, :n], in_=pR[0:64, :n], func=AF.Square), [pR], [sqR])
                    P.op("act", lambda e, rR=rR, pR=pR, n=n: e.activation(out=rR[0:64, :n], in_=pR[0:64, :n], func=AF.Copy), [pR], [rR])
                    P.op("act", lambda e, sqK=sqK, pK=pK, n=n: e.activation(out=sqK[:, :n], in_=pK[:, :n], func=AF.Square), [pK], [sqK])
                    P.op("act", lambda e, rK=rK, pK=pK, n=n: e.activation(out=rK[:, :n], in_=pK[:, :n], func=AF.Copy), [pK], [rK])
                    drain_one()
                    qB.append((h, t0, n, kpe, sqk, rN, rR, rK, sqN, sqR, sqK))
                for ti in range(n // 128):
                    ov = ovs.next()
                    for half in range(2):
                        pV = psA.next()
                        for kc in range(4):
                            P.op("pe", lambda e, kc=kc, pV=pV, ckv=ckv, ti=ti, half=half: e.matmul(
                                pV[:, 0:384].rearrange("p (h x) -> p h x", x=128),
                                lhsT=ckv[:, kc, ti * 128:(ti + 1) * 128],
                                rhs=wkvb[:, kc, :].rearrange("p (h x) -> p h x", x=256)[:, 3 * half:3 * half + 3, 128:256],
                                start=(kc == 0), stop=(kc == 3)), [ckv, wkvb], [pV])
                        if half == 0:
                            P.op("act", lambda e, ov=ov, pV=pV: e.activation(out=ov[:, 0:384], in_=pV[:, 0:384], func=AF.Copy), [pV], [ov])
                        else:
                            P.op("dve", lambda e, ov=ov, pV=pV: e.tensor_copy(out=ov[:, 384:768], in_=pV[:, 0:384]), [pV], [ov])
                    P.dma("pool", lambda e, ov=ov, r=t0 + ti * 128: e.dma_start(out=VC[r:r + 128, :], in_=ov[:]), ov, VC)
            while qB or qC:
                drain_one()

    def phase_C(l, last):
        with P.scope():
            psS = Rot(psb[0:3])
            psO = Rot(psb[3:5])
            psD = Rot(psb[5:7])
            ptp = Rot([P.sb("pt%d" % i, [128, 512], BF16) for i in range(6)])
            rdp = Rot([P.sb("rd%d" % i, [128, 512], F32) for i in range(2)])
            tmp = Rot([P.sb("tm%d" % i, [128, 512], F32) for i in range(2)])
            szp = Rot([P.sb("sz%d" % i, [128, 512], BF16) for i in range(2)])
            yp = Rot([P.sb("y%d" % i, [128, 512], BF16) for i in range(2)])
            bg_jobs = make_bg(l + 1, psb[7], extra=[(l, ("out",))], load_q="act") if (bg_next and not last) else []
            pipe = []
            nstep = [0]

            def push(fn):
                pipe.append(fn)
                if len(pipe) > 2:
                    pipe.pop(0)()
                nstep[0] += 1
                if bg_jobs and nstep[0] % 8 == 0:
                    bg_jobs.pop(0)()

            def flush():
                while pipe:
                    pipe.pop(0)()

            def finalize(pO, pD, M, n, row0, t0, sink_heads=None, three=False, fold=False):
                rd = rdp.next()
                if fold:
                    P.op("act", lambda e: e.activation(out=rd[0:64, :n], in_=pO[64:128, :n], func=AF.Ln), [pO], [rd])
                elif sink_heads is not None:
                    for g, hh in enumerate(sink_heads):
                        P.op("dve", lambda e, g=g, hh=hh: e.tensor_scalar(out=rd[0:M, g * 128:(g + 1) * 128], in0=pD[0:M, g * 128:(g + 1) * 128], scalar1=esink[0:M, hh:hh + 1], scalar2=None, op0=ALU.add), [pD, esink], [rd])
                    P.op("act", lambda e: e.activation(out=rd[0:M, :n], in_=rd[0:M, :n], func=AF.Ln), [rd], [rd])
                else:
                    P.op("act", lambda e: e.activation(out=rd[0:M, :n], in_=pD[0:M, :n], func=AF.Ln), [pD], [rd])
                P.op("act", lambda e: e.activation(out=rd[0:M, :n], in_=rd[0:M, :n], func=AF.Exp, scale=-1.0), [rd], [rd])
                szt = szp.next()
                if three:
                    P.dma("sp", lambda e: e.dma_start(out=szt[0:64, 0:384].rearrange("p (g t) -> p g t", g=3), in_=SZ[row0:row0 + 192, t0:t0 + 128].rearrange("(g d) t -> d g t", g=3)), SZ, szt)
                else:
                    P.dma("sp", lambda e: e.dma_start(out=szt[0:M, :n], in_=SZ[row0:row0 + M, t0:t0 + n]), SZ, szt)
                tm = tmp.next()
                P.op("dve", lambda e: e.tensor_tensor(out=tm[0:M, :n], in0=pO[0:M, :n], in1=rd[0:M, :n], op=ALU.mult), [pO, rd], [tm])
                y = yp.next()
                P.op("pool", lambda e: e.tensor_tensor(out=y[0:M, :n], in0=tm[0:M, :n], in1=szt[0:M, :n], op=ALU.mult), [tm, szt], [y])
                if three:
                    P.dma("pool", lambda e: e.dma_start(out=YT[row0:row0 + 192, t0:t0 + 128].rearrange("(g d) t -> d g t", g=3), in_=y[0:64, 0:384].rearrange("p (g t) -> p g t", g=3)), y, YT)
                else:
                    P.dma("pool", lambda e: e.dma_start(out=YT[row0:row0 + M, t0:t0 + n], in_=y[0:M, :n]), y, YT)

            def attend(keys, s_mm, v_of, M, n, scale, fin, fold=False, sink=None):
                pO = psO.next()
                pD = None if fold else psD.next()
                nk = len(keys)
                MM = 128 if fold else M

                def pv(i, pt):
                    kt = keys[i][0]
                    vb, vap = v_of(kt)
                    first = (i == 0)
                    if first and sink is not None:
                        P.op("pe", lambda e: e.matmul(pO[0:128, 0:384].rearrange("p (g t) -> p g t", g=3), lhsT=selr[0:33, :], rhs=esr[0:33, sink * 3:sink * 3 + 3, :], start=True, stop=False), [selr, esr], [pO])
                        first = False
                    P.op("pe", lambda e: e.matmul(pO[0:MM, :n], lhsT=vap, rhs=pt[:, :n], start=first, stop=(i == nk - 1)), [vb, pt], [pO])
                    if not fold:
                        P.op("pe", lambda e: e.matmul(pD[0:M, :n], lhsT=ones[:, 0:M], rhs=pt[:, :n], start=(i == 0), stop=(i == nk - 1)), [ones, pt], [pD])
                    if i == nk - 1:
                        fin(pO, pD)

                for i, (kt, mk) in enumerate(keys):
                    pS = psS.next()
                    s_mm(pS, kt)
                    pt = ptp.next()
                    P.op("act", lambda e, pS=pS, pt=pt: e.activation(out=pt[:, :n], in_=pS[:, :n], func=AF.Exp, scale=scale), [pS], [pt])
                    if mk is not None:
                        P.op("dve", lambda e, pt=pt, mk=mk: e.tensor_tensor(out=pt[:, :n], in0=pt[:, :n], in1=swamask[:, mk, :], op=ALU.mult), [pt, swamask], [pt])
                    push(lambda i=i, pt=pt: pv(i, pt))

            with P.scope():
                kNs = Rot([P.sb("kN%d" % i, [128, T], BF16) for i in range(2)])
                kRs = Rot([P.sb("kR%d" % i, [64, T], BF16) for i in range(2)])
                qNs = Rot([P.sb("qN%d" % i, [128, T], BF16) for i in range(2)])
                qRs = Rot([P.sb("qR%d" % i, [64, T], BF16) for i in range(2)])
                vs = Rot([P.sb("vc%d" % i, [128, 18, 128], BF16) for i in range(2)])
                for h in range(6):
                    kN = kNs.next(); kR = kRs.next(); qN = qNs.next(); qR = qRs.next(); v = vs.next()
                    P.dma("sp", lambda e, kN=kN, h=h: e.dma_start(out=kN[:], in_=KCN[h * 128:(h + 1) * 128, :]), KCN, kN)
                    P.dma("sp", lambda e, kR=kR, h=h: e.dma_start(out=kR[:], in_=KCR[h * 64:(h + 1) * 64, :]), KCR, kR)
                    P.dma("sp", lambda e, qN=qN, h=h: e.dma_start(out=qN[:], in_=QCN[h * 128:(h + 1) * 128, :]), QCN, qN)
                    P.dma("sp", lambda e, qR=qR, h=h: e.dma_start(out=qR[:], in_=QCR[h * 64:(h + 1) * 64, :]), QCR, qR)
                    P.dma("sp", lambda e, v=v, h=h: e.dma_start(out=v[:], in_=VC[:, h * 128:(h + 1) * 128].rearrange("(t p) d -> p t d", p=128)), VC, v)
                    blocks = [(t0, n, list(range(18))) for (t0, n) in TB[1:]]
                    if not last:
                        blocks.append((0, 256, [0, 1]))
                    for (t0, n, kts) in blocks:
                        def s_mm(pS, kt, t0=t0, n=n, kN=kN, kR=kR, qN=qN, qR=qR):
                            P.op("pe", lambda e: e.matmul(pS[:, :n], lhsT=kN[:, kt * 128:(kt + 1) * 128], rhs=qN[:, t0:t0 + n], start=True, stop=False), [kN, qN], [pS])
                            P.op("pe", lambda e: e.matmul(pS[:, :n], lhsT=kR[0:64, kt * 128:(kt + 1) * 128], rhs=qR[0:64, t0:t0 + n], start=False, stop=True), [kR, qR], [pS])
                        attend([(kt, None) for kt in kts], s_mm, lambda kt, v=v: (v, v[:, kt, :]), 128, n, 192 ** -0.5,
                               lambda pO, pD, n=n, h=h, t0=t0: finalize(pO, pD, 128, n, 1280 + h * 128, t0))
                flush()

            with P.scope():
                kTs = Rot([P.sb("kb%d" % i, [64, T], BF16) for i in range(2)])
                qs = Rot([P.sb("qb%d" % i, [64, 3, T], BF16) for i in range(2)])
                vs = Rot([P.sb("vb%d" % i, [128, 18, 128], BF16) for i in range(2)])
                for vv in vs.bufs:
                    P.op("pool", lambda e, vv=vv: e.memset(vv[:, :, 64:128], 1.0), [], [vv])
                for kvh in range(4):
                    kT = kTs.next(); q = qs.next(); v = vs.next()
                    P.dma("sp", lambda e, kT=kT, kvh=kvh: e.dma_start(out=kT[:], in_=KBT[kvh * 64:(kvh + 1) * 64, :]), KBT, kT)
                    P.dma("sp", lambda e, q=q, kvh=kvh: e.dma_start(out=q[:], in_=QBT[kvh * 192:(kvh + 1) * 192, :].rearrange("(g d) t -> d g t", g=3)), QBT, q)
                    P.dma("sp", lambda e, v=v, kvh=kvh: e.dma_start(out=v[:, :, 0:64], in_=VB[:, kvh * 64:(kvh + 1) * 64].rearrange("(t p) d -> p t d", p=128)), VB, v)
                    qtiles = []
                    for qt in range(16):
                        keys = [(0, None), (1, None)]
                        if qt > 0:
                            keys.append((2 + qt - 1, 0))
                        keys.append((2 + qt, None))
                        if qt < 15:
                            keys.append((2 + qt + 1, 1))
                        qtiles.append((256 + qt * 128, keys))
                    if not last:
                        qtiles.append((0, [(0, None), (1, None)]))
                        qtiles.append((128, [(0, None), (1, None)]))
                    for (tok0, keys) in qtiles:
                        def s_mm(pS, kt, tok0=tok0, kT=kT, q=q):
                            P.op("pe", lambda e: e.matmul(pS[:, 0:384].rearrange("p (g t) -> p g t", g=3), lhsT=kT[0:64, kt * 128:(kt + 1) * 128], rhs=q[0:64, :, tok0:tok0 + 128], start=True, stop=True), [kT, q], [pS])
                        attend(keys, s_mm, lambda kt, v=v: (v, v[:, kt, :]), 64, 384, 0.125,
                               lambda pO, pD, kvh=kvh, tok0=tok0: finalize(pO, pD, 64, 384, 512 + kvh * 192, tok0, three=True, fold=True), fold=True, sink=kvh)
                flush()

            with P.scope():
                kTs = Rot([P.sb("ka%d" % i, [64, T], BF16) for i in range(2)])
                qs = Rot([P.sb("qa%d" % i, [64, T], BF16) for i in range(2)])
                v0s = Rot([P.sb("va0_%d" % i, [128, 18, 128], BF16) for i in range(2)])
                v1s = Rot([P.sb("va1_%d" % i, [128, 17, 128], BF16) for i in range(2)])
                for vv in v0s.bufs + v1s.bufs:
                    P.op("pool", lambda e, vv=vv: e.memset(vv[:, :, 64:128], 1.0), [], [vv])
                gfs = Rot([P.sb("gf%d" % i, [128, 896], F32) for i in range(2)])
                Gs = Rot([P.sb("G%d" % i, [128, 16, 64], BF16) for i in range(2)])
                for h in range(8):
                    kT = kTs.next(); q = qs.next(); v0 = v0s.next(); v1 = v1s.next(); gf = gfs.next(); G = Gs.next()
                    P.dma("sp", lambda e, kT=kT, h=h: e.dma_start(out=kT[:], in_=KAT[h * 64:(h + 1) * 64, :]), KAT, kT)
                    P.dma("sp", lambda e, q=q, h=h: e.dma_start(out=q[:], in_=QAT[h * 64:(h + 1) * 64, :]), QAT, q)
                    P.dma("sp", lambda e, v0=v0, h=h: e.dma_start(out=v0[:, :, 0:64], in_=VA[:, h * 64:(h + 1) * 64].rearrange("(t p) d -> p t d", p=128)), VA, v0)
                    P.dma("sp", lambda e, v1=v1, h=h: e.dma_start(out=v1[:, :, 0:64], in_=VA[64:64 + 17 * 128, h * 64:(h + 1) * 64].rearrange("(t p) d -> p t d", p=128)), VA, v1)
                    P.dma("sp", lambda e, gf=gf, h=h: e.dma_start(out=gf[:], in_=gb_d[l, :, h * 896:(h + 1) * 896]), gb_d, gf)
                    P.op("act", lambda e, gf=gf: e.activation(out=gf[:], in_=gf[:], func=AF.Exp), [gf], [gf])
                    for m in range(14):
                        P.op("dve", lambda e, gf=gf, G=G, m=m: e.tensor_tensor(out=G[:, m, :], in0=gf[:, m * 64:(m + 1) * 64], in1=namask[:], op=ALU.mult), [gf, namask], [G])
                    def pv_stage(pt, ri, rs_, pO, pD, fin, h=h, v0=v0, v1=v1):
                        for i in range(6):
                            if i < 2:
                                vb, vap = v0, v0[:, i, :]
                            elif rs_ % 2 == 0:
                                vb, vap = v0, v0[:, 2 + rs_ // 2 + (i - 2), :]
                            else:
                                vb, vap = v1, v1[:, (3 + rs_) // 2 + (i - 2), :]
                            P.op("pe", lambda e, i=i, vap=vap: e.matmul(pO[0:128, ri * 64:(ri + 1) * 64], lhsT=vap, rhs=pt[:, i * 64:(i + 1) * 64], start=(i == 0), stop=(i == 5)), [vb, pt], [pO])
                        if fin is not None:
                            finalize(pO, None, 64, 512, h * 64, fin, fold=True)

                    for rg in range(4):
                        pO = psO.next()
                        pD = None
                        for ri in range(8):
                            r = rg * 8 + ri
                            rs_ = min(max(r - 4, 0), 24)
                            m0 = rs_ - r + 7
                            tq = 256 + r * 64
                            ks = 256 + rs_ * 64
                            pS = psS.next()
                            for i in range(6):
                                k0 = i * 128 if i < 2 else ks + (i - 2) * 128
                                P.op("pe", lambda e, i=i, k0=k0, pS=pS, tq=tq, kT=kT, q=q: e.matmul(pS[:, i * 64:(i + 1) * 64], lhsT=kT[0:64, k0:k0 + 128], rhs=q[0:64, tq:tq + 64], start=True, stop=True), [kT, q], [pS])
                            pt = ptp.next()
                            P.op("act", lambda e, pS=pS, pt=pt: e.activation(out=pt[:, 0:384], in_=pS[:, 0:384], func=AF.Exp, scale=0.125), [pS], [pt])
                            P.op("dve", lambda e, pt=pt, G=G, m0=m0: e.tensor_tensor(out=pt[:, 128:384].rearrange("p (j c) -> p j c", c=64), in0=pt[:, 128:384].rearrange("p (j c) -> p j c", c=64), in1=G[:, m0:m0 + 8, :].rearrange("p (j two) c -> p j two c", two=2)[:, :, 0, :], op=ALU.mult), [pt, G], [pt])
                            push(lambda a=(pt, ri, rs_, pO, pD, (256 + rg * 512) if ri == 7 else None), f=pv_stage: f(*a))
                    if not last:
                        def s_mm(pS, kt, kT=kT, q=q):
                            P.op("pe", lambda e: e.matmul(pS[:, 0:256], lhsT=kT[0:64, kt * 128:(kt + 1) * 128], rhs=q[0:64, 0:256], start=True, stop=True), [kT, q], [pS])
                        attend([(0, None), (1, None)], s_mm, lambda kt, v0=v0: (v0, v0[:, kt, :]), 64, 256, 0.125,
                               lambda pO, pD, h=h: finalize(pO, pD, 64, 256, h * 64, 0, fold=True), fold=True)
                flush()
            while bg_jobs:
                bg_jobs.pop(0)()

    def phase_D(l, last):
        with P.scope():
            wout = P.sb("wout", [128, 16, D], BF16)
            gate_row = [P.sb("gate_l", [128, D], F32), P.sb("gate_c", [128, D], F32)]
            for s in range(2):
                P.dma("sp", lambda e, s=s: e.dma_start(out=gate_row[s][:], in_=MODs[l][s, 4096:6144].partition_broadcast(128)), MODs[l], gate_row[s])
            for cb in range(4):
                P.dma("sp", lambda e, cb=cb: e.dma_start(out=wout[:, :, cb * 512:(cb + 1) * 512], in_=WOUTs[l][:, cb * 512:(cb + 1) * 512].rearrange("(c p) n -> p c n", p=128)), WOUTs[l], wout)
            yts = Rot([P.sb("yT%d" % i, [128, 16, 512], BF16) for i in range(2)])
            xts = Rot([P.sb("xt%d" % i, [128, D], F32) for i in range(2)])
            ots = Rot([P.sb("ot%d" % i, [128, D], F32) for i in range(2)])
            tmp = Rot([P.sb("tm%d" % i, [128, 512], F32) for i in range(2)])
            psA = Rot(psb[0:4])
            blocks = list(TB[1:])
            if not last:
                blocks.append(TB[0])
            for (t0, n) in blocks:
                yt = yts.next()
                P.dma("sp", lambda e, yt=yt, t0=t0, n=n: e.dma_start(out=yt[:, :, :n], in_=YT[:, t0:t0 + n].rearrange("(c p) t -> p c t", p=128)), YT, yt)
                for ti in range(n // 128):
                    isctx = t0 < 256
                    r0 = (t0 + ti * 128) if isctx else (t0 - 256 + ti * 128)
                    if l == 0:
                        src = ctx_d if isctx else x_d
                    else:
                        src = XC1 if isctx else X1
                    if last:
                        dst = out_d
                    else:
                        dst = XC1 if isctx else X1
                    g = gate_row[1 if isctx else 0]
                    xt = xts.next()
                    ot = ots.next()
                    P.dma("sp", lambda e, xt=xt, src=src, r0=r0: e.dma_start(out=xt[:], in_=src[r0:r0 + 128, :]), src, xt)
                    for cb in range(4):
                        po = psA.next()
                        for kc in range(16):
                            P.op("pe", lambda e, kc=kc, po=po, yt=yt, ti=ti, cb=cb: e.matmul(po[:, :], lhsT=yt[:, kc, ti * 128:(ti + 1) * 128], rhs=wout[:, kc, cb * 512:(cb + 1) * 512], start=(kc == 0), stop=(kc == 15)), [yt, wout], [po])
                        tm = tmp.next()
                        P.op("dve", lambda e, tm=tm, po=po, g=g, cb=cb: e.tensor_tensor(out=tm[:], in0=po[:], in1=g[:, cb * 512:(cb + 1) * 512], op=ALU.mult), [po, g], [tm])
                        P.op("pool", lambda e, tm=tm, ot=ot, xt=xt, cb=cb: e.tensor_tensor(out=ot[:, cb * 512:(cb + 1) * 512], in0=tm[:], in1=xt[:, cb * 512:(cb + 1) * 512], op=ALU.add), [tm, xt], [ot])
                    P.dma("pool", lambda e, ot=ot, dst=dst, r0=r0: e.dma_start(out=dst[r0:r0 + 128, :], in_=ot[:]), ot, dst)

    bg_next = (nlayers == 2)
    for l in range(nlayers):
        last = (l == nlayers - 1) and nlayers == 2
        if l == 0 or not bg_next:
            phase_cast_ada(l)
        else:
            ada_tail(l)
        if stop == "ada":
            break
        phase_B(l)
        if stop in ("B1", "B"):
            break
        phase_B3(l)
        if stop == "B3":
            break
        phase_C(l, last)
        if stop == "C":
            break
        phase_D(l, last)
    P.barrier()
    P.emit()
    return nc


def _consts():
    ident = np.eye(128, dtype=np.float32)
    bd = np.zeros((128, 128), np.float32)
    bd[:64, :64] = 1
    bd[64:, 64:] = 1
    on = np.ones((128, 128), np.float32)
    r64 = np.zeros((64, 64), np.float32)
    for i in range(16):
        r64[i + 16, i] = -1
        r64[i, i + 16] = 1
        r64[i + 48, i + 32] = -1
        r64[i + 32, i + 48] = 1
    rot = np.zeros((128, 128), np.float32)
    rot[:64, :64] = r64
    rot[64:, 64:] = r64
    cmat = np.ascontiguousarray(np.stack([ident, bd, on, rot], axis=1))
    t = np.arange(2048)
    row = (t // 64).astype(np.float32)
    col = (t % 64).astype(np.float32)
    inv = (np.float32(10000.0) ** (-np.arange(16, dtype=np.float32) / np.float32(16))).astype(np.float32)
    ar = row[:, None] * inv
    ac = col[:, None] * inv
    ang = np.concatenate([ar, ar, ac, ac], -1)
    cos = np.cos(ang).astype(np.float32).T
    sin = np.sin(ang).astype(np.float32).T
    cossin = np.zeros((128, 2, 2048), np.float32)
    cossin[:64, 0] = cos
    cossin[64:, 0] = cos
    cossin[:64, 1] = sin
    cossin[64:, 1] = sin
    kc = np.arange(128) % 64
    c = np.arange(64)
    qs = np.clip(c - 8, 0, 48)
    namask = ((kc[:, None] >= qs[None, :]) & (kc[:, None] < qs[None, :] + 16)).astype(np.float32)
    i = np.arange(128)
    lo = (i[None, :] <= i[:, None]).astype(np.float32)
    hi = (i[:, None] <= i[None, :]).astype(np.float32)
    swamask = np.stack([np.tile(lo, (1, 3)), np.tile(hi, (1, 3))], axis=1).astype(np.float32)
    return cmat, cossin, namask, np.ascontiguousarray(swamask)


def _gather_rpb(rpb):
    p = np.arange(128)
    kc = p % 64
    half = p // 64
    c = np.arange(64)
    dc = np.clip(kc[:, None] - c[None, :], -15, 15) + 15
    m = np.arange(14)
    dr = m[None, :, None] + half[:, None, None]
    g = rpb[:, :, dr, dc[:, None, :]]
    g = np.transpose(g, (0, 2, 1, 3, 4)).reshape(2, 128, 8 * 14 * 64)
    return np.ascontiguousarray(g.astype(np.float32))


def _cols(norm_w, qn_a, kn_a, qn_b, kn_b, qa_norm, kva_norm, qn_c, kn_c):
    out = np.ones((2, 128, 34), np.float32)
    for l in range(2):
        out[l, :, 0] = np.tile(qn_a[l], 2)
        out[l, :, 1] = np.tile(kn_a[l], 2)
        out[l, :, 2] = np.tile(qn_b[l], 2)
        out[l, :, 3] = np.tile(kn_b[l], 2)
        out[l, :, 4:10] = qa_norm[l].reshape(6, 128).T
        out[l, :, 10:14] = kva_norm[l].reshape(4, 128).T
        out[l, :, 14] = qn_c[l][:128]
        out[l, :64, 15] = qn_c[l][128:]
        out[l, :, 16] = kn_c[l][:128]
        out[l, :64, 17] = kn_c[l][128:]
        out[l, :, 18:34] = norm_w[l].reshape(16, 128).T
    return out


def make_in_maps(x, c, ctx, c_ctx, norm_w, w_ada, b_ada, w_in, qn_a, kn_a, rpb_a, qn_b, kn_b, sink_b,
                 qa_norm, kva_norm, w_qb, w_kvb, qn_c, kn_c, w_out):
    f = lambda a: np.ascontiguousarray(np.asarray(a, dtype=np.float32))
    x, c, ctx, c_ctx = f(x), f(c), f(ctx), f(c_ctx)
    cmat, cossin, namask, swamask = _consts()
    cols = _cols(f(norm_w), f(qn_a), f(kn_a), f(qn_b), f(kn_b), f(qa_norm), f(kva_norm), f(qn_c), f(kn_c))
    gb = _gather_rpb(f(rpb_a))
    sink = np.ascontiguousarray(np.broadcast_to(f(sink_b)[:, None, :], (2, 64, 12)))
    shared = dict(w_ada=f(w_ada), b_ada=f(b_ada), w_in=f(w_in), w_qb=f(w_qb), w_kvb=f(w_kvb), w_out=f(w_out),
                  cols=cols, sink=sink, gb=gb, cmat=cmat, cossin=cossin, namask=namask, swamask=swamask)
    maps = []
    for b in range(8):
        cc = np.zeros((128, 16, 2), np.float32)
        cc[:, :, 0] = c[b].reshape(16, 128).T
        cc[:, :, 1] = c_ctx.reshape(16, 128).T
        m = dict(shared)
        m["x"] = x[b]
        m["ctx"] = ctx[b]
        m["cc"] = np.ascontiguousarray(cc.reshape(128, 32))
        maps.append(m)
    return maps


def kernel(**inputs):
    maps = make_in_maps(**inputs)
    nc = build(2)
    res = run_bass_kernel_spmd(nc, maps, core_ids=list(range(8)))
    return np.stack([np.asarray(r["out"], dtype=np.float32) for r in res.results], axis=0)
```

```python
import contextlib
import numpy as np
import concourse.bass as bass
import concourse.mybir as mybir
from concourse.bass_utils import run_bass_kernel_spmd

F32 = mybir.dt.float32
BF16 = mybir.dt.bfloat16
AF = mybir.ActivationFunctionType
ALU = mybir.AluOpType

CH = 30000
DCH = 1800
D = 2048
T = 2304
EPS = 1e-6
N_IN = 6208
TB = [(0, 256), (256, 512), (768, 512), (1280, 512), (1792, 512)]


class Buf:
    def __init__(self, t, name, is_dram=False, is_psum=False):
        self.t = t
        self.name = name
        self.is_dram = is_dram
        self.is_psum = is_psum
        self.writers = {}
        self.readers = {}

    def __getitem__(self, k):
        return self.t[k]


class Prog:
    ENG = ["pe", "act", "dve", "pool", "sp"]

    def __init__(self, nc):
        self.nc = nc
        self.ops = {e: [] for e in self.ENG}
        self.cnt = {e: 0 for e in self.ENG}
        self.seen = {e: {} for e in self.ENG}
        self.sems = {}
        self.ndma = {}
        self.latest = {}
        self.stack = None
        self.uid = 0

    def sb(self, name, shape, dtype):
        self.uid += 1
        t = self.stack.enter_context(self.nc.sbuf_tensor("%s_u%d" % (name, self.uid), list(shape), dtype))
        return Buf(t, name)

    def gsb(self, name, shape, dtype):
        return Buf(self.nc.alloc_sbuf_tensor(name, list(shape), dtype), name)

    def ps(self, name, shape, dtype=F32):
        return Buf(self.nc.alloc_psum_tensor(name, list(shape), dtype), name, is_psum=True)

    def dram(self, name, shape, dtype, kind="Internal"):
        return Buf(self.nc.dram_tensor(name, list(shape), dtype, kind=kind), name, is_dram=True)

    def sem(self, key):
        if key not in self.sems:
            self.sems[key] = self.nc.alloc_semaphore("s%d" % len(self.sems))
        return self.sems[key]

    def _collect(self, E, reads, writes, own_keys):
        deps = {}

        def add(k, v):
            if deps.get(k, 0) < v:
                deps[k] = v
        for r in reads:
            for k, v in r.writers.items():
                if E == "pe" and k[:2] == ("eng", "pe"):
                    continue
                add(k, v)
            if r.is_psum:
                for k, v in r.readers.items():
                    if k[:2] not in own_keys:
                        add(k, v)
        for w in writes:
            for k, v in w.writers.items():
                if k[:2] not in own_keys:
                    add(k, v)
            for k, v in w.readers.items():
                if k[:2] not in own_keys:
                    add(k, v)
        waits = []
        for k, v in deps.items():
            if self.seen[E].get(k, 0) >= v:
                continue
            self.seen[E][k] = v
            waits.append((self.sem(k), v))
        return waits

    def _commit(self, ev, reads, writes):
        k, v = ev
        self.latest[k] = v
        for r in reads:
            if r.readers.get(k, 0) < v:
                r.readers[k] = v
        for w in writes:
            if w.writers.get(k, 0) < v:
                w.writers[k] = v
            w.readers = {}

    def op(self, E, fn, reads=(), writes=()):
        waits = self._collect(E, reads, writes, (("eng", E),))
        idx = self.cnt[E]
        self.cnt[E] += 1
        key = ("eng", E, idx // CH)
        val = idx % CH + 1
        self.ops[E].append((waits, fn, self.sem(key), 1))
        self._commit((key, val), reads, writes)

    def dma(self, Q, fn, src, dst):
        side = dst if src.is_dram else src
        n = self.ndma.get(side.name, 0)
        key = ("dma", side.name, n // DCH)
        waits = self._collect(Q, [src], [dst], (("dma", side.name),))
        val = 16 * (n % DCH + 1)
        self.ndma[side.name] = n + 1
        self.ops[Q].append((waits, fn, self.sem(key), 16))
        self._commit((key, val), [src], [dst])

    def barrier(self):
        for E in self.ENG:
            waits = []
            for k, v in self.latest.items():
                if self.seen[E].get(k, 0) >= v:
                    continue
                self.seen[E][k] = v
                waits.append((self.sem(k), v))
            if waits:
                self.ops[E].append((waits, None, None, 0))

    @contextlib.contextmanager
    def scope(self):
        old = self.stack
        with contextlib.ExitStack() as st:
            self.stack = st
            yield
            self.barrier()
        self.stack = old

    def emit(self):
        nc = self.nc
        ops = self.ops

        def run(eng, lst):
            for waits, fn, sem, inc in lst:
                for s, v in waits:
                    eng.wait_ge(s, v)
                if fn is not None:
                    fn(eng).then_inc(sem, inc)

        with nc.Block() as block:
            @block.sync
            def _(e):
                run(e, ops["sp"])

            @block.tensor
            def _(e):
                run(e, ops["pe"])

            @block.scalar
            def _(e):
                run(e, ops["act"])

            @block.vector
            def _(e):
                run(e, ops["dve"])

            @block.gpsimd
            def _(e):
                run(e, ops["pool"])


class Rot:
    def __init__(self, bufs):
        self.bufs = bufs
        self.i = 0

    def next(self):
        b = self.bufs[self.i % len(self.bufs)]
        self.i += 1
        return b


def build(nlayers=2, dbg=(), stop=None):
    nc = bass.Bass("TRN2", target_bir_lowering=False)
    P = Prog(nc)
    EI = "ExternalInput"

    def scr(name, shape, dtype):
        return P.dram(name, shape, dtype, kind=("ExternalOutput" if name in dbg else "Internal"))

    x_d = P.dram("x", [2048, D], F32, EI)
    ctx_d = P.dram("ctx", [256, D], F32, EI)
    cc_d = P.dram("cc", [128, 32], F32, EI)
    w_ada_d = P.dram("w_ada", [2, D, 6144], F32, EI)
    b_ada_d = P.dram("b_ada", [2, 6144], F32, EI)
    w_in_d = P.dram("w_in", [2, D, N_IN], F32, EI)
    w_qb_d = P.dram("w_qb", [2, 768, 1152], F32, EI)
    w_kvb_d = P.dram("w_kvb", [2, 512, 1536], F32, EI)
    w_out_d = P.dram("w_out", [2, D, D], F32, EI)
    cols_d = P.dram("cols", [2, 128, 34], F32, EI)
    sink_d = P.dram("sink", [2, 64, 12], F32, EI)
    gb_d = P.dram("gb", [2, 128, 8 * 896], F32, EI)
    cmat_d = P.dram("cmat", [128, 4, 128], F32, EI)
    cossin_d = P.dram("cossin", [128, 2, 2048], F32, EI)
    namask_d = P.dram("namask", [128, 64], F32, EI)
    swamask_d = P.dram("swamask", [128, 2, 384], F32, EI)
    out_d = P.dram("out", [2048, D], F32, "ExternalOutput")

    WINs = [scr("WIN%d" % i, [D, N_IN], BF16) for i in range(2)]
    WQBs = [scr("WQB%d" % i, [768, 1152], BF16) for i in range(2)]
    WKVBs = [scr("WKVB%d" % i, [512, 1536], BF16) for i in range(2)]
    WOUTs = [scr("WOUT%d" % i, [D, D], BF16) for i in range(2)]
    MODs = [scr("MOD" if i == 0 else "MOD1", [2, 6144], F32) for i in range(2)]
    QAT = scr("QAT", [512, T], BF16)
    KAT = scr("KAT", [512, T], BF16)
    VA = scr("VA", [T, 512], BF16)
    QBT = scr("QBT", [768, T], BF16)
    KBT = scr("KBT", [256, T], BF16)
    VB = scr("VB", [T, 256], BF16)
    CQN = scr("CQN", [768, T], BF16)
    CKVN = scr("CKVN", [512, T], BF16)
    KPE = scr("KPE", [64, T], F32)
    SZ = scr("SZ", [D, T], BF16)
    QCN = scr("QCN", [768, T], BF16)
    QCR = scr("QCR", [384, T], BF16)
    KCN = scr("KCN", [768, T], BF16)
    KCR = scr("KCR", [384, T], BF16)
    VC = scr("VC", [T, 768], BF16)
    YT = scr("YT", [D, T], BF16)
    X1 = scr("X1", [2048, D], F32)
    XC1 = scr("XC1", [256, D], F32)

    psb = [P.ps("ps%d" % i, [128, 512], F32) for i in range(8)]

    ident = P.gsb("ident", [128, 128], F32)
    bd64 = P.gsb("bd64", [128, 128], BF16)
    ones = P.gsb("ones", [128, 128], BF16)
    rotm = P.gsb("rotm", [128, 128], BF16)
    cos_t = P.gsb("cos_t", [128, 2048], F32)
    sin_t = P.gsb("sin_t", [128, 2048], F32)
    namask = P.gsb("namask_s", [128, 64], F32)
    swamask = P.gsb("swamask_s", [128, 2, 384], BF16)
    scT = P.gsb("scT", [128, 32], F32)
    cols = P.gsb("cols_s", [128, 34], F32)
    modcol = P.gsb("modcol", [128, 2, 32], F32)
    gcol = P.gsb("gcol", [128, 2, 16], F32)
    esink = P.gsb("esink", [64, 12], F32)
    ones2 = P.gsb("ones2", [1, 2], F32)
    epsc = P.gsb("epsc", [128, 1], F32)
    esr = P.gsb("esr", [33, 12, 128], BF16)
    selr = P.gsb("selr", [33, 128], BF16)
    onesf = P.gsb("onesf", [64, 128], F32)
    eshi = P.gsb("eshi", [64, 12], BF16)
    eslo = P.gsb("eslo", [64, 12], F32)

    with P.scope():
        cm = P.sb("cm", [128, 4, 128], F32)
        swf = P.sb("swf", [128, 2, 384], F32)
        cct = P.sb("cct", [128, 32], F32)
        P.dma("sp", lambda e: e.dma_start(out=cm[:], in_=cmat_d[:]), cmat_d, cm)
        P.dma("sp", lambda e: e.dma_start(out=swf[:], in_=swamask_d[:]), swamask_d, swf)
        P.dma("sp", lambda e: e.dma_start(out=cct[:], in_=cc_d[:]), cc_d, cct)
        P.dma("sp", lambda e: e.dma_start(out=cos_t[:], in_=cossin_d[:, 0, :]), cossin_d, cos_t)
        P.dma("sp", lambda e: e.dma_start(out=sin_t[:], in_=cossin_d[:, 1, :]), cossin_d, sin_t)
        P.dma("sp", lambda e: e.dma_start(out=namask[:], in_=namask_d[:]), namask_d, namask)
        P.op("dve", lambda e: e.tensor_copy(out=ident[:], in_=cm[:, 0, :]), [cm], [ident])
        P.op("dve", lambda e: e.tensor_copy(out=bd64[:], in_=cm[:, 1, :]), [cm], [bd64])
        P.op("dve", lambda e: e.tensor_copy(out=ones[:], in_=cm[:, 2, :]), [cm], [ones])
        P.op("dve", lambda e: e.tensor_copy(out=rotm[:], in_=cm[:, 3, :]), [cm], [rotm])
        P.op("dve", lambda e: e.tensor_copy(out=swamask[:], in_=swf[:]), [swf], [swamask])
        P.op("dve", lambda e: e.memset(ones2[:], 1.0), [], [ones2])
        P.op("dve", lambda e: e.memset(epsc[:], EPS), [], [epsc])
        P.op("dve", lambda e: e.memset(onesf[:], 1.0), [], [onesf])
        P.op("dve", lambda e: e.memset(selr[:], 0.0), [], [selr])
        P.op("dve", lambda e: e.memset(selr[0:1, 64:128], 1.0), [], [selr])
        P.op("dve", lambda e: e.memset(selr[32:33, 64:128], 1.0), [], [selr])
        P.op("dve", lambda e: e.memset(esr[:], 0.0), [], [esr])
        P.op("act", lambda e: e.activation(out=scT[:], in_=cct[:], func=AF.Silu), [cct], [scT])

    def make_bg(l, ps_bank, CW=3104, store_q="pool", which=("qb", "kvb", "out"), ada=True, extra=(), load_q="sp"):
        stg = [P.sb("stg%d" % i, [128, CW], F32) for i in range(2)]
        bft = [P.sb("bft%d" % i, [128, CW], BF16) for i in range(2)]
        wts = [P.sb("wada%d" % i, [128, 16, 256], F32) for i in range(2)]
        badas = [P.sb("bada%d" % i, [1, 256], F32) for i in range(2)]
        mods = [P.sb("modsb%d" % i, [2, 256], F32) for i in range(2)]
        cast = []
        for (ll, wh) in list(extra) + [(l, which)]:
            for (nm, src, dst, R, C) in [("in", w_in_d, WINs[ll], D, N_IN), ("qb", w_qb_d, WQBs[ll], 768, 1152),
                                         ("kvb", w_kvb_d, WKVBs[ll], 512, 1536), ("out", w_out_d, WOUTs[ll], D, D)]:
                if nm not in wh:
                    continue
                for rc in range(R // 128):
                    for c0 in range(0, C, CW):
                        cast.append((ll, src, dst, rc, c0, min(C, c0 + CW) - c0))

        def cast_stages(k, ll, src, dst, rc, c0, cw):
            st = stg[k % 2]
            bt = bft[k % 2]
            h = cw // 2

            def s0():
                P.dma(load_q if load_q != "alt" else "sp", lambda e: e.dma_start(out=st[:, :cw], in_=src[ll, rc * 128:(rc + 1) * 128, c0:c0 + cw]), src, st)

            def s1():
                P.op("pool", lambda e: e.tensor_copy(out=bt[:, :h], in_=st[:, :h]), [st], [bt])
                P.op("dve", lambda e: e.tensor_copy(out=bt[:, h:cw], in_=st[:, h:cw]), [st], [bt])

            def s2():
                P.dma(store_q, lambda e: e.dma_start(out=dst[rc * 128:(rc + 1) * 128, c0:c0 + cw], in_=bt[:, :cw]), bt, dst)
            return (s0, s1, s2)

        def ada_stages(k):
            w = wts[k % 2]
            bada = badas[k % 2]
            md = mods[k % 2]
            c0 = k * 256

            def s0():
                lq = load_q if load_q != "alt" else ("sp" if k % 2 == 0 else "act")
                P.dma(lq, lambda e: e.dma_start(out=w[:], in_=w_ada_d[l, :, c0:c0 + 256].rearrange("(c p) n -> p c n", p=128)), w_ada_d, w)
                P.dma(lq, lambda e: e.dma_start(out=bada[:], in_=b_ada_d[l:l + 1, c0:c0 + 256]), b_ada_d, bada)

            def s1():
                pm = ps_bank
                for kc in range(16):
                    P.op("pe", lambda e, kc=kc: e.matmul(pm[0:2, 0:256], lhsT=scT[:, 2 * kc:2 * kc + 2], rhs=w[:, kc, :], start=(kc == 0), stop=False), [scT, w], [pm])
                P.op("pe", lambda e: e.matmul(pm[0:2, 0:256], lhsT=ones2[0:1, 0:2], rhs=bada[0:1, :], start=False, stop=True), [ones2, bada], [pm])
                P.op("act", lambda e: e.activation(out=md[0:2, :], in_=pm[0:2, 0:256], func=AF.Copy), [pm], [md])

            def s2():
                P.dma("pool", lambda e: e.dma_start(out=MODs[l][:, c0:c0 + 256], in_=md[0:2, :]), md, MODs[l])
            return (s0, s1, s2)

        def lagged(stages):
            ticks = []
            n = len(stages)
            for t in range(n + 2):
                def tick(t=t):
                    if t < n:
                        stages[t][0]()
                    if 0 <= t - 1 < n:
                        stages[t - 1][1]()
                    if 0 <= t - 2 < n:
                        stages[t - 2][2]()
                ticks.append(tick)
            return ticks
        ct = lagged([cast_stages(k, *c) for k, c in enumerate(cast)])
        at = lagged([ada_stages(k) for k in range(24)]) if ada else []
        jobs = []
        while ct or at:
            for _ in range(3):
                if ct:
                    jobs.append(ct.pop(0))
            if at:
                jobs.append(at.pop(0))
        return jobs

    def ada_tail(l):
        with P.scope():
            nwt = P.sb("nwt", [128, 2, 16], F32)
            snk = P.sb("snk", [64, 12], F32)
            P.dma("sp", lambda e: e.dma_start(out=cols[:], in_=cols_d[l]), cols_d, cols)
            P.dma("sp", lambda e: e.dma_start(out=snk[:], in_=sink_d[l]), sink_d, snk)
            P.op("act", lambda e: e.activation(out=esink[:], in_=snk[:], func=AF.Exp), [snk], [esink])
            P.op("dve", lambda e: e.tensor_copy(out=eshi[:], in_=esink[:]), [esink], [eshi])
            P.op("dve", lambda e: e.tensor_tensor(out=eslo[:], in0=esink[:], in1=eshi[:], op=ALU.subtract), [esink, eshi], [eslo])
            for hh in range(12):
                P.op("dve", lambda e, hh=hh: e.tensor_scalar(out=esr[0:1, hh, :], in0=onesf[0:1, :], scalar1=eshi[0:1, hh:hh + 1], scalar2=None, op0=ALU.mult), [onesf, eshi], [esr])
                P.op("dve", lambda e, hh=hh: e.tensor_scalar(out=esr[32:33, hh, :], in0=onesf[32:33, :], scalar1=eslo[32:33, hh:hh + 1], scalar2=None, op0=ALU.mult), [onesf, eslo], [esr])
            for s_ in range(2):
                P.dma("sp", lambda e, s_=s_: e.dma_start(out=modcol[:, s_, :], in_=MODs[l][s_, 0:4096].rearrange("(c p) -> p c", p=128), allow_slow_non_contiguous=True), MODs[l], modcol)
            for s_ in range(2):
                P.op("dve", lambda e, s_=s_: e.tensor_scalar(out=nwt[:, s_, :], in0=modcol[:, s_, 16:32], scalar1=1.0, scalar2=None, op0=ALU.add), [modcol], [nwt])
                P.op("dve", lambda e, s_=s_: e.tensor_tensor(out=gcol[:, s_, :], in0=nwt[:, s_, :], in1=cols[:, 18:34], op=ALU.mult), [nwt, cols], [gcol])

    def phase_cast_ada(l):
        with P.scope():
            for j in make_bg(l, psb[7], CW=N_IN, store_q="act", load_q="alt", which=(("qb", "kvb") if (l == 0 and bg_next) else ("qb", "kvb", "out"))):
                j()
        ada_tail(l)

    def rope_tail(P_, qn, M, t0, n, obf, f32p, psC):
        pr = psC.next()
        P.op("pe", lambda e: e.matmul(pr[0:M, :n], lhsT=rotm[0:M, 0:M], rhs=qn[0:M, :n], start=True, stop=True), [rotm, qn], [pr])
        t1 = f32p.next()
        t2 = f32p.next()
        c0 = t0 - 256
        e1 = "dve" if M == 128 else "pool"
        P.op(e1, lambda e: e.tensor_tensor(out=t1[0:M, :n], in0=qn[0:M, :n], in1=cos_t[0:M, c0:c0 + n], op=ALU.mult), [qn, cos_t], [t1])
        P.op("dve", lambda e: e.tensor_tensor(out=t2[0:M, :n], in0=pr[0:M, :n], in1=sin_t[0:M, c0:c0 + n], op=ALU.mult), [pr, sin_t], [t2])
        o = obf.next()
        P.op(e1, lambda e: e.tensor_tensor(out=o[0:M, :n], in0=t1[0:M, :n], in1=t2[0:M, :n], op=ALU.add), [t1, t2], [o])
        return o

    def rstd_from(pss, M, n, scale, rsp):
        rs = rsp.next()
        P.op("act", lambda e: e.activation(out=rs[0:M, :n], in_=pss[0:M, :n], func=AF.Ln, scale=scale, bias=epsc[0:M, :]), [pss, epsc], [rs])
        P.op("act", lambda e: e.activation(out=rs[0:M, :n], in_=rs[0:M, :n], func=AF.Exp, scale=-0.5), [rs], [rs])
        return rs

    def phase_B(l):
        with P.scope():
            hT = P.sb("hT", [128, 16, T], BF16)
            hT2 = Buf(hT.t, "hT2")
            with P.scope():
                xts = Rot([P.sb("xt%d" % i, [128, D], F32) for i in range(2)])
                xns = Rot([P.sb("xn%d" % i, [128, D], F32) for i in range(2)])
                junk = P.sb("junk", [128, D], BF16)
                sss = Rot([P.sb("ss%d" % i, [128, 1], F32) for i in range(2)])
                pT = Rot(psb[0:4])
                for tt in range(18):
                    s = 1 if tt < 2 else 0
                    if l == 0:
                        src = ctx_d if tt < 2 else x_d
                    else:
                        src = XC1 if tt < 2 else X1
                    r0 = tt * 128 if tt < 2 else (tt - 2) * 128
                    xt = xts.next()
                    xn = xns.next()
                    ss = sss.next()
                    P.dma("sp", lambda e, xt=xt, src=src, r0=r0: e.dma_start(out=xt[:], in_=src[r0:r0 + 128, :]), src, xt)
                    P.op("act", lambda e, xt=xt, ss=ss: e.activation(out=junk[:], in_=xt[:], func=AF.Square, accum_out=ss[:]), [xt], [junk, ss])
                    P.op("act", lambda e, ss=ss: e.activation(out=ss[:], in_=ss[:], func=AF.Sqrt, scale=1.0 / D, bias=EPS), [ss], [ss])
                    P.op("dve", lambda e, ss=ss: e.reciprocal(out=ss[:], in_=ss[:]), [ss], [ss])
                    P.op("dve", lambda e, xt=xt, xn=xn, ss=ss: e.tensor_scalar(out=xn[:], in0=xt[:], scalar1=ss[:], scalar2=None, op0=ALU.mult), [xt, ss], [xn])
                    for g4 in range(4):
                        pb = pT.next()
                        for c4 in range(4):
                            c = g4 * 4 + c4
                            P.op("pe", lambda e, pb=pb, xn=xn, c=c, c4=c4: e.transpose(out=pb[:, c4 * 128:(c4 + 1) * 128], in_=xn[:, c * 128:(c + 1) * 128], identity=ident[:]), [xn, ident], [pb])
                        for c4 in range(4):
                            c = g4 * 4 + c4
                            if g4 % 2 == 0:
                                P.op("act", lambda e, pb=pb, c=c, c4=c4, s=s, tt=tt: e.activation(out=hT[:, c, tt * 128:(tt + 1) * 128], in_=pb[:, c4 * 128:(c4 + 1) * 128], func=AF.Identity, scale=gcol[:, s, c:c + 1], bias=modcol[:, s, c:c + 1]), [pb, gcol, modcol], [hT])
                            else:
                                P.op("dve", lambda e, pb=pb, c=c, c4=c4, s=s, tt=tt: e.tensor_scalar(out=hT[:, c, tt * 128:(tt + 1) * 128], in0=pb[:, c4 * 128:(c4 + 1) * 128], scalar1=gcol[:, s, c:c + 1], scalar2=modcol[:, s, c:c + 1], op0=ALU.mult, op1=ALU.add), [pb, gcol, modcol], [hT2])
            if stop == "B1":
                return
            with P.scope():
                wts = Rot([P.sb("wt%d" % i, [128, 16, 768], BF16) for i in range(2)])
                sqp = Rot([P.sb("sq%d" % i, [128, 512], BF16) for i in range(3)])
                rsp = Rot([P.sb("rs%d" % i, [128, 512], F32) for i in range(3)])
                obf = Rot([P.sb("ob%d" % i, [128, 512], BF16) for i in range(6)])
                f32p = Rot([P.sb("f32_%d" % i, [128, 512], F32) for i in range(4)])
                raw = P.sb("raw", [128, 6, 512], F32)
                psA = Rot(psb[0:3])
                psB = Rot(psb[3:5])
                psC = Rot(psb[5:7])

                wstg = Rot([P.sb("wstg%d" % i, [128, 16, 128], F32) for i in range(3)])
                wgroups = [(0, 512), (512, 512), (1024, 512), (1536, 768), (2304, 256), (2560, 256), (2816, 768),
                           (3584, 512), (4096, 64)] + [(4160 + i * 512, 512) for i in range(4)]
                wcache = {}
                wticks = []

                def issue_w(idx, spread):
                    col0, ncols = wgroups[idx]
                    wt = wts.next()
                    subs = [(c, min(128, ncols - c)) for c in range(0, ncols, 128)]
                    sts = {}

                    def dma(k):
                        c, cw = subs[k]
                        st = wstg.next()
                        sts[k] = st
                        P.dma("sp", lambda e: e.dma_start(out=st[:, :, :cw], in_=w_in_d[l, :, col0 + c:col0 + c + cw].rearrange("(c p) n -> p c n", p=128)), w_in_d, st)

                    def cast(k):
                        c, cw = subs[k]
                        st = sts[k]
                        P.op("dve", lambda e: e.tensor_copy(out=wt[:, :, c:c + cw], in_=st[:, :, :cw]), [st], [wt])

                    n = len(subs)
                    for t in range(n + 3):
                        def tick(t=t):
                            if t - 3 >= 0:
                                cast(t - 3)
                            if t < n:
                                dma(t)
                        if spread:
                            wticks.append(tick)
                        else:
                            tick()
                    wcache[idx] = wt

                def wtick():
                    if wticks:
                        wticks.pop(0)()

                def load_w(col0, ncols):
                    idx = [g[0] for g in wgroups].index(col0)
                    assert wgroups[idx][1] == ncols
                    while wticks:
                        wticks.pop(0)()
                    if idx not in wcache:
                        issue_w(idx, False)
                    wt = wcache[idx]
                    if idx + 1 < len(wgroups) and (idx + 1) not in wcache:
                        issue_w(idx + 1, True)
                    return wt

                def main_mm(wt, j, M, t0, n):
                    wtick()
                    pu = psA.next()
                    for kc in range(16):
                        P.op("pe", lambda e, kc=kc: e.matmul(pu[0:M, :n], lhsT=wt[:, kc, j * 128:j * 128 + M], rhs=hT[:, kc, t0:t0 + n], start=(kc == 0), stop=(kc == 15)), [wt, hT, hT2], [pu])
                    return pu

                def headnorm_group(col0, nch, gi, dst, do_rope, tbs=TB):
                    wt = load_w(col0, nch * 128)
                    q1 = []
                    q2 = []

                    def stage1(pu, sq, j, t0, n):
                        pss = psB.next()
                        P.op("pe", lambda e: e.matmul(pss[:, :n], lhsT=bd64[:], rhs=sq[:, :n], start=True, stop=True), [bd64, sq], [pss])
                        rs = rstd_from(pss, 128, n, 1.0 / 64, rsp)
                        o1 = obf.next()
                        P.op("dve", lambda e: e.scalar_tensor_tensor(out=o1[:, :n], in0=pu[:, :n], scalar=cols[:, gi:gi + 1], in1=rs[:, :n], op0=ALU.mult, op1=ALU.mult), [pu, cols, rs], [o1])
                        q2.append((o1, j, t0, n))

                    def stage2(o1, j, t0, n):
                        if do_rope and t0 >= 256:
                            o2 = rope_tail(P, o1, 128, t0, n, obf, f32p, psC)
                        else:
                            o2 = o1
                        P.dma("pool", lambda e: e.dma_start(out=dst[j * 128:(j + 1) * 128, t0:t0 + n], in_=o2[:, :n]), o2, dst)

                    for j in range(nch):
                        for (t0, n) in tbs:
                            pu = main_mm(wt, j, 128, t0, n)
                            sq = sqp.next()
                            P.op("act", lambda e, pu=pu, sq=sq, n=n: e.activation(out=sq[:, :n], in_=pu[:, :n], func=AF.Square), [pu], [sq])
                            if q2:
                                stage2(*q2.pop(0))
                            if q1:
                                stage1(*q1.pop(0))
                            q1.append((pu, sq, j, t0, n))
                    while q1 or q2:
                        if q2:
                            stage2(*q2.pop(0))
                        if q1:
                            stage1(*q1.pop(0))

                def allnorm_group(col0, nch, gi, dst, tbs=TB):
                    wt = load_w(col0, nch * 128)
                    for (t0, n) in tbs:
                        pss = psB.next()
                        pend = None
                        for j in range(nch):
                            pu = main_mm(wt, j, 128, t0, n)
                            sq = sqp.next()
                            P.op("act", lambda e, pu=pu, j=j, n=n: e.activation(out=raw[:, j, :n], in_=pu[:, :n], func=AF.Copy), [pu], [raw])
                            P.op("act", lambda e, pu=pu, sq=sq, n=n: e.activation(out=sq[:, :n], in_=pu[:, :n], func=AF.Square), [pu], [sq])
                            if pend is not None:
                                pj, psq = pend
                                P.op("pe", lambda e, pj=pj, psq=psq, n=n, pss=pss: e.matmul(pss[:, :n], lhsT=ones[:], rhs=psq[:, :n], start=(pj == 0), stop=False), [ones, psq], [pss])
                            pend = (j, sq)
                        pj, psq = pend
                        P.op("pe", lambda e, pj=pj, psq=psq, n=n, pss=pss: e.matmul(pss[:, :n], lhsT=ones[:], rhs=psq[:, :n], start=(pj == 0), stop=True), [ones, psq], [pss])
                        rs = rstd_from(pss, 128, n, 1.0 / (nch * 128), rsp)
                        for j in range(nch):
                            o = obf.next()
                            P.op("dve", lambda e, o=o, j=j, n=n, rs=rs: e.scalar_tensor_tensor(out=o[:, :n], in0=raw[:, j, :n], scalar=cols[:, gi + j:gi + j + 1], in1=rs[:, :n], op0=ALU.mult, op1=ALU.mult), [raw, cols, rs], [o])
                            P.dma("pool", lambda e, o=o, j=j, t0=t0, n=n: e.dma_start(out=dst[j * 128:(j + 1) * 128, t0:t0 + n], in_=o[:, :n]), o, dst)

                def v_group(col0, ncols, dst):
                    wt = load_w(col0, ncols)
                    for tt in range(18):
                        wtick()
                        pv = psA.next()
                        for kc in range(16):
                            P.op("pe", lambda e, kc=kc, tt=tt, pv=pv: e.matmul(pv[:, :ncols], lhsT=hT[:, kc, tt * 128:(tt + 1) * 128], rhs=wt[:, kc, :ncols], start=(kc == 0), stop=(kc == 15)), [hT, hT2, wt], [pv])
                        o = obf.next()
                        if tt % 2 == 0:
                            P.op("act", lambda e, o=o, pv=pv: e.activation(out=o[:, :ncols], in_=pv[:, :ncols], func=AF.Copy), [pv], [o])
                        else:
                            P.op("dve", lambda e, o=o, pv=pv: e.tensor_copy(out=o[:, :ncols], in_=pv[:, :ncols]), [pv], [o])
                        P.dma("pool", lambda e, o=o, tt=tt: e.dma_start(out=dst[tt * 128:(tt + 1) * 128, :], in_=o[:, :ncols]), o, dst)

                def kpe_group():
                    wt = load_w(4096, 64)
                    for (t0, n) in TB:
                        pu = main_mm(wt, 0, 64, t0, n)
                        o = f32p.next()
                        P.op("act", lambda e, o=o, pu=pu, n=n: e.activation(out=o[0:64, :n], in_=pu[0:64, :n], func=AF.Copy), [pu], [o])
                        P.dma("pool", lambda e, o=o, t0=t0, n=n: e.dma_start(out=KPE[:, t0:t0 + n], in_=o[0:64, :n]), o, KPE)

                def z_group():
                    for half in range(4):
                        wt = load_w(4160 + half * 512, 512)
                        for j in range(4):
                            for (t0, n) in qtb:
                                pu = main_mm(wt, j, 128, t0, n)
                                o = obf.next()
                                P.op("act", lambda e, o=o, pu=pu, n=n: e.activation(out=o[:, :n], in_=pu[:, :n], func=AF.Silu), [pu], [o])
                                r = (half * 4 + j) * 128
                                P.dma("pool", lambda e, o=o, r=r, t0=t0, n=n: e.dma_start(out=SZ[r:r + 128, t0:t0 + n], in_=o[:, :n]), o, SZ)

                qtb = TB[1:] if (l == nlayers - 1 and nlayers == 2) else TB
                headnorm_group(0, 4, 0, QAT, False, qtb)
                headnorm_group(512, 4, 1, KAT, False)
                v_group(1024, 512, VA)
                headnorm_group(1536, 6, 2, QBT, True, qtb)
                headnorm_group(2304, 2, 3, KBT, True)
                v_group(2560, 256, VB)
                allnorm_group(2816, 6, 4, CQN, qtb)
                allnorm_group(3584, 4, 10, CKVN)
                kpe_group()
                z_group()

    def phase_B3(l):
        with P.scope():
            wqb = P.sb("wqb", [128, 6, 1152], BF16)
            wkvb = P.sb("wkvb", [128, 4, 1536], BF16)
            P.dma("sp", lambda e: e.dma_start(out=wqb[:], in_=WQBs[l][:].rearrange("(c p) n -> p c n", p=128)), WQBs[l], wqb)
            P.dma("sp", lambda e: e.dma_start(out=wkvb[:], in_=WKVBs[l][:].rearrange("(c p) n -> p c n", p=128)), WKVBs[l], wkvb)
            cqs = Rot([P.sb("cq%d" % i, [128, 6, 512], BF16) for i in range(2)])
            ckvs = Rot([P.sb("ckv%d" % i, [128, 4, 512], BF16) for i in range(2)])
            kpes = Rot([P.sb("kpe%d" % i, [64, 512], F32) for i in range(2)])
            sqks = Rot([P.sb("sqk%d" % i, [64, 512], BF16) for i in range(2)])
            sqp = Rot([P.sb("sq%d" % i, [128, 512], BF16) for i in range(9)])
            rawp = Rot([P.sb("raw%d" % i, [128, 512], F32) for i in range(9)])
            rsp = Rot([P.sb("rs%d" % i, [128, 512], F32) for i in range(4)])
            obf = Rot([P.sb("ob%d" % i, [128, 512], BF16) for i in range(12)])
            f32p = Rot([P.sb("f32_%d" % i, [128, 512], F32) for i in range(4)])
            ovs = Rot([P.sb("ov%d" % i, [128, 768], BF16) for i in range(2)])
            psA = Rot(psb[0:4])
            psB = Rot(psb[4:6])
            psC = Rot(psb[6:8])
            qB = []
            qC = []

            def stageB(h, t0, n, kpe, sqk, rN, rR, rK, sqN, sqR, sqK):
                pq = psB.next()
                P.op("pe", lambda e: e.matmul(pq[:, :n], lhsT=ones[:], rhs=sqN[:, :n], start=True, stop=False), [ones, sqN], [pq])
                P.op("pe", lambda e: e.matmul(pq[:, :n], lhsT=ones[0:64, :], rhs=sqR[0:64, :n], start=False, stop=True), [ones, sqR], [pq])
                pk = psB.next()
                P.op("pe", lambda e: e.matmul(pk[:, :n], lhsT=ones[:], rhs=sqK[:, :n], start=True, stop=False), [ones, sqK], [pk])
                P.op("pe", lambda e: e.matmul(pk[:, :n], lhsT=ones[0:64, :], rhs=sqk[0:64, :n], start=False, stop=True), [ones, sqk], [pk])
                rq = rstd_from(pq, 128, n, 1.0 / 192, rsp)
                rk = rstd_from(pk, 128, n, 1.0 / 192, rsp)
                oN = obf.next(); oR = obf.next(); oK = obf.next(); oKR = obf.next()
                P.op("dve", lambda e: e.scalar_tensor_tensor(out=oN[:, :n], in0=rN[:, :n], scalar=cols[:, 14:15], in1=rq[:, :n], op0=ALU.mult, op1=ALU.mult), [rN, cols, rq], [oN])
                P.op("dve", lambda e: e.scalar_tensor_tensor(out=oR[0:64, :n], in0=rR[0:64, :n], scalar=cols[0:64, 15:16], in1=rq[0:64, :n], op0=ALU.mult, op1=ALU.mult), [rR, cols, rq], [oR])
                P.op("dve", lambda e: e.scalar_tensor_tensor(out=oK[:, :n], in0=rK[:, :n], scalar=cols[:, 16:17], in1=rk[:, :n], op0=ALU.mult, op1=ALU.mult), [rK, cols, rk], [oK])
                P.op("dve", lambda e: e.scalar_tensor_tensor(out=oKR[0:64, :n], in0=kpe[0:64, :n], scalar=cols[0:64, 17:18], in1=rk[0:64, :n], op0=ALU.mult, op1=ALU.mult), [kpe, cols, rk], [oKR])
                P.dma("act", lambda e: e.dma_start(out=QCN[h * 128:(h + 1) * 128, t0:t0 + n], in_=oN[:, :n]), oN, QCN)
                P.dma("act", lambda e: e.dma_start(out=KCN[h * 128:(h + 1) * 128, t0:t0 + n], in_=oK[:, :n]), oK, KCN)
                qC.append((h, t0, n, oR, oKR))

            def stageC(h, t0, n, oR, oKR):
                if t0 >= 256:
                    oR = rope_tail(P, oR, 64, t0, n, obf, f32p, psC)
                    oKR = rope_tail(P, oKR, 64, t0, n, obf, f32p, psC)
                P.dma("sp", lambda e: e.dma_start(out=QCR[h * 64:(h + 1) * 64, t0:t0 + n], in_=oR[0:64, :n]), oR, QCR)
                P.dma("sp", lambda e: e.dma_start(out=KCR[h * 64:(h + 1) * 64, t0:t0 + n], in_=oKR[0:64, :n]), oKR, KCR)

            def drain_one():
                if qC:
                    stageC(*qC.pop(0))
                if qB:
                    stageB(*qB.pop(0))

            for (t0, n) in TB:
                cq = cqs.next()
                ckv = ckvs.next()
                kpe = kpes.next()
                sqk = sqks.next()
                P.dma("sp", lambda e, cq=cq, t0=t0, n=n: e.dma_start(out=cq[:, :, :n], in_=CQN[:, t0:t0 + n].rearrange("(c p) t -> p c t", p=128)), CQN, cq)
                P.dma("sp", lambda e, ckv=ckv, t0=t0, n=n: e.dma_start(out=ckv[:, :, :n], in_=CKVN[:, t0:t0 + n].rearrange("(c p) t -> p c t", p=128)), CKVN, ckv)
                P.dma("sp", lambda e, kpe=kpe, t0=t0, n=n: e.dma_start(out=kpe[:, :n], in_=KPE[:, t0:t0 + n]), KPE, kpe)
                P.op("act", lambda e, kpe=kpe, sqk=sqk, n=n: e.activation(out=sqk[:, :n], in_=kpe[:, :n], func=AF.Square), [kpe], [sqk])
                for h in range(6):
                    pN = psA.next()
                    for kc in range(6):
                        P.op("pe", lambda e, kc=kc, pN=pN, h=h, cq=cq, n=n: e.matmul(pN[:, :n], lhsT=wqb[:, kc, h * 192:h * 192 + 128], rhs=cq[:, kc, :n], start=(kc == 0), stop=(kc == 5)), [wqb, cq], [pN])
                    pR = psA.next()
                    for kc in range(6):
                        P.op("pe", lambda e, kc=kc, pR=pR, h=h, cq=cq, n=n: e.matmul(pR[0:64, :n], lhsT=wqb[:, kc, h * 192 + 128:h * 192 + 192], rhs=cq[:, kc, :n], start=(kc == 0), stop=(kc == 5)), [wqb, cq], [pR])
                    pK = psA.next()
                    for kc in range(4):
                        P.op("pe", lambda e, kc=kc, pK=pK, h=h, ckv=ckv, n=n: e.matmul(pK[:, :n], lhsT=wkvb[:, kc, h * 256:h * 256 + 128], rhs=ckv[:, kc, :n], start=(kc == 0), stop=(kc == 3)), [wkvb, ckv], [pK])
                    sqN = sqp.next(); sqR = sqp.next(); sqK = sqp.next()
                    rN = rawp.next(); rR = rawp.next(); rK = rawp.next()
                    P.op("act", lambda e, sqN=sqN, pN=pN, n=n: e.activation(out=sqN[:, :n], in_=pN[:, :n], func=AF.Square), [pN], [sqN])
                    P.op("act", lambda e, rN=rN, pN=pN, n=n: e.activation(out=rN[:, :n], in_=pN[:, :n], func=AF.Copy), [pN], [rN])
                    P.op("act", lambda e, sqR=sqR, pR=pR, n=n: e.activation(out=sqR[0:64, :n], in_=pR[0:64, :n], func=AF.Square), [pR], [sqR])
                    P.op("act", lambda e, rR=rR, pR=pR, n=n: e.activation(out=rR[0:64, :n], in_=pR[0:64, :n], func=AF.Copy), [pR], [rR])
                    P.op("act", lambda e, sqK=sqK, pK=pK, n=n: e.activation(out=sqK[:, :n], in_=pK[:, :n], func=AF.Square), [pK], [sqK])
                    P.op("act", lambda e, rK=rK, pK=pK, n=n: e.activation(out=rK[:, :n], in_=pK[:, :n], func=AF.Copy), [pK], [rK])
                    drain_one()
                    qB.append((h, t0, n, kpe, sqk, rN, rR, rK, sqN, sqR, sqK))
                for ti in range(n // 128):
                    ov = ovs.next()
                    for half in range(2):
                        pV = psA.next()
                        for kc in range(4):
                            P.op("pe", lambda e, kc=kc, pV=pV, ckv=ckv, ti=ti, half=half: e.matmul(
                                pV[:, 0:384].rearrange("p (h x) -> p h x", x=128),
                                lhsT=ckv[:, kc, ti * 128:(ti + 1) * 128],
                                rhs=wkvb[:, kc, :].rearrange("p (h x) -> p h x", x=256)[:, 3 * half:3 * half + 3, 128:256],
                                start=(kc == 0), stop=(kc == 3)), [ckv, wkvb], [pV])
                        if half == 0:
                            P.op("act", lambda e, ov=ov, pV=pV: e.activation(out=ov[:, 0:384], in_=pV[:, 0:384], func=AF.Copy), [pV], [ov])
                        else:
                            P.op("dve", lambda e, ov=ov, pV=pV: e.tensor_copy(out=ov[:, 384:768], in_=pV[:, 0:384]), [pV], [ov])
                    P.dma("pool", lambda e, ov=ov, r=t0 + ti * 128: e.dma_start(out=VC[r:r + 128, :], in_=ov[:]), ov, VC)
            while qB or qC:
                drain_one()

    def phase_C(l, last):
        with P.scope():
            psS = Rot(psb[0:3])
            psO = Rot(psb[3:5])
            psD = Rot(psb[5:7])
            ptp = Rot([P.sb("pt%d" % i, [128, 512], BF16) for i in range(6)])
            rdp = Rot([P.sb("rd%d" % i, [128, 512], F32) for i in range(2)])
            tmp = Rot([P.sb("tm%d" % i, [128, 512], F32) for i in range(2)])
            szp = Rot([P.sb("sz%d" % i, [128, 512], BF16) for i in range(2)])
            yp = Rot([P.sb("y%d" % i, [128, 512], BF16) for i in range(2)])
            bg_jobs = make_bg(l + 1, psb[7], extra=[(l, ("out",))], load_q="act") if (bg_next and not last) else []
            pipe = []
            nstep = [0]

            def push(fn):
                pipe.append(fn)
                if len(pipe) > 2:
                    pipe.pop(0)()
                nstep[0] += 1
                if bg_jobs and nstep[0] % 8 == 0:
                    bg_jobs.pop(0)()

            def flush():
                while pipe:
                    pipe.pop(0)()

            def finalize(pO, pD, M, n, row0, t0, sink_heads=None, three=False, fold=False):
                rd = rdp.next()
                if fold:
                    P.op("act", lambda e: e.activation(out=rd[0:64, :n], in_=pO[64:128, :n], func=AF.Ln), [pO], [rd])
                elif sink_heads is not None:
                    for g, hh in enumerate(sink_heads):
                        P.op("dve", lambda e, g=g, hh=hh: e.tensor_scalar(out=rd[0:M, g * 128:(g + 1) * 128], in0=pD[0:M, g * 128:(g + 1) * 128], scalar1=esink[0:M, hh:hh + 1], scalar2=None, op0=ALU.add), [pD, esink], [rd])
                    P.op("act", lambda e: e.activation(out=rd[0:M, :n], in_=rd[0:M, :n], func=AF.Ln), [rd], [rd])
                else:
                    P.op("act", lambda e: e.activation(out=rd[0:M, :n], in_=pD[0:M, :n], func=AF.Ln), [pD], [rd])
                P.op("act", lambda e: e.activation(out=rd[0:M, :n], in_=rd[0:M, :n], func=AF.Exp, scale=-1.0), [rd], [rd])
                szt = szp.next()
                if three:
                    P.dma("sp", lambda e: e.dma_start(out=szt[0:64, 0:384].rearrange("p (g t) -> p g t", g=3), in_=SZ[row0:row0 + 192, t0:t0 + 128].rearrange("(g d) t -> d g t", g=3)), SZ, szt)
                else:
                    P.dma("sp", lambda e: e.dma_start(out=szt[0:M, :n], in_=SZ[row0:row0 + M, t0:t0 + n]), SZ, szt)
                tm = tmp.next()
                P.op("dve", lambda e: e.tensor_tensor(out=tm[0:M, :n], in0=pO[0:M, :n], in1=rd[0:M, :n], op=ALU.mult), [pO, rd], [tm])
                y = yp.next()
                P.op("pool", lambda e: e.tensor_tensor(out=y[0:M, :n], in0=tm[0:M, :n], in1=szt[0:M, :n], op=ALU.mult), [tm, szt], [y])
                if three:
                    P.dma("pool", lambda e: e.dma_start(out=YT[row0:row0 + 192, t0:t0 + 128].rearrange("(g d) t -> d g t", g=3), in_=y[0:64, 0:384].rearrange("p (g t) -> p g t", g=3)), y, YT)
                else:
                    P.dma("pool", lambda e: e.dma_start(out=YT[row0:row0 + M, t0:t0 + n], in_=y[0:M, :n]), y, YT)

            def attend(keys, s_mm, v_of, M, n, scale, fin, fold=False, sink=None):
                pO = psO.next()
                pD = None if fold else psD.next()
                nk = len(keys)
                MM = 128 if fold else M

                def pv(i, pt):
                    kt = keys[i][0]
                    vb, vap = v_of(kt)
                    first = (i == 0)
                    if first and sink is not None:
                        P.op("pe", lambda e: e.matmul(pO[0:128, 0:384].rearrange("p (g t) -> p g t", g=3), lhsT=selr[0:33, :], rhs=esr[0:33, sink * 3:sink * 3 + 3, :], start=True, stop=False), [selr, esr], [pO])
                        first = False
                    P.op("pe", lambda e: e.matmul(pO[0:MM, :n], lhsT=vap, rhs=pt[:, :n], start=first, stop=(i == nk - 1)), [vb, pt], [pO])
                    if not fold:
                        P.op("pe", lambda e: e.matmul(pD[0:M, :n], lhsT=ones[:, 0:M], rhs=pt[:, :n], start=(i == 0), stop=(i == nk - 1)), [ones, pt], [pD])
                    if i == nk - 1:
                        fin(pO, pD)

                for i, (kt, mk) in enumerate(keys):
                    pS = psS.next()
                    s_mm(pS, kt)
                    pt = ptp.next()
                    P.op("act", lambda e, pS=pS, pt=pt: e.activation(out=pt[:, :n], in_=pS[:, :n], func=AF.Exp, scale=scale), [pS], [pt])
                    if mk is not None:
                        P.op("dve", lambda e, pt=pt, mk=mk: e.tensor_tensor(out=pt[:, :n], in0=pt[:, :n], in1=swamask[:, mk, :], op=ALU.mult), [pt, swamask], [pt])
                    push(lambda i=i, pt=pt: pv(i, pt))

            with P.scope():
                kNs = Rot([P.sb("kN%d" % i, [128, T], BF16) for i in range(2)])
                kRs = Rot([P.sb("kR%d" % i, [128, T], BF16) for i in range(2)])
                qNs = Rot([P.sb("qN%d" % i, [128, T], BF16) for i in range(2)])
                qRs = Rot([P.sb("qR%d" % i, [128, T], BF16) for i in range(2)])
                for bb in kRs.bufs + qRs.bufs:
                    P.op("pool", lambda e, bb=bb: e.memset(bb[64:128, :], 0.0), [], [bb])
                vs = Rot([P.sb("vc%d" % i, [128, 18, 128], BF16) for i in range(2)])
                for h in range(6):
                    kN = kNs.next(); kR = kRs.next(); qN = qNs.next(); qR = qRs.next(); v = vs.next()
                    P.dma("sp", lambda e, kN=kN, h=h: e.dma_start(out=kN[:], in_=KCN[h * 128:(h + 1) * 128, :]), KCN, kN)
                    P.dma("sp", lambda e, kR=kR, h=h: e.dma_start(out=kR[0:64, :], in_=KCR[h * 64:(h + 1) * 64, :]), KCR, kR)
                    P.dma("sp", lambda e, qN=qN, h=h: e.dma_start(out=qN[:], in_=QCN[h * 128:(h + 1) * 128, :]), QCN, qN)
                    P.dma("sp", lambda e, qR=qR, h=h: e.dma_start(out=qR[0:64, :], in_=QCR[h * 64:(h + 1) * 64, :]), QCR, qR)
                    P.dma("sp", lambda e, v=v, h=h: e.dma_start(out=v[:], in_=VC[:, h * 128:(h + 1) * 128].rearrange("(t p) d -> p t d", p=128)), VC, v)
                    blocks = [(t0, n, list(range(18))) for (t0, n) in TB[1:]]
                    if not last:
                        blocks.append((0, 256, [0, 1]))
                    for (t0, n, kts) in blocks:
                        def s_mm(pS, kt, t0=t0, n=n, kN=kN, kR=kR, qN=qN, qR=qR):
                            P.op("pe", lambda e: e.matmul(pS[:, :n], lhsT=kN[:, kt * 128:(kt + 1) * 128], rhs=qN[:, t0:t0 + n], start=True, stop=False), [kN, qN], [pS])
                            P.op("pe", lambda e: e.matmul(pS[:, :n], lhsT=kR[:, kt * 128:(kt + 1) * 128], rhs=qR[:, t0:t0 + n], start=False, stop=True), [kR, qR], [pS])
                        attend([(kt, None) for kt in kts], s_mm, lambda kt, v=v: (v, v[:, kt, :]), 128, n, 192 ** -0.5,
                               lambda pO, pD, n=n, h=h, t0=t0: finalize(pO, pD, 128, n, 1280 + h * 128, t0))
                flush()

            with P.scope():
                kTs = Rot([P.sb("kb%d" % i, [64, T], BF16) for i in range(2)])
                qs = Rot([P.sb("qb%d" % i, [64, 3, T], BF16) for i in range(2)])
                vs = Rot([P.sb("vb%d" % i, [128, 18, 128], BF16) for i in range(2)])
                for vv in vs.bufs:
                    P.op("pool", lambda e, vv=vv: e.memset(vv[:, :, 64:128], 1.0), [], [vv])
                for kvh in range(4):
                    kT = kTs.next(); q = qs.next(); v = vs.next()
                    P.dma("sp", lambda e, kT=kT, kvh=kvh: e.dma_start(out=kT[:], in_=KBT[kvh * 64:(kvh + 1) * 64, :]), KBT, kT)
                    P.dma("sp", lambda e, q=q, kvh=kvh: e.dma_start(out=q[:], in_=QBT[kvh * 192:(kvh + 1) * 192, :].rearrange("(g d) t -> d g t", g=3)), QBT, q)
                    P.dma("sp", lambda e, v=v, kvh=kvh: e.dma_start(out=v[:, :, 0:64], in_=VB[:, kvh * 64:(kvh + 1) * 64].rearrange("(t p) d -> p t d", p=128)), VB, v)
                    qtiles = []
                    for qt in range(16):
                        keys = [(0, None), (1, None)]
                        if qt > 0:
                            keys.append((2 + qt - 1, 0))
                        keys.append((2 + qt, None))
                        if qt < 15:
                            keys.append((2 + qt + 1, 1))
                        qtiles.append((256 + qt * 128, keys))
                    if not last:
                        qtiles.append((0, [(0, None), (1, None)]))
                        qtiles.append((128, [(0, None), (1, None)]))
                    for (tok0, keys) in qtiles:
                        def s_mm(pS, kt, tok0=tok0, kT=kT, q=q):
                            P.op("pe", lambda e: e.matmul(pS[:, 0:384].rearrange("p (g t) -> p g t", g=3), lhsT=kT[0:64, kt * 128:(kt + 1) * 128], rhs=q[0:64, :, tok0:tok0 + 128], start=True, stop=True), [kT, q], [pS])
                        attend(keys, s_mm, lambda kt, v=v: (v, v[:, kt, :]), 64, 384, 0.125,
                               lambda pO, pD, kvh=kvh, tok0=tok0: finalize(pO, pD, 64, 384, 512 + kvh * 192, tok0, three=True, fold=True), fold=True, sink=kvh)
                flush()

            with P.scope():
                kTs = Rot([P.sb("ka%d" % i, [64, T], BF16) for i in range(2)])
                qs = Rot([P.sb("qa%d" % i, [64, T], BF16) for i in range(2)])
                v0s = Rot([P.sb("va0_%d" % i, [128, 18, 128], BF16) for i in range(2)])
                v1s = Rot([P.sb("va1_%d" % i, [128, 17, 128], BF16) for i in range(2)])
                for vv in v0s.bufs + v1s.bufs:
                    P.op("pool", lambda e, vv=vv: e.memset(vv[:, :, 64:128], 1.0), [], [vv])
                gfs = Rot([P.sb("gf%d" % i, [128, 896], F32) for i in range(2)])
                Gs = Rot([P.sb("G%d" % i, [128, 16, 64], BF16) for i in range(2)])
                for h in range(8):
                    kT = kTs.next(); q = qs.next(); v0 = v0s.next(); v1 = v1s.next(); gf = gfs.next(); G = Gs.next()
                    P.dma("sp", lambda e, kT=kT, h=h: e.dma_start(out=kT[:], in_=KAT[h * 64:(h + 1) * 64, :]), KAT, kT)
                    P.dma("sp", lambda e, q=q, h=h: e.dma_start(out=q[:], in_=QAT[h * 64:(h + 1) * 64, :]), QAT, q)
                    P.dma("sp", lambda e, v0=v0, h=h: e.dma_start(out=v0[:, :, 0:64], in_=VA[:, h * 64:(h + 1) * 64].rearrange("(t p) d -> p t d", p=128)), VA, v0)
                    P.dma("sp", lambda e, v1=v1, h=h: e.dma_start(out=v1[:, :, 0:64], in_=VA[64:64 + 17 * 128, h * 64:(h + 1) * 64].rearrange("(t p) d -> p t d", p=128)), VA, v1)
                    P.dma("sp", lambda e, gf=gf, h=h: e.dma_start(out=gf[:], in_=gb_d[l, :, h * 896:(h + 1) * 896]), gb_d, gf)
                    P.op("act", lambda e, gf=gf: e.activation(out=gf[:], in_=gf[:], func=AF.Exp), [gf], [gf])
                    for m in range(14):
                        P.op("dve", lambda e, gf=gf, G=G, m=m: e.tensor_tensor(out=G[:, m, :], in0=gf[:, m * 64:(m + 1) * 64], in1=namask[:], op=ALU.mult), [gf, namask], [G])
                    def pv_stage(pt, ri, rs_, pO, pD, fin, h=h, v0=v0, v1=v1):
                        for i in range(6):
                            if i < 2:
                                vb, vap = v0, v0[:, i, :]
                            elif rs_ % 2 == 0:
                                vb, vap = v0, v0[:, 2 + rs_ // 2 + (i - 2), :]
                            else:
                                vb, vap = v1, v1[:, (3 + rs_) // 2 + (i - 2), :]
                            P.op("pe", lambda e, i=i, vap=vap: e.matmul(pO[0:128, ri * 64:(ri + 1) * 64], lhsT=vap, rhs=pt[:, i * 64:(i + 1) * 64], start=(i == 0), stop=(i == 5)), [vb, pt], [pO])
                        if fin is not None:
                            finalize(pO, None, 64, 512, h * 64, fin, fold=True)

                    for rg in range(4):
                        pO = psO.next()
                        pD = None
                        for ri in range(8):
                            r = rg * 8 + ri
                            rs_ = min(max(r - 4, 0), 24)
                            m0 = rs_ - r + 7
                            tq = 256 + r * 64
                            ks = 256 + rs_ * 64
                            pS = psS.next()
                            for i in range(6):
                                k0 = i * 128 if i < 2 else ks + (i - 2) * 128
                                P.op("pe", lambda e, i=i, k0=k0, pS=pS, tq=tq, kT=kT, q=q: e.matmul(pS[:, i * 64:(i + 1) * 64], lhsT=kT[0:64, k0:k0 + 128], rhs=q[0:64, tq:tq + 64], start=True, stop=True), [kT, q], [pS])
                            pt = ptp.next()
                            P.op("act", lambda e, pS=pS, pt=pt: e.activation(out=pt[:, 0:384], in_=pS[:, 0:384], func=AF.Exp, scale=0.125), [pS], [pt])
                            P.op("dve", lambda e, pt=pt, G=G, m0=m0: e.tensor_tensor(out=pt[:, 128:384].rearrange("p (j c) -> p j c", c=64), in0=pt[:, 128:384].rearrange("p (j c) -> p j c", c=64), in1=G[:, m0:m0 + 8, :].rearrange("p (j two) c -> p j two c", two=2)[:, :, 0, :], op=ALU.mult), [pt, G], [pt])
                            push(lambda a=(pt, ri, rs_, pO, pD, (256 + rg * 512) if ri == 7 else None), f=pv_stage: f(*a))
                    if not last:
                        def s_mm(pS, kt, kT=kT, q=q):
                            P.op("pe", lambda e: e.matmul(pS[:, 0:256], lhsT=kT[0:64, kt * 128:(kt + 1) * 128], rhs=q[0:64, 0:256], start=True, stop=True), [kT, q], [pS])
                        attend([(0, None), (1, None)], s_mm, lambda kt, v0=v0: (v0, v0[:, kt, :]), 64, 256, 0.125,
                               lambda pO, pD, h=h: finalize(pO, pD, 64, 256, h * 64, 0, fold=True), fold=True)
                flush()
            while bg_jobs:
                bg_jobs.pop(0)()

    def phase_D(l, last):
        with P.scope():
            wout = P.sb("wout", [128, 16, D], BF16)
            gate_row = [P.sb("gate_l", [128, D], F32), P.sb("gate_c", [128, D], F32)]
            for s in range(2):
                P.dma("sp", lambda e, s=s: e.dma_start(out=gate_row[s][:], in_=MODs[l][s, 4096:6144].partition_broadcast(128)), MODs[l], gate_row[s])
            for cb in range(4):
                P.dma("sp", lambda e, cb=cb: e.dma_start(out=wout[:, :, cb * 512:(cb + 1) * 512], in_=WOUTs[l][:, cb * 512:(cb + 1) * 512].rearrange("(c p) n -> p c n", p=128)), WOUTs[l], wout)
            yts = Rot([P.sb("yT%d" % i, [128, 16, 512], BF16) for i in range(2)])
            xts = Rot([P.sb("xt%d" % i, [128, D], F32) for i in range(2)])
            ots = Rot([P.sb("ot%d" % i, [128, D], F32) for i in range(2)])
            tmp = Rot([P.sb("tm%d" % i, [128, 512], F32) for i in range(2)])
            psA = Rot(psb[0:4])
            blocks = list(TB[1:])
            if not last:
                blocks.append(TB[0])
            for (t0, n) in blocks:
                yt = yts.next()
                P.dma("sp", lambda e, yt=yt, t0=t0, n=n: e.dma_start(out=yt[:, :, :n], in_=YT[:, t0:t0 + n].rearrange("(c p) t -> p c t", p=128)), YT, yt)
                for ti in range(n // 128):
                    isctx = t0 < 256
                    r0 = (t0 + ti * 128) if isctx else (t0 - 256 + ti * 128)
                    if l == 0:
                        src = ctx_d if isctx else x_d
                    else:
                        src = XC1 if isctx else X1
                    if last:
                        dst = out_d
                    else:
                        dst = XC1 if isctx else X1
                    g = gate_row[1 if isctx else 0]
                    xt = xts.next()
                    ot = ots.next()
                    P.dma("sp", lambda e, xt=xt, src=src, r0=r0: e.dma_start(out=xt[:], in_=src[r0:r0 + 128, :]), src, xt)
                    for cb in range(4):
                        po = psA.next()
                        for kc in range(16):
                            P.op("pe", lambda e, kc=kc, po=po, yt=yt, ti=ti, cb=cb: e.matmul(po[:, :], lhsT=yt[:, kc, ti * 128:(ti + 1) * 128], rhs=wout[:, kc, cb * 512:(cb + 1) * 512], start=(kc == 0), stop=(kc == 15)), [yt, wout], [po])
                        tm = tmp.next()
                        P.op("dve", lambda e, tm=tm, po=po, g=g, cb=cb: e.tensor_tensor(out=tm[:], in0=po[:], in1=g[:, cb * 512:(cb + 1) * 512], op=ALU.mult), [po, g], [tm])
                        P.op("pool", lambda e, tm=tm, ot=ot, xt=xt, cb=cb: e.tensor_tensor(out=ot[:, cb * 512:(cb + 1) * 512], in0=tm[:], in1=xt[:, cb * 512:(cb + 1) * 512], op=ALU.add), [tm, xt], [ot])
                    P.dma("pool", lambda e, ot=ot, dst=dst, r0=r0: e.dma_start(out=dst[r0:r0 + 128, :], in_=ot[:]), ot, dst)

    bg_next = (nlayers == 2)
    for l in range(nlayers):
        last = (l == nlayers - 1) and nlayers == 2
        if l == 0 or not bg_next:
            phase_cast_ada(l)
        else:
            ada_tail(l)
        if stop == "ada":
            break
        phase_B(l)
        if stop in ("B1", "B"):
            break
        phase_B3(l)
        if stop == "B3":
            break
        phase_C(l, last)
        if stop == "C":
            break
        phase_D(l, last)
    P.barrier()
    P.emit()
    return nc


def _consts():
    ident = np.eye(128, dtype=np.float32)
    bd = np.zeros((128, 128), np.float32)
    bd[:64, :64] = 1
    bd[64:, 64:] = 1
    on = np.ones((128, 128), np.float32)
    r64 = np.zeros((64, 64), np.float32)
    for i in range(16):
        r64[i + 16, i] = -1
        r64[i, i + 16] = 1
        r64[i + 48, i + 32] = -1
        r64[i + 32, i + 48] = 1
    rot = np.zeros((128, 128), np.float32)
    rot[:64, :64] = r64
    rot[64:, 64:] = r64
    cmat = np.ascontiguousarray(np.stack([ident, bd, on, rot], axis=1))
    t = np.arange(2048)
    row = (t // 64).astype(np.float32)
    col = (t % 64).astype(np.float32)
    inv = (np.float32(10000.0) ** (-np.arange(16, dtype=np.float32) / np.float32(16))).astype(np.float32)
    ar = row[:, None] * inv
    ac = col[:, None] * inv
    ang = np.concatenate([ar, ar, ac, ac], -1)
    cos = np.cos(ang).astype(np.float32).T
    sin = np.sin(ang).astype(np.float32).T
    cossin = np.zeros((128, 2, 2048), np.float32)
    cossin[:64, 0] = cos
    cossin[64:, 0] = cos
    cossin[:64, 1] = sin
    cossin[64:, 1] = sin
    kc = np.arange(128) % 64
    c = np.arange(64)
    qs = np.clip(c - 8, 0, 48)
    namask = ((kc[:, None] >= qs[None, :]) & (kc[:, None] < qs[None, :] + 16)).astype(np.float32)
    i = np.arange(128)
    lo = (i[None, :] <= i[:, None]).astype(np.float32)
    hi = (i[:, None] <= i[None, :]).astype(np.float32)
    swamask = np.stack([np.tile(lo, (1, 3)), np.tile(hi, (1, 3))], axis=1).astype(np.float32)
    return cmat, cossin, namask, np.ascontiguousarray(swamask)


def _gather_rpb(rpb):
    p = np.arange(128)
    kc = p % 64
    half = p // 64
    c = np.arange(64)
    dc = np.clip(kc[:, None] - c[None, :], -15, 15) + 15
    m = np.arange(14)
    dr = m[None, :, None] + half[:, None, None]
    g = rpb[:, :, dr, dc[:, None, :]]
    g = np.transpose(g, (0, 2, 1, 3, 4)).reshape(2, 128, 8 * 14 * 64)
    return np.ascontiguousarray(g.astype(np.float32))


def _cols(norm_w, qn_a, kn_a, qn_b, kn_b, qa_norm, kva_norm, qn_c, kn_c):
    out = np.ones((2, 128, 34), np.float32)
    for l in range(2):
        out[l, :, 0] = np.tile(qn_a[l], 2)
        out[l, :, 1] = np.tile(kn_a[l], 2)
        out[l, :, 2] = np.tile(qn_b[l], 2)
        out[l, :, 3] = np.tile(kn_b[l], 2)
        out[l, :, 4:10] = qa_norm[l].reshape(6, 128).T
        out[l, :, 10:14] = kva_norm[l].reshape(4, 128).T
        out[l, :, 14] = qn_c[l][:128]
        out[l, :64, 15] = qn_c[l][128:]
        out[l, :, 16] = kn_c[l][:128]
        out[l, :64, 17] = kn_c[l][128:]
        out[l, :, 18:34] = norm_w[l].reshape(16, 128).T
    return out


def make_in_maps(x, c, ctx, c_ctx, norm_w, w_ada, b_ada, w_in, qn_a, kn_a, rpb_a, qn_b, kn_b, sink_b,
                 qa_norm, kva_norm, w_qb, w_kvb, qn_c, kn_c, w_out):
    f = lambda a: np.ascontiguousarray(np.asarray(a, dtype=np.float32))
    x, c, ctx, c_ctx = f(x), f(c), f(ctx), f(c_ctx)
    cmat, cossin, namask, swamask = _consts()
    cols = _cols(f(norm_w), f(qn_a), f(kn_a), f(qn_b), f(kn_b), f(qa_norm), f(kva_norm), f(qn_c), f(kn_c))
    gb = _gather_rpb(f(rpb_a))
    sink = np.ascontiguousarray(np.broadcast_to(f(sink_b)[:, None, :], (2, 64, 12)))
    shared = dict(w_ada=f(w_ada), b_ada=f(b_ada), w_in=f(w_in), w_qb=f(w_qb), w_kvb=f(w_kvb), w_out=f(w_out),
                  cols=cols, sink=sink, gb=gb, cmat=cmat, cossin=cossin, namask=namask, swamask=swamask)
    maps = []
    for b in range(8):
        cc = np.zeros((128, 16, 2), np.float32)
        cc[:, :, 0] = c[b].reshape(16, 128).T
        cc[:, :, 1] = c_ctx.reshape(16, 128).T
        m = dict(shared)
        m["x"] = x[b]
        m["ctx"] = ctx[b]
        m["cc"] = np.ascontiguousarray(cc.reshape(128, 32))
        maps.append(m)
    return maps


def kernel(**inputs):
    maps = make_in_maps(**inputs)
    nc = build(2)
    res = run_bass_kernel_spmd(nc, maps, core_ids=list(range(8)))
    return np.stack([np.asarray(r["out"], dtype=np.float32) for r in res.results], axis=0)
```

```python
import contextlib
import numpy as np
import concourse.bass as bass
import concourse.mybir as mybir
from concourse.bass_utils import run_bass_kernel_spmd

F32 = mybir.dt.float32
BF16 = mybir.dt.bfloat16
AF = mybir.ActivationFunctionType
ALU = mybir.AluOpType

CH = 30000
DCH = 1800
D = 2048
T = 2304
EPS = 1e-6
N_IN = 6208
TB = [(0, 256), (256, 512), (768, 512), (1280, 512), (1792, 512)]


class Buf:
    def __init__(self, t, name, is_dram=False, is_psum=False):
        self.t = t
        self.name = name
        self.is_dram = is_dram
        self.is_psum = is_psum
        self.writers = {}
        self.readers = {}

    def __getitem__(self, k):
        return self.t[k]


class Prog:
    ENG = ["pe", "act", "dve", "pool", "sp"]

    def __init__(self, nc):
        self.nc = nc
        self.ops = {e: [] for e in self.ENG}
        self.cnt = {e: 0 for e in self.ENG}
        self.seen = {e: {} for e in self.ENG}
        self.sems = {}
        self.ndma = {}
        self.latest = {}
        self.stack = None
        self.uid = 0

    def sb(self, name, shape, dtype):
        self.uid += 1
        t = self.stack.enter_context(self.nc.sbuf_tensor("%s_u%d" % (name, self.uid), list(shape), dtype))
        return Buf(t, name)

    def gsb(self, name, shape, dtype):
        return Buf(self.nc.alloc_sbuf_tensor(name, list(shape), dtype), name)

    def ps(self, name, shape, dtype=F32):
        return Buf(self.nc.alloc_psum_tensor(name, list(shape), dtype), name, is_psum=True)

    def dram(self, name, shape, dtype, kind="Internal"):
        return Buf(self.nc.dram_tensor(name, list(shape), dtype, kind=kind), name, is_dram=True)

    def sem(self, key):
        if key not in self.sems:
            self.sems[key] = self.nc.alloc_semaphore("s%d" % len(self.sems))
        return self.sems[key]

    def _collect(self, E, reads, writes, own_keys):
        deps = {}

        def add(k, v):
            if deps.get(k, 0) < v:
                deps[k] = v
        for r in reads:
            for k, v in r.writers.items():
                if E == "pe" and k[:2] == ("eng", "pe"):
                    continue
                add(k, v)
            if r.is_psum:
                for k, v in r.readers.items():
                    if k[:2] not in own_keys:
                        add(k, v)
        for w in writes:
            for k, v in w.writers.items():
                if k[:2] not in own_keys:
                    add(k, v)
            for k, v in w.readers.items():
                if k[:2] not in own_keys:
                    add(k, v)
        waits = []
        for k, v in deps.items():
            if self.seen[E].get(k, 0) >= v:
                continue
            self.seen[E][k] = v
            waits.append((self.sem(k), v))
        return waits

    def _commit(self, ev, reads, writes):
        k, v = ev
        self.latest[k] = v
        for r in reads:
            if r.readers.get(k, 0) < v:
                r.readers[k] = v
        for w in writes:
            if w.writers.get(k, 0) < v:
                w.writers[k] = v
            w.readers = {}

    def op(self, E, fn, reads=(), writes=()):
        waits = self._collect(E, reads, writes, (("eng", E),))
        idx = self.cnt[E]
        self.cnt[E] += 1
        key = ("eng", E, idx // CH)
        val = idx % CH + 1
        self.ops[E].append((waits, fn, self.sem(key), 1))
        self._commit((key, val), reads, writes)

    def dma(self, Q, fn, src, dst):
        side = dst if src.is_dram else src
        n = self.ndma.get(side.name, 0)
        key = ("dma", side.name, n // DCH)
        waits = self._collect(Q, [src], [dst], (("dma", side.name),))
        val = 16 * (n % DCH + 1)
        self.ndma[side.name] = n + 1
        self.ops[Q].append((waits, fn, self.sem(key), 16))
        self._commit((key, val), [src], [dst])

    def barrier(self):
        for E in self.ENG:
            waits = []
            for k, v in self.latest.items():
                if self.seen[E].get(k, 0) >= v:
                    continue
                self.seen[E][k] = v
                waits.append((self.sem(k), v))
            if waits:
                self.ops[E].append((waits, None, None, 0))

    @contextlib.contextmanager
    def scope(self):
        old = self.stack
        with contextlib.ExitStack() as st:
            self.stack = st
            yield
            self.barrier()
        self.stack = old

    def emit(self):
        nc = self.nc
        ops = self.ops

        def run(eng, lst):
            for waits, fn, sem, inc in lst:
                for s, v in waits:
                    eng.wait_ge(s, v)
                if fn is not None:
                    fn(eng).then_inc(sem, inc)

        with nc.Block() as block:
            @block.sync
            def _(e):
                run(e, ops["sp"])

            @block.tensor
            def _(e):
                run(e, ops["pe"])

            @block.scalar
            def _(e):
                run(e, ops["act"])

            @block.vector
            def _(e):
                run(e, ops["dve"])

            @block.gpsimd
            def _(e):
                run(e, ops["pool"])


class Rot:
    def __init__(self, bufs):
        self.bufs = bufs
        self.i = 0

    def next(self):
        b = self.bufs[self.i % len(self.bufs)]
        self.i += 1
        return b


def build(nlayers=2, dbg=(), stop=None):
    nc = bass.Bass("TRN2", target_bir_lowering=False)
    P = Prog(nc)
    EI = "ExternalInput"

    def scr(name, shape, dtype):
        return P.dram(name, shape, dtype, kind=("ExternalOutput" if name in dbg else "Internal"))

    x_d = P.dram("x", [2048, D], F32, EI)
    ctx_d = P.dram("ctx", [256, D], F32, EI)
    cc_d = P.dram("cc", [128, 32], F32, EI)
    w_ada_d = P.dram("w_ada", [2, D, 6144], F32, EI)
    b_ada_d = P.dram("b_ada", [2, 6144], F32, EI)
    w_in_d = P.dram("w_in", [2, D, N_IN], F32, EI)
    w_qb_d = P.dram("w_qb", [2, 768, 1152], F32, EI)
    w_kvb_d = P.dram("w_kvb", [2, 512, 1536], F32, EI)
    w_out_d = P.dram("w_out", [2, D, D], F32, EI)
    cols_d = P.dram("cols", [2, 128, 34], F32, EI)
    sink_d = P.dram("sink", [2, 64, 12], F32, EI)
    gb_d = P.dram("gb", [2, 128, 8 * 896], F32, EI)
    cmat_d = P.dram("cmat", [128, 4, 128], F32, EI)
    cossin_d = P.dram("cossin", [128, 2, 2048], F32, EI)
    namask_d = P.dram("namask", [128, 64], F32, EI)
    swamask_d = P.dram("swamask", [128, 2, 384], F32, EI)
    out_d = P.dram("out", [2048, D], F32, "ExternalOutput")

    WINs = [scr("WIN%d" % i, [D, N_IN], BF16) for i in range(2)]
    WQBs = [scr("WQB%d" % i, [768, 1152], BF16) for i in range(2)]
    WKVBs = [scr("WKVB%d" % i, [512, 1536], BF16) for i in range(2)]
    WOUTs = [scr("WOUT%d" % i, [D, D], BF16) for i in range(2)]
    MODs = [scr("MOD" if i == 0 else "MOD1", [2, 6144], F32) for i in range(2)]
    QAT = scr("QAT", [512, T], BF16)
    KAT = scr("KAT", [512, T], BF16)
    VA = scr("VA", [T, 512], BF16)
    QBT = scr("QBT", [768, T], BF16)
    KBT = scr("KBT", [256, T], BF16)
    VB = scr("VB", [T, 256], BF16)
    CQN = scr("CQN", [768, T], BF16)
    CKVN = scr("CKVN", [512, T], BF16)
    KPE = scr("KPE", [64, T], F32)
    SZ = scr("SZ", [D, T], BF16)
    QCN = scr("QCN", [768, T], BF16)
    QCR = scr("QCR", [384, T], BF16)
    KCN = scr("KCN", [768, T], BF16)
    KCR = scr("KCR", [384, T], BF16)
    VC = scr("VC", [T, 768], BF16)
    YT = scr("YT", [D, T], BF16)
    X1 = scr("X1", [2048, D], F32)
    XC1 = scr("XC1", [256, D], F32)

    psb = [P.ps("ps%d" % i, [128, 512], F32) for i in range(8)]

    ident = P.gsb("ident", [128, 128], F32)
    bd64 = P.gsb("bd64", [128, 128], BF16)
    ones = P.gsb("ones", [128, 128], BF16)
    rotm = P.gsb("rotm", [128, 128], BF16)
    cos_t = P.gsb("cos_t", [128, 2048], F32)
    sin_t = P.gsb("sin_t", [128, 2048], F32)
    namask = P.gsb("namask_s", [128, 64], F32)
    swamask = P.gsb("swamask_s", [128, 2, 384], BF16)
    scT = P.gsb("scT", [128, 32], F32)
    cols = P.gsb("cols_s", [128, 34], F32)
    modcol = P.gsb("modcol", [128, 2, 32], F32)
    gcol = P.gsb("gcol", [128, 2, 16], F32)
    esink = P.gsb("esink", [64, 12], F32)
    ones2 = P.gsb("ones2", [1, 2], F32)
    epsc = P.gsb("epsc", [128, 1], F32)
    esr = P.gsb("esr", [33, 12, 128], BF16)
    selr = P.gsb("selr", [33, 128], BF16)
    onesf = P.gsb("onesf", [64, 128], F32)
    eshi = P.gsb("eshi", [64, 12], BF16)
    eslo = P.gsb("eslo", [64, 12], F32)

    with P.scope():
        cm = P.sb("cm", [128, 4, 128], F32)
        swf = P.sb("swf", [128, 2, 384], F32)
        cct = P.sb("cct", [128, 32], F32)
        P.dma("sp", lambda e: e.dma_start(out=cm[:], in_=cmat_d[:]), cmat_d, cm)
        P.dma("sp", lambda e: e.dma_start(out=swf[:], in_=swamask_d[:]), swamask_d, swf)
        P.dma("sp", lambda e: e.dma_start(out=cct[:], in_=cc_d[:]), cc_d, cct)
        P.dma("sp", lambda e: e.dma_start(out=cos_t[:], in_=cossin_d[:, 0, :]), cossin_d, cos_t)
        P.dma("sp", lambda e: e.dma_start(out=sin_t[:], in_=cossin_d[:, 1, :]), cossin_d, sin_t)
        P.dma("sp", lambda e: e.dma_start(out=namask[:], in_=namask_d[:]), namask_d, namask)
        P.op("dve", lambda e: e.tensor_copy(out=ident[:], in_=cm[:, 0, :]), [cm], [ident])
        P.op("dve", lambda e: e.tensor_copy(out=bd64[:], in_=cm[:, 1, :]), [cm], [bd64])
        P.op("dve", lambda e: e.tensor_copy(out=ones[:], in_=cm[:, 2, :]), [cm], [ones])
        P.op("dve", lambda e: e.tensor_copy(out=rotm[:], in_=cm[:, 3, :]), [cm], [rotm])
        P.op("dve", lambda e: e.tensor_copy(out=swamask[:], in_=swf[:]), [swf], [swamask])
        P.op("dve", lambda e: e.memset(ones2[:], 1.0), [], [ones2])
        P.op("dve", lambda e: e.memset(epsc[:], EPS), [], [epsc])
        P.op("dve", lambda e: e.memset(onesf[:], 1.0), [], [onesf])
        P.op("dve", lambda e: e.memset(selr[:], 0.0), [], [selr])
        P.op("dve", lambda e: e.memset(selr[0:1, 64:128], 1.0), [], [selr])
        P.op("dve", lambda e: e.memset(selr[32:33, 64:128], 1.0), [], [selr])
        P.op("dve", lambda e: e.memset(esr[:], 0.0), [], [esr])
        P.op("act", lambda e: e.activation(out=scT[:], in_=cct[:], func=AF.Silu), [cct], [scT])

    def make_bg(l, ps_bank, CW=3104, store_q="pool", which=("qb", "kvb", "out"), ada=True, extra=(), load_q="sp"):
        stg = [P.sb("stg%d" % i, [128, CW], F32) for i in range(2)]
        bft = [P.sb("bft%d" % i, [128, CW], BF16) for i in range(2)]
        wts = [P.sb("wada%d" % i, [128, 16, 256], F32) for i in range(2)]
        badas = [P.sb("bada%d" % i, [1, 256], F32) for i in range(2)]
        mods = [P.sb("modsb%d" % i, [2, 256], F32) for i in range(2)]
        cast = []
        for (ll, wh) in list(extra) + [(l, which)]:
            for (nm, src, dst, R, C) in [("in", w_in_d, WINs[ll], D, N_IN), ("qb", w_qb_d, WQBs[ll], 768, 1152),
                                         ("kvb", w_kvb_d, WKVBs[ll], 512, 1536), ("out", w_out_d, WOUTs[ll], D, D)]:
                if nm not in wh:
                    continue
                for rc in range(R // 128):
                    for c0 in range(0, C, CW):
                        cast.append((ll, src, dst, rc, c0, min(C, c0 + CW) - c0))

        def cast_stages(k, ll, src, dst, rc, c0, cw):
            st = stg[k % 2]
            bt = bft[k % 2]
            h = cw // 2

            def s0():
                P.dma(load_q if load_q != "alt" else "sp", lambda e: e.dma_start(out=st[:, :cw], in_=src[ll, rc * 128:(rc + 1) * 128, c0:c0 + cw]), src, st)

            def s1():
                P.op("pool", lambda e: e.tensor_copy(out=bt[:, :h], in_=st[:, :h]), [st], [bt])
                P.op("dve", lambda e: e.tensor_copy(out=bt[:, h:cw], in_=st[:, h:cw]), [st], [bt])

            def s2():
                P.dma(store_q, lambda e: e.dma_start(out=dst[rc * 128:(rc + 1) * 128, c0:c0 + cw], in_=bt[:, :cw]), bt, dst)
            return (s0, s1, s2)

        def ada_stages(k):
            w = wts[k % 2]
            bada = badas[k % 2]
            md = mods[k % 2]
            c0 = k * 256

            def s0():
                lq = load_q if load_q != "alt" else ("sp" if k % 2 == 0 else "act")
                P.dma(lq, lambda e: e.dma_start(out=w[:], in_=w_ada_d[l, :, c0:c0 + 256].rearrange("(c p) n -> p c n", p=128)), w_ada_d, w)
                P.dma(lq, lambda e: e.dma_start(out=bada[:], in_=b_ada_d[l:l + 1, c0:c0 + 256]), b_ada_d, bada)

            def s1():
                pm = ps_bank
                for kc in range(16):
                    P.op("pe", lambda e, kc=kc: e.matmul(pm[0:2, 0:256], lhsT=scT[:, 2 * kc:2 * kc + 2], rhs=w[:, kc, :], start=(kc == 0), stop=False), [scT, w], [pm])
                P.op("pe", lambda e: e.matmul(pm[0:2, 0:256], lhsT=ones2[0:1, 0:2], rhs=bada[0:1, :], start=False, stop=True), [ones2, bada], [pm])
                P.op("act", lambda e: e.activation(out=md[0:2, :], in_=pm[0:2, 0:256], func=AF.Copy), [pm], [md])

            def s2():
                P.dma("pool", lambda e: e.dma_start(out=MODs[l][:, c0:c0 + 256], in_=md[0:2, :]), md, MODs[l])
            return (s0, s1, s2)

        def lagged(stages):
            ticks = []
            n = len(stages)
            for t in range(n + 2):
                def tick(t=t):
                    if t < n:
                        stages[t][0]()
                    if 0 <= t - 1 < n:
                        stages[t - 1][1]()
                    if 0 <= t - 2 < n:
                        stages[t - 2][2]()
                ticks.append(tick)
            return ticks
        ct = lagged([cast_stages(k, *c) for k, c in enumerate(cast)])
        at = lagged([ada_stages(k) for k in range(24)]) if ada else []
        jobs = []
        while ct or at:
            for _ in range(3):
                if ct:
                    jobs.append(ct.pop(0))
            if at:
                jobs.append(at.pop(0))
        return jobs

    def ada_tail(l):
        with P.scope():
            nwt = P.sb("nwt", [128, 2, 16], F32)
            snk = P.sb("snk", [64, 12], F32)
            P.dma("sp", lambda e: e.dma_start(out=cols[:], in_=cols_d[l]), cols_d, cols)
            P.dma("sp", lambda e: e.dma_start(out=snk[:], in_=sink_d[l]), sink_d, snk)
            P.op("act", lambda e: e.activation(out=esink[:], in_=snk[:], func=AF.Exp), [snk], [esink])
            P.op("dve", lambda e: e.tensor_copy(out=eshi[:], in_=esink[:]), [esink], [eshi])
            P.op("dve", lambda e: e.tensor_tensor(out=eslo[:], in0=esink[:], in1=eshi[:], op=ALU.subtract), [esink, eshi], [eslo])
            for hh in range(12):
                P.op("dve", lambda e, hh=hh: e.tensor_scalar(out=esr[0:1, hh, :], in0=onesf[0:1, :], scalar1=eshi[0:1, hh:hh + 1], scalar2=None, op0=ALU.mult), [onesf, eshi], [esr])
                P.op("dve", lambda e, hh=hh: e.tensor_scalar(out=esr[32:33, hh, :], in0=onesf[32:33, :], scalar1=eslo[32:33, hh:hh + 1], scalar2=None, op0=ALU.mult), [onesf, eslo], [esr])
            for s_ in range(2):
                P.dma("sp", lambda e, s_=s_: e.dma_start(out=modcol[:, s_, :], in_=MODs[l][s_, 0:4096].rearrange("(c p) -> p c", p=128), allow_slow_non_contiguous=True), MODs[l], modcol)
            for s_ in range(2):
                P.op("dve", lambda e, s_=s_: e.tensor_scalar(out=nwt[:, s_, :], in0=modcol[:, s_, 16:32], scalar1=1.0, scalar2=None, op0=ALU.add), [modcol], [nwt])
                P.op("dve", lambda e, s_=s_: e.tensor_tensor(out=gcol[:, s_, :], in0=nwt[:, s_, :], in1=cols[:, 18:34], op=ALU.mult), [nwt, cols], [gcol])

    def phase_cast_ada(l):
        with P.scope():
            for j in make_bg(l, psb[7], CW=N_IN, store_q="act", load_q="alt", which=(("qb", "kvb") if (l == 0 and bg_next) else ("qb", "kvb", "out"))):
                j()
        ada_tail(l)

    def rope_tail(P_, qn, M, t0, n, obf, f32p, psC):
        pr = psC.next()
        P.op("pe", lambda e: e.matmul(pr[0:M, :n], lhsT=rotm[0:M, 0:M], rhs=qn[0:M, :n], start=True, stop=True), [rotm, qn], [pr])
        t1 = f32p.next()
        t2 = f32p.next()
        c0 = t0 - 256
        e1 = "dve" if M == 128 else "pool"
        P.op(e1, lambda e: e.tensor_tensor(out=t1[0:M, :n], in0=qn[0:M, :n], in1=cos_t[0:M, c0:c0 + n], op=ALU.mult), [qn, cos_t], [t1])
        P.op("dve", lambda e: e.tensor_tensor(out=t2[0:M, :n], in0=pr[0:M, :n], in1=sin_t[0:M, c0:c0 + n], op=ALU.mult), [pr, sin_t], [t2])
        o = obf.next()
        P.op(e1, lambda e: e.tensor_tensor(out=o[0:M, :n], in0=t1[0:M, :n], in1=t2[0:M, :n], op=ALU.add), [t1, t2], [o])
        return o

    def rstd_from(pss, M, n, scale, rsp):
        rs = rsp.next()
        P.op("act", lambda e: e.activation(out=rs[0:M, :n], in_=pss[0:M, :n], func=AF.Ln, scale=scale, bias=epsc[0:M, :]), [pss, epsc], [rs])
        P.op("act", lambda e: e.activation(out=rs[0:M, :n], in_=rs[0:M, :n], func=AF.Exp, scale=-0.5), [rs], [rs])
        return rs

    def phase_B(l):
        with P.scope():
            hT = P.sb("hT", [128, 16, T], BF16)
            hT2 = Buf(hT.t, "hT2")
            with P.scope():
                xts = Rot([P.sb("xt%d" % i, [128, D], F32) for i in range(2)])
                xns = Rot([P.sb("xn%d" % i, [128, D], F32) for i in range(2)])
                junk = P.sb("junk", [128, D], BF16)
                sss = Rot([P.sb("ss%d" % i, [128, 1], F32) for i in range(2)])
                pT = Rot(psb[0:4])
                for tt in range(18):
                    s = 1 if tt < 2 else 0
                    if l == 0:
                        src = ctx_d if tt < 2 else x_d
                    else:
                        src = XC1 if tt < 2 else X1
                    r0 = tt * 128 if tt < 2 else (tt - 2) * 128
                    xt = xts.next()
                    xn = xns.next()
                    ss = sss.next()
                    P.dma("sp", lambda e, xt=xt, src=src, r0=r0: e.dma_start(out=xt[:], in_=src[r0:r0 + 128, :]), src, xt)
                    P.op("act", lambda e, xt=xt, ss=ss: e.activation(out=junk[:], in_=xt[:], func=AF.Square, accum_out=ss[:]), [xt], [junk, ss])
                    P.op("act", lambda e, ss=ss: e.activation(out=ss[:], in_=ss[:], func=AF.Sqrt, scale=1.0 / D, bias=EPS), [ss], [ss])
                    P.op("dve", lambda e, ss=ss: e.reciprocal(out=ss[:], in_=ss[:]), [ss], [ss])
                    P.op("dve", lambda e, xt=xt, xn=xn, ss=ss: e.tensor_scalar(out=xn[:], in0=xt[:], scalar1=ss[:], scalar2=None, op0=ALU.mult), [xt, ss], [xn])
                    for g4 in range(4):
                        pb = pT.next()
                        for c4 in range(4):
                            c = g4 * 4 + c4
                            P.op("pe", lambda e, pb=pb, xn=xn, c=c, c4=c4: e.transpose(out=pb[:, c4 * 128:(c4 + 1) * 128], in_=xn[:, c * 128:(c + 1) * 128], identity=ident[:]), [xn, ident], [pb])
                        for c4 in range(4):
                            c = g4 * 4 + c4
                            if g4 % 2 == 0:
                                P.op("act", lambda e, pb=pb, c=c, c4=c4, s=s, tt=tt: e.activation(out=hT[:, c, tt * 128:(tt + 1) * 128], in_=pb[:, c4 * 128:(c4 + 1) * 128], func=AF.Identity, scale=gcol[:, s, c:c + 1], bias=modcol[:, s, c:c + 1]), [pb, gcol, modcol], [hT])
                            else:
                                P.op("dve", lambda e, pb=pb, c=c, c4=c4, s=s, tt=tt: e.tensor_scalar(out=hT[:, c, tt * 128:(tt + 1) * 128], in0=pb[:, c4 * 128:(c4 + 1) * 128], scalar1=gcol[:, s, c:c + 1], scalar2=modcol[:, s, c:c + 1], op0=ALU.mult, op1=ALU.add), [pb, gcol, modcol], [hT2])
            if stop == "B1":
                return
            with P.scope():
                wts = Rot([P.sb("wt%d" % i, [128, 16, 768], BF16) for i in range(2)])
                sqp = Rot([P.sb("sq%d" % i, [128, 512], BF16) for i in range(3)])
                rsp = Rot([P.sb("rs%d" % i, [128, 512], F32) for i in range(3)])
                obf = Rot([P.sb("ob%d" % i, [128, 512], BF16) for i in range(6)])
                f32p = Rot([P.sb("f32_%d" % i, [128, 512], F32) for i in range(4)])
                raw = P.sb("raw", [128, 6, 512], F32)
                psA = Rot(psb[0:3])
                psB = Rot(psb[3:5])
                psC = Rot(psb[5:7])

                wstg = Rot([P.sb("wstg%d" % i, [128, 16, 128], F32) for i in range(3)])
                wgroups = [(0, 512), (512, 512), (1024, 512), (1536, 768), (2304, 256), (2560, 256), (2816, 768),
                           (3584, 512), (4096, 64)] + [(4160 + i * 512, 512) for i in range(4)]
                wcache = {}
                wticks = []

                def issue_w(idx, spread):
                    col0, ncols = wgroups[idx]
                    wt = wts.next()
                    subs = [(c, min(128, ncols - c)) for c in range(0, ncols, 128)]
                    sts = {}

                    def dma(k):
                        c, cw = subs[k]
                        st = wstg.next()
                        sts[k] = st
                        P.dma("sp", lambda e: e.dma_start(out=st[:, :, :cw], in_=w_in_d[l, :, col0 + c:col0 + c + cw].rearrange("(c p) n -> p c n", p=128)), w_in_d, st)

                    def cast(k):
                        c, cw = subs[k]
                        st = sts[k]
                        P.op("dve", lambda e: e.tensor_copy(out=wt[:, :, c:c + cw], in_=st[:, :, :cw]), [st], [wt])

                    n = len(subs)
                    for t in range(n + 3):
                        def tick(t=t):
                            if t - 3 >= 0:
                                cast(t - 3)
                            if t < n:
                                dma(t)
                        if spread:
                            wticks.append(tick)
                        else:
                            tick()
                    wcache[idx] = wt

                def wtick():
                    if wticks:
                        wticks.pop(0)()

                def load_w(col0, ncols):
                    idx = [g[0] for g in wgroups].index(col0)
                    assert wgroups[idx][1] == ncols
                    while wticks:
                        wticks.pop(0)()
                    if idx not in wcache:
                        issue_w(idx, False)
                    wt = wcache[idx]
                    if idx + 1 < len(wgroups) and (idx + 1) not in wcache:
                        issue_w(idx + 1, True)
                    return wt

                def main_mm(wt, j, M, t0, n):
                    wtick()
                    pu = psA.next()
                    for kc in range(16):
                        P.op("pe", lambda e, kc=kc: e.matmul(pu[0:M, :n], lhsT=wt[:, kc, j * 128:j * 128 + M], rhs=hT[:, kc, t0:t0 + n], start=(kc == 0), stop=(kc == 15)), [wt, hT, hT2], [pu])
                    return pu

                def headnorm_group(col0, nch, gi, dst, do_rope, tbs=TB):
                    wt = load_w(col0, nch * 128)
                    q1 = []
                    q2 = []

                    def stage1(pu, sq, j, t0, n):
                        pss = psB.next()
                        P.op("pe", lambda e: e.matmul(pss[:, :n], lhsT=bd64[:], rhs=sq[:, :n], start=True, stop=True), [bd64, sq], [pss])
                        rs = rstd_from(pss, 128, n, 1.0 / 64, rsp)
                        o1 = obf.next()
                        P.op("dve", lambda e: e.scalar_tensor_tensor(out=o1[:, :n], in0=pu[:, :n], scalar=cols[:, gi:gi + 1], in1=rs[:, :n], op0=ALU.mult, op1=ALU.mult), [pu, cols, rs], [o1])
                        q2.append((o1, j, t0, n))

                    def stage2(o1, j, t0, n):
                        if do_rope and t0 >= 256:
                            o2 = rope_tail(P, o1, 128, t0, n, obf, f32p, psC)
                        else:
                            o2 = o1
                        P.dma("pool", lambda e: e.dma_start(out=dst[j * 128:(j + 1) * 128, t0:t0 + n], in_=o2[:, :n]), o2, dst)

                    for j in range(nch):
                        for (t0, n) in tbs:
                            pu = main_mm(wt, j, 128, t0, n)
                            sq = sqp.next()
                            P.op("act", lambda e, pu=pu, sq=sq, n=n: e.activation(out=sq[:, :n], in_=pu[:, :n], func=AF.Square), [pu], [sq])
                            if q2:
                                stage2(*q2.pop(0))
                            if q1:
                                stage1(*q1.pop(0))
                            q1.append((pu, sq, j, t0, n))
                    while q1 or q2:
                        if q2:
                            stage2(*q2.pop(0))
                        if q1:
                            stage1(*q1.pop(0))

                def allnorm_group(col0, nch, gi, dst, tbs=TB):
                    wt = load_w(col0, nch * 128)
                    for (t0, n) in tbs:
                        pss = psB.next()
                        pend = None
                        for j in range(nch):
                            pu = main_mm(wt, j, 128, t0, n)
                            sq = sqp.next()
                            P.op("act", lambda e, pu=pu, j=j, n=n: e.activation(out=raw[:, j, :n], in_=pu[:, :n], func=AF.Copy), [pu], [raw])
                            P.op("act", lambda e, pu=pu, sq=sq, n=n: e.activation(out=sq[:, :n], in_=pu[:, :n], func=AF.Square), [pu], [sq])
                            if pend is not None:
                                pj, psq = pend
                                P.op("pe", lambda e, pj=pj, psq=psq, n=n, pss=pss: e.matmul(pss[:, :n], lhsT=ones[:], rhs=psq[:, :n], start=(pj == 0), stop=False), [ones, psq], [pss])
                            pend = (j, sq)
                        pj, psq = pend
                        P.op("pe", lambda e, pj=pj, psq=psq, n=n, pss=pss: e.matmul(pss[:, :n], lhsT=ones[:], rhs=psq[:, :n], start=(pj == 0), stop=True), [ones, psq], [pss])
                        rs = rstd_from(pss, 128, n, 1.0 / (nch * 128), rsp)
                        for j in range(nch):
                            o = obf.next()
                            P.op("dve", lambda e, o=o, j=j, n=n, rs=rs: e.scalar_tensor_tensor(out=o[:, :n], in0=raw[:, j, :n], scalar=cols[:, gi + j:gi + j + 1], in1=rs[:, :n], op0=ALU.mult, op1=ALU.mult), [raw, cols, rs], [o])
                            P.dma("pool", lambda e, o=o, j=j, t0=t0, n=n: e.dma_start(out=dst[j * 128:(j + 1) * 128, t0:t0 + n], in_=o[:, :n]), o, dst)

                def v_group(col0, ncols, dst):
                    wt = load_w(col0, ncols)
                    for tt in range(18):
                        wtick()
                        pv = psA.next()
                        for kc in range(16):
                            P.op("pe", lambda e, kc=kc, tt=tt, pv=pv: e.matmul(pv[:, :ncols], lhsT=hT[:, kc, tt * 128:(tt + 1) * 128], rhs=wt[:, kc, :ncols], start=(kc == 0), stop=(kc == 15)), [hT, hT2, wt], [pv])
                        o = obf.next()
                        if tt % 2 == 0:
                            P.op("act", lambda e, o=o, pv=pv: e.activation(out=o[:, :ncols], in_=pv[:, :ncols], func=AF.Copy), [pv], [o])
                        else:
                            P.op("dve", lambda e, o=o, pv=pv: e.tensor_copy(out=o[:, :ncols], in_=pv[:, :ncols]), [pv], [o])
                        P.dma("pool", lambda e, o=o, tt=tt: e.dma_start(out=dst[tt * 128:(tt + 1) * 128, :], in_=o[:, :ncols]), o, dst)

                def kpe_group():
                    wt = load_w(4096, 64)
                    for (t0, n) in TB:
                        pu = main_mm(wt, 0, 64, t0, n)
                        o = f32p.next()
                        P.op("act", lambda e, o=o, pu=pu, n=n: e.activation(out=o[0:64, :n], in_=pu[0:64, :n], func=AF.Copy), [pu], [o])
                        P.dma("pool", lambda e, o=o, t0=t0, n=n: e.dma_start(out=KPE[:, t0:t0 + n], in_=o[0:64, :n]), o, KPE)

                def z_group():
                    for half in range(4):
                        wt = load_w(4160 + half * 512, 512)
                        for j in range(4):
                            for (t0, n) in qtb:
                                pu = main_mm(wt, j, 128, t0, n)
                                o = obf.next()
                                P.op("act", lambda e, o=o, pu=pu, n=n: e.activation(out=o[:, :n], in_=pu[:, :n], func=AF.Silu), [pu], [o])
                                r = (half * 4 + j) * 128
                                P.dma("pool", lambda e, o=o, r=r, t0=t0, n=n: e.dma_start(out=SZ[r:r + 128, t0:t0 + n], in_=o[:, :n]), o, SZ)

                qtb = TB[1:] if (l == nlayers - 1 and nlayers == 2) else TB
                headnorm_group(0, 4, 0, QAT, False, qtb)
                headnorm_group(512, 4, 1, KAT, False)
                v_group(1024, 512, VA)
                headnorm_group(1536, 6, 2, QBT, True, qtb)
                headnorm_group(2304, 2, 3, KBT, True)
                v_group(2560, 256, VB)
                allnorm_group(2816, 6, 4, CQN, qtb)
                allnorm_group(3584, 4, 10, CKVN)
                kpe_group()
                z_group()

    def phase_B3(l):
        with P.scope():
            wqb = P.sb("wqb", [128, 6, 1152], BF16)
            wkvb = P.sb("wkvb", [128, 4, 1536], BF16)
            P.dma("sp", lambda e: e.dma_start(out=wqb[:], in_=WQBs[l][:].rearrange("(c p) n -> p c n", p=128)), WQBs[l], wqb)
            P.dma("sp", lambda e: e.dma_start(out=wkvb[:], in_=WKVBs[l][:].rearrange("(c p) n -> p c n", p=128)), WKVBs[l], wkvb)
            cqs = Rot([P.sb("cq%d" % i, [128, 6, 512], BF16) for i in range(2)])
            ckvs = Rot([P.sb("ckv%d" % i, [128, 4, 512], BF16) for i in range(2)])
            kpes = Rot([P.sb("kpe%d" % i, [64, 512], F32) for i in range(2)])
            sqks = Rot([P.sb("sqk%d" % i, [64, 512], BF16) for i in range(2)])
            sqp = Rot([P.sb("sq%d" % i, [128, 512], BF16) for i in range(9)])
            rawp = Rot([P.sb("raw%d" % i, [128, 512], F32) for i in range(9)])
            rsp = Rot([P.sb("rs%d" % i, [128, 512], F32) for i in range(4)])
            obf = Rot([P.sb("ob%d" % i, [128, 512], BF16) for i in range(12)])
            f32p = Rot([P.sb("f32_%d" % i, [128, 512], F32) for i in range(4)])
            ovs = Rot([P.sb("ov%d" % i, [128, 768], BF16) for i in range(2)])
            psA = Rot(psb[0:4])
            psB = Rot(psb[4:6])
            psC = Rot(psb[6:8])
            qB = []
            qC = []

            def stageB(h, t0, n, kpe, sqk, rN, rR, rK, sqN, sqR, sqK):
                pq = psB.next()
                P.op("pe", lambda e: e.matmul(pq[:, :n], lhsT=ones[:], rhs=sqN[:, :n], start=True, stop=False), [ones, sqN], [pq])
                P.op("pe", lambda e: e.matmul(pq[:, :n], lhsT=ones[0:64, :], rhs=sqR[0:64, :n], start=False, stop=True), [ones, sqR], [pq])
                pk = psB.next()
                P.op("pe", lambda e: e.matmul(pk[:, :n], lhsT=ones[:], rhs=sqK[:, :n], start=True, stop=False), [ones, sqK], [pk])
                P.op("pe", lambda e: e.matmul(pk[:, :n], lhsT=ones[0:64, :], rhs=sqk[0:64, :n], start=False, stop=True), [ones, sqk], [pk])
                rq = rstd_from(pq, 128, n, 1.0 / 192, rsp)
                rk = rstd_from(pk, 128, n, 1.0 / 192, rsp)
                oN = obf.next(); oR = obf.next(); oK = obf.next(); oKR = obf.next()
                P.op("dve", lambda e: e.scalar_tensor_tensor(out=oN[:, :n], in0=rN[:, :n], scalar=cols[:, 14:15], in1=rq[:, :n], op0=ALU.mult, op1=ALU.mult), [rN, cols, rq], [oN])
                P.op("dve", lambda e: e.scalar_tensor_tensor(out=oR[0:64, :n], in0=rR[0:64, :n], scalar=cols[0:64, 15:16], in1=rq[0:64, :n], op0=ALU.mult, op1=ALU.mult), [rR, cols, rq], [oR])
                P.op("dve", lambda e: e.scalar_tensor_tensor(out=oK[:, :n], in0=rK[:, :n], scalar=cols[:, 16:17], in1=rk[:, :n], op0=ALU.mult, op1=ALU.mult), [rK, cols, rk], [oK])
                P.op("dve", lambda e: e.scalar_tensor_tensor(out=oKR[0:64, :n], in0=kpe[0:64, :n], scalar=cols[0:64, 17:18], in1=rk[0:64, :n], op0=ALU.mult, op1=ALU.mult), [kpe, cols, rk], [oKR])
                P.dma("act", lambda e: e.dma_start(out=QCN[h * 128:(h + 1) * 128, t0:t0 + n], in_=oN[:, :n]), oN, QCN)
                P.dma("act", lambda e: e.dma_start(out=KCN[h * 128:(h + 1) * 128, t0:t0 + n], in_=oK[:, :n]), oK, KCN)
                qC.append((h, t0, n, oR, oKR))

            def stageC(h, t0, n, oR, oKR):
                if t0 >= 256:
                    oR = rope_tail(P, oR, 64, t0, n, obf, f32p, psC)
                    oKR = rope_tail(P, oKR, 64, t0, n, obf, f32p, psC)
                P.dma("sp", lambda e: e.dma_start(out=QCR[h * 64:(h + 1) * 64, t0:t0 + n], in_=oR[0:64, :n]), oR, QCR)
                P.dma("sp", lambda e: e.dma_start(out=KCR[h * 64:(h + 1) * 64, t0:t0 + n], in_=oKR[0:64, :n]), oKR, KCR)

            def drain_one():
                if qC:
                    stageC(*qC.pop(0))
                if qB:
                    stageB(*qB.pop(0))

            for (t0, n) in TB:
                cq = cqs.next()
                ckv = ckvs.next()
                kpe = kpes.next()
                sqk = sqks.next()
                P.dma("sp", lambda e, cq=cq, t0=t0, n=n: e.dma_start(out=cq[:, :, :n], in_=CQN[:, t0:t0 + n].rearrange("(c p) t -> p c t", p=128)), CQN, cq)
                P.dma("sp", lambda e, ckv=ckv, t0=t0, n=n: e.dma_start(out=ckv[:, :, :n], in_=CKVN[:, t0:t0 + n].rearrange("(c p) t -> p c t", p=128)), CKVN, ckv)
                P.dma("sp", lambda e, kpe=kpe, t0=t0, n=n: e.dma_start(out=kpe[:, :n], in_=KPE[:, t0:t0 + n]), KPE, kpe)
                P.op("act", lambda e, kpe=kpe, sqk=sqk, n=n: e.activation(out=sqk[:, :n], in_=kpe[:, :n], func=AF.Square), [kpe], [sqk])
                for h in range(6):
                    pN = psA.next()
                    for kc in range(6):
                        P.op("pe", lambda e, kc=kc, pN=pN, h=h, cq=cq, n=n: e.matmul(pN[:, :n], lhsT=wqb[:, kc, h * 192:h * 192 + 128], rhs=cq[:, kc, :n], start=(kc == 0), stop=(kc == 5)), [wqb, cq], [pN])
                    pR = psA.next()
                    for kc in range(6):
                        P.op("pe", lambda e, kc=kc, pR=pR, h=h, cq=cq, n=n: e.matmul(pR[0:64, :n], lhsT=wqb[:, kc, h * 192 + 128:h * 192 + 192], rhs=cq[:, kc, :n], start=(kc == 0), stop=(kc == 5)), [wqb, cq], [pR])
                    pK = psA.next()
                    for kc in range(4):
                        P.op("pe", lambda e, kc=kc, pK=pK, h=h, ckv=ckv, n=n: e.matmul(pK[:, :n], lhsT=wkvb[:, kc, h * 256:h * 256 + 128], rhs=ckv[:, kc, :n], start=(kc == 0), stop=(kc == 3)), [wkvb, ckv], [pK])
                    sqN = sqp.next(); sqR = sqp.next(); sqK = sqp.next()
                    rN = rawp.next(); rR = rawp.next(); rK = rawp.next()
                    P.op("act", lambda e, sqN=sqN, pN=pN, n=n: e.activation(out=sqN[:, :n], in_=pN[:, :n], func=AF.Square), [pN], [sqN])
                    P.op("act", lambda e, rN=rN, pN=pN, n=n: e.activation(out=rN[:, :n], in_=pN[:, :n], func=AF.Copy), [pN], [rN])
                    P.op("act", lambda e, sqR=sqR, pR=pR, n=n: e.activation(out=sqR[0:64, :n], in_=pR[0:64, :n], func=AF.Square), [pR], [sqR])
                    P.op("act", lambda e, rR=rR, pR=pR, n=n: e.activation(out=rR[0:64, :n], in_=pR[0:64, :n], func=AF.Copy), [pR], [rR])
                    P.op("act", lambda e, sqK=sqK, pK=pK, n=n: e.activation(out=sqK[:, :n], in_=pK[:, :n], func=AF.Square), [pK], [sqK])
                    P.op("act", lambda e, rK=rK, pK=pK, n=n: e.activation(out=rK[:, :n], in_=pK[:, :n], func=AF.Copy), [pK], [rK])
                    drain_one()
                    qB.append((h, t0, n, kpe, sqk, rN, rR, rK, sqN, sqR, sqK))
                for ti in range(n // 128):
                    ov = ovs.next()
                    for half in range(2):
                        pV = psA.next()
                        for kc in range(4):
                            P.op("pe", lambda e, kc=kc, pV=pV, ckv=ckv, ti=ti, half=half: e.matmul(
                                pV[:, 0:384].rearrange("p (h x) -> p h x", x=128),
                                lhsT=ckv[:, kc, ti * 128:(ti + 1) * 128],
                                rhs=wkvb[:, kc, :].rearrange("p (h x) -> p h x", x=256)[:, 3 * half:3 * half + 3, 128:256],
                                start=(kc == 0), stop=(kc == 3)), [ckv, wkvb], [pV])
                        if half == 0:
                            P.op("act", lambda e, ov=ov, pV=pV: e.activation(out=ov[:, 0:384], in_=pV[:, 0:384], func=AF.Copy), [pV], [ov])
                        else:
                            P.op("dve", lambda e, ov=ov, pV=pV: e.tensor_copy(out=ov[:, 384:768], in_=pV[:, 0:384]), [pV], [ov])
                    P.dma("pool", lambda e, ov=ov, r=t0 + ti * 128: e.dma_start(out=VC[r:r + 128, :], in_=ov[:]), ov, VC)
            while qB or qC:
                drain_one()

    def phase_C(l, last):
        with P.scope():
            psS = Rot(psb[0:3])
            psO = Rot(psb[3:5])
            psD = Rot(psb[5:7])
            ptp = Rot([P.sb("pt%d" % i, [128, 512], BF16) for i in range(6)])
            rdp = Rot([P.sb("rd%d" % i, [128, 512], F32) for i in range(2)])
            tmp = Rot([P.sb("tm%d" % i, [128, 512], F32) for i in range(2)])
            szp = Rot([P.sb("sz%d" % i, [128, 512], BF16) for i in range(2)])
            yp = Rot([P.sb("y%d" % i, [128, 512], BF16) for i in range(2)])
            bg_jobs = make_bg(l + 1, psb[7], extra=[(l, ("out",))], load_q="act") if (bg_next and not last) else []
            pipe = []
            nstep = [0]

            def push(fn):
                pipe.append(fn)
                if len(pipe) > 2:
                    pipe.pop(0)()
                nstep[0] += 1
                if bg_jobs and nstep[0] % 8 == 0:
                    bg_jobs.pop(0)()

            def flush():
                while pipe:
                    pipe.pop(0)()

            def finalize(pO, pD, M, n, row0, t0, sink_heads=None, three=False, fold=False):
                rd = rdp.next()
                if fold:
                    P.op("act", lambda e: e.activation(out=rd[0:64, :n], in_=pO[64:128, :n], func=AF.Ln), [pO], [rd])
                elif sink_heads is not None:
                    for g, hh in enumerate(sink_heads):
                        P.op("dve", lambda e, g=g, hh=hh: e.tensor_scalar(out=rd[0:M, g * 128:(g + 1) * 128], in0=pD[0:M, g * 128:(g + 1) * 128], scalar1=esink[0:M, hh:hh + 1], scalar2=None, op0=ALU.add), [pD, esink], [rd])
                    P.op("act", lambda e: e.activation(out=rd[0:M, :n], in_=rd[0:M, :n], func=AF.Ln), [rd], [rd])
                else:
                    P.op("act", lambda e: e.activation(out=rd[0:M, :n], in_=pD[0:M, :n], func=AF.Ln), [pD], [rd])
                P.op("act", lambda e: e.activation(out=rd[0:M, :n], in_=rd[0:M, :n], func=AF.Exp, scale=-1.0), [rd], [rd])
                szt = szp.next()
                if three:
                    P.dma("sp", lambda e: e.dma_start(out=szt[0:64, 0:384].rearrange("p (g t) -> p g t", g=3), in_=SZ[row0:row0 + 192, t0:t0 + 128].rearrange("(g d) t -> d g t", g=3)), SZ, szt)
                else:
                    P.dma("sp", lambda e: e.dma_start(out=szt[0:M, :n], in_=SZ[row0:row0 + M, t0:t0 + n]), SZ, szt)
                tm = tmp.next()
                P.op("dve", lambda e: e.tensor_tensor(out=tm[0:M, :n], in0=pO[0:M, :n], in1=rd[0:M, :n], op=ALU.mult), [pO, rd], [tm])
                y = yp.next()
                P.op("pool", lambda e: e.tensor_tensor(out=y[0:M, :n], in0=tm[0:M, :n], in1=szt[0:M, :n], op=ALU.mult), [tm, szt], [y])
                if three:
                    P.dma("pool", lambda e: e.dma_start(out=YT[row0:row0 + 192, t0:t0 + 128].rearrange("(g d) t -> d g t", g=3), in_=y[0:64, 0:384].rearrange("p (g t) -> p g t", g=3)), y, YT)
                else:
                    P.dma("pool", lambda e: e.dma_start(out=YT[row0:row0 + M, t0:t0 + n], in_=y[0:M, :n]), y, YT)

            def attend(keys, s_mm, v_of, M, n, scale, fin, fold=False, sink=None):
                pO = psO.next()
                pD = None if fold else psD.next()
                nk = len(keys)
                MM = 128 if fold else M

                def pv(i, pt):
                    kt = keys[i][0]
                    vb, vap = v_of(kt)
                    first = (i == 0)
                    if first and sink is not None:
                        P.op("pe", lambda e: e.matmul(pO[0:128, 0:384].rearrange("p (g t) -> p g t", g=3), lhsT=selr[0:33, :], rhs=esr[0:33, sink * 3:sink * 3 + 3, :], start=True, stop=False), [selr, esr], [pO])
                        first = False
                    P.op("pe", lambda e: e.matmul(pO[0:MM, :n], lhsT=vap, rhs=pt[:, :n], start=first, stop=(i == nk - 1)), [vb, pt], [pO])
                    if not fold:
                        P.op("pe", lambda e: e.matmul(pD[0:M, :n], lhsT=ones[:, 0:M], rhs=pt[:, :n], start=(i == 0), stop=(i == nk - 1)), [ones, pt], [pD])
                    if i == nk - 1:
                        fin(pO, pD)

                for i, (kt, mk) in enumerate(keys):
                    pS = psS.next()
                    s_mm(pS, kt)
                    pt = ptp.next()
                    P.op("act", lambda e, pS=pS, pt=pt: e.activation(out=pt[:, :n], in_=pS[:, :n], func=AF.Exp, scale=scale), [pS], [pt])
                    if mk is not None:
                        P.op("dve", lambda e, pt=pt, mk=mk: e.tensor_tensor(out=pt[:, :n], in0=pt[:, :n], in1=swamask[:, mk, :], op=ALU.mult), [pt, swamask], [pt])
                    push(lambda i=i, pt=pt: pv(i, pt))

            with P.scope():
                kNs = Rot([P.sb("kN%d" % i, [128, T], BF16) for i in range(2)])
                kRs = Rot([P.sb("kR%d" % i, [128, T], BF16) for i in range(2)])
                qNs = Rot([P.sb("qN%d" % i, [128, T], BF16) for i in range(2)])
                qRs = Rot([P.sb("qR%d" % i, [128, T], BF16) for i in range(2)])
                for bb in kRs.bufs + qRs.bufs:
                    P.op("pool", lambda e, bb=bb: e.memset(bb[64:128, :], 0.0), [], [bb])
                vs = Rot([P.sb("vc%d" % i, [128, 18, 128], BF16) for i in range(2)])
                for h in range(6):
                    kN = kNs.next(); kR = kRs.next(); qN = qNs.next(); qR = qRs.next(); v = vs.next()
                    P.dma("sp", lambda e, kN=kN, h=h: e.dma_start(out=kN[:], in_=KCN[h * 128:(h + 1) * 128, :]), KCN, kN)
                    P.dma("sp", lambda e, kR=kR, h=h: e.dma_start(out=kR[0:64, :], in_=KCR[h * 64:(h + 1) * 64, :]), KCR, kR)
                    P.dma("sp", lambda e, qN=qN, h=h: e.dma_start(out=qN[:], in_=QCN[h * 128:(h + 1) * 128, :]), QCN, qN)
                    P.dma("sp", lambda e, qR=qR, h=h: e.dma_start(out=qR[0:64, :], in_=QCR[h * 64:(h + 1) * 64, :]), QCR, qR)
                    P.dma("sp", lambda e, v=v, h=h: e.dma_start(out=v[:], in_=VC[:, h * 128:(h + 1) * 128].rearrange("(t p) d -> p t d", p=128)), VC, v)
                    blocks = [(t0, n, list(range(18))) for (t0, n) in TB[1:]]
                    if not last:
                        blocks.append((0, 256, [0, 1]))
                    for (t0, n, kts) in blocks:
                        def s_mm(pS, kt, t0=t0, n=n, kN=kN, kR=kR, qN=qN, qR=qR):
                            P.op("pe", lambda e: e.matmul(pS[:, :n], lhsT=kN[:, kt * 128:(kt + 1) * 128], rhs=qN[:, t0:t0 + n], start=True, stop=False), [kN, qN], [pS])
                            P.op("pe", lambda e: e.matmul(pS[:, :n], lhsT=kR[:, kt * 128:(kt + 1) * 128], rhs=qR[:, t0:t0 + n], start=False, stop=True), [kR, qR], [pS])
                        attend([(kt, None) for kt in kts], s_mm, lambda kt, v=v: (v, v[:, kt, :]), 128, n, 192 ** -0.5,
                               lambda pO, pD, n=n, h=h, t0=t0: finalize(pO, pD, 128, n, 1280 + h * 128, t0))
                flush()

            with P.scope():
                kTs = Rot([P.sb("kb%d" % i, [128, T], BF16) for i in range(2)])
                qs = Rot([P.sb("qb%d" % i, [128, 3, T], BF16) for i in range(2)])
                for bb in kTs.bufs:
                    P.op("pool", lambda e, bb=bb: e.memset(bb[64:128, :], 0.0), [], [bb])
                for bb in qs.bufs:
                    P.op("pool", lambda e, bb=bb: e.memset(bb[64:128, :, :], 0.0), [], [bb])
                vs = Rot([P.sb("vb%d" % i, [128, 18, 128], BF16) for i in range(2)])
                for vv in vs.bufs:
                    P.op("pool", lambda e, vv=vv: e.memset(vv[:, :, 64:128], 1.0), [], [vv])
                for kvh in range(4):
                    kT = kTs.next(); q = qs.next(); v = vs.next()
                    P.dma("sp", lambda e, kT=kT, kvh=kvh: e.dma_start(out=kT[0:64, :], in_=KBT[kvh * 64:(kvh + 1) * 64, :]), KBT, kT)
                    P.dma("sp", lambda e, q=q, kvh=kvh: e.dma_start(out=q[0:64, :, :], in_=QBT[kvh * 192:(kvh + 1) * 192, :].rearrange("(g d) t -> d g t", g=3)), QBT, q)
                    P.dma("sp", lambda e, v=v, kvh=kvh: e.dma_start(out=v[:, :, 0:64], in_=VB[:, kvh * 64:(kvh + 1) * 64].rearrange("(t p) d -> p t d", p=128)), VB, v)
                    qtiles = []
                    for qt in range(16):
                        keys = [(0, None), (1, None)]
                        if qt > 0:
                            keys.append((2 + qt - 1, 0))
                        keys.append((2 + qt, None))
                        if qt < 15:
                            keys.append((2 + qt + 1, 1))
                        qtiles.append((256 + qt * 128, keys))
                    if not last:
                        qtiles.append((0, [(0, None), (1, None)]))
                        qtiles.append((128, [(0, None), (1, None)]))
                    for (tok0, keys) in qtiles:
                        def s_mm(pS, kt, tok0=tok0, kT=kT, q=q):
                            P.op("pe", lambda e: e.matmul(pS[:, 0:384].rearrange("p (g t) -> p g t", g=3), lhsT=kT[:, kt * 128:(kt + 1) * 128], rhs=q[:, :, tok0:tok0 + 128], start=True, stop=True), [kT, q], [pS])
                        attend(keys, s_mm, lambda kt, v=v: (v, v[:, kt, :]), 64, 384, 0.125,
                               lambda pO, pD, kvh=kvh, tok0=tok0: finalize(pO, pD, 64, 384, 512 + kvh * 192, tok0, three=True, fold=True), fold=True, sink=kvh)
                flush()

            with P.scope():
                kTs = Rot([P.sb("ka%d" % i, [64, T], BF16) for i in range(2)])
                qs = Rot([P.sb("qa%d" % i, [64, T], BF16) for i in range(2)])
                v0s = Rot([P.sb("va0_%d" % i, [128, 18, 128], BF16) for i in range(2)])
                v1s = Rot([P.sb("va1_%d" % i, [128, 17, 128], BF16) for i in range(2)])
                for vv in v0s.bufs + v1s.bufs:
                    P.op("pool", lambda e, vv=vv: e.memset(vv[:, :, 64:128], 1.0), [], [vv])
                gfs = Rot([P.sb("gf%d" % i, [128, 896], F32) for i in range(2)])
                Gs = Rot([P.sb("G%d" % i, [128, 16, 64], BF16) for i in range(2)])
                for h in range(8):
                    kT = kTs.next(); q = qs.next(); v0 = v0s.next(); v1 = v1s.next(); gf = gfs.next(); G = Gs.next()
                    P.dma("sp", lambda e, kT=kT, h=h: e.dma_start(out=kT[:], in_=KAT[h * 64:(h + 1) * 64, :]), KAT, kT)
                    P.dma("sp", lambda e, q=q, h=h: e.dma_start(out=q[:], in_=QAT[h * 64:(h + 1) * 64, :]), QAT, q)
                    P.dma("sp", lambda e, v0=v0, h=h: e.dma_start(out=v0[:, :, 0:64], in_=VA[:, h * 64:(h + 1) * 64].rearrange("(t p) d -> p t d", p=128)), VA, v0)
                    P.dma("sp", lambda e, v1=v1, h=h: e.dma_start(out=v1[:, :, 0:64], in_=VA[64:64 + 17 * 128, h * 64:(h + 1) * 64].rearrange("(t p) d -> p t d", p=128)), VA, v1)
                    P.dma("sp", lambda e, gf=gf, h=h: e.dma_start(out=gf[:], in_=gb_d[l, :, h * 896:(h + 1) * 896]), gb_d, gf)
                    P.op("act", lambda e, gf=gf: e.activation(out=gf[:], in_=gf[:], func=AF.Exp), [gf], [gf])
                    for m in range(14):
                        P.op("dve", lambda e, gf=gf, G=G, m=m: e.tensor_tensor(out=G[:, m, :], in0=gf[:, m * 64:(m + 1) * 64], in1=namask[:], op=ALU.mult), [gf, namask], [G])
                    def pv_stage(pt, ri, rs_, pO, pD, fin, h=h, v0=v0, v1=v1):
                        for i in range(6):
                            if i < 2:
                                vb, vap = v0, v0[:, i, :]
                            elif rs_ % 2 == 0:
                                vb, vap = v0, v0[:, 2 + rs_ // 2 + (i - 2), :]
                            else:
                                vb, vap = v1, v1[:, (3 + rs_) // 2 + (i - 2), :]
                            P.op("pe", lambda e, i=i, vap=vap: e.matmul(pO[0:128, ri * 64:(ri + 1) * 64], lhsT=vap, rhs=pt[:, i * 64:(i + 1) * 64], start=(i == 0), stop=(i == 5)), [vb, pt], [pO])
                        if fin is not None:
                            finalize(pO, None, 64, 512, h * 64, fin, fold=True)

                    for rg in range(4):
                        pO = psO.next()
                        pD = None
                        for ri in range(8):
                            r = rg * 8 + ri
                            rs_ = min(max(r - 4, 0), 24)
                            m0 = rs_ - r + 7
                            tq = 256 + r * 64
                            ks = 256 + rs_ * 64
                            pS = psS.next()
                            for i in range(6):
                                k0 = i * 128 if i < 2 else ks + (i - 2) * 128
                                P.op("pe", lambda e, i=i, k0=k0, pS=pS, tq=tq, kT=kT, q=q: e.matmul(pS[:, i * 64:(i + 1) * 64], lhsT=kT[0:64, k0:k0 + 128], rhs=q[0:64, tq:tq + 64], start=True, stop=True), [kT, q], [pS])
                            pt = ptp.next()
                            P.op("act", lambda e, pS=pS, pt=pt: e.activation(out=pt[:, 0:384], in_=pS[:, 0:384], func=AF.Exp, scale=0.125), [pS], [pt])
                            P.op("dve", lambda e, pt=pt, G=G, m0=m0: e.tensor_tensor(out=pt[:, 128:384].rearrange("p (j c) -> p j c", c=64), in0=pt[:, 128:384].rearrange("p (j c) -> p j c", c=64), in1=G[:, m0:m0 + 8, :].rearrange("p (j two) c -> p j two c", two=2)[:, :, 0, :], op=ALU.mult), [pt, G], [pt])
                            push(lambda a=(pt, ri, rs_, pO, pD, (256 + rg * 512) if ri == 7 else None), f=pv_stage: f(*a))
                    if not last:
                        def s_mm(pS, kt, kT=kT, q=q):
                            P.op("pe", lambda e: e.matmul(pS[:, 0:256], lhsT=kT[0:64, kt * 128:(kt + 1) * 128], rhs=q[0:64, 0:256], start=True, stop=True), [kT, q], [pS])
                        attend([(0, None), (1, None)], s_mm, lambda kt, v0=v0: (v0, v0[:, kt, :]), 64, 256, 0.125,
                               lambda pO, pD, h=h: finalize(pO, pD, 64, 256, h * 64, 0, fold=True), fold=True)
                flush()
            while bg_jobs:
                bg_jobs.pop(0)()

    def phase_D(l, last):
        with P.scope():
            wout = P.sb("wout", [128, 16, D], BF16)
            gate_row = [P.sb("gate_l", [128, D], F32), P.sb("gate_c", [128, D], F32)]
            for s in range(2):
                P.dma("sp", lambda e, s=s: e.dma_start(out=gate_row[s][:], in_=MODs[l][s, 4096:6144].partition_broadcast(128)), MODs[l], gate_row[s])
            for cb in range(4):
                P.dma("sp", lambda e, cb=cb: e.dma_start(out=wout[:, :, cb * 512:(cb + 1) * 512], in_=WOUTs[l][:, cb * 512:(cb + 1) * 512].rearrange("(c p) n -> p c n", p=128)), WOUTs[l], wout)
            yts = Rot([P.sb("yT%d" % i, [128, 16, 512], BF16) for i in range(2)])
            xts = Rot([P.sb("xt%d" % i, [128, D], F32) for i in range(2)])
            ots = Rot([P.sb("ot%d" % i, [128, D], F32) for i in range(2)])
            tmp = Rot([P.sb("tm%d" % i, [128, 512], F32) for i in range(2)])
            psA = Rot(psb[0:4])
            blocks = list(TB[1:])
            if not last:
                blocks.append(TB[0])
            for (t0, n) in blocks:
                yt = yts.next()
                P.dma("sp", lambda e, yt=yt, t0=t0, n=n: e.dma_start(out=yt[:, :, :n], in_=YT[:, t0:t0 + n].rearrange("(c p) t -> p c t", p=128)), YT, yt)
                for ti in range(n // 128):
                    isctx = t0 < 256
                    r0 = (t0 + ti * 128) if isctx else (t0 - 256 + ti * 128)
                    if l == 0:
                        src = ctx_d if isctx else x_d
                    else:
                        src = XC1 if isctx else X1
                    if last:
                        dst = out_d
                    else:
                        dst = XC1 if isctx else X1
                    g = gate_row[1 if isctx else 0]
                    xt = xts.next()
                    ot = ots.next()
                    P.dma("sp", lambda e, xt=xt, src=src, r0=r0: e.dma_start(out=xt[:], in_=src[r0:r0 + 128, :]), src, xt)
                    for cb in range(4):
                        po = psA.next()
                        for kc in range(16):
                            P.op("pe", lambda e, kc=kc, po=po, yt=yt, ti=ti, cb=cb: e.matmul(po[:, :], lhsT=yt[:, kc, ti * 128:(ti + 1) * 128], rhs=wout[:, kc, cb * 512:(cb + 1) * 512], start=(kc == 0), stop=(kc == 15)), [yt, wout], [po])
                        tm = tmp.next()
                        P.op("dve", lambda e, tm=tm, po=po, g=g, cb=cb: e.tensor_tensor(out=tm[:], in0=po[:], in1=g[:, cb * 512:(cb + 1) * 512], op=ALU.mult), [po, g], [tm])
                        P.op("pool", lambda e, tm=tm, ot=ot, xt=xt, cb=cb: e.tensor_tensor(out=ot[:, cb * 512:(cb + 1) * 512], in0=tm[:], in1=xt[:, cb * 512:(cb + 1) * 512], op=ALU.add), [tm, xt], [ot])
                    P.dma("pool", lambda e, ot=ot, dst=dst, r0=r0: e.dma_start(out=dst[r0:r0 + 128, :], in_=ot[:]), ot, dst)

    bg_next = (nlayers == 2)
    for l in range(nlayers):
        last = (l == nlayers - 1) and nlayers == 2
        if l == 0 or not bg_next:
            phase_cast_ada(l)
        else:
            ada_tail(l)
        if stop == "ada":
            break
        phase_B(l)
        if stop in ("B1", "B"):
            break
        phase_B3(l)
        if stop == "B3":
            break
        phase_C(l, last)
        if stop == "C":
            break
        phase_D(l, last)
    P.barrier()
    P.emit()
    return nc


def _consts():
    ident = np.eye(128, dtype=np.float32)
    bd = np.zeros((128, 128), np.float32)
    bd[:64, :64] = 1
    bd[64:, 64:] = 1
    on = np.ones((128, 128), np.float32)
    r64 = np.zeros((64, 64), np.float32)
    for i in range(16):
        r64[i + 16, i] = -1
        r64[i, i + 16] = 1
        r64[i + 48, i + 32] = -1
        r64[i + 32, i + 48] = 1
    rot = np.zeros((128, 128), np.float32)
    rot[:64, :64] = r64
    rot[64:, 64:] = r64
    cmat = np.ascontiguousarray(np.stack([ident, bd, on, rot], axis=1))
    t = np.arange(2048)
    row = (t // 64).astype(np.float32)
    col = (t % 64).astype(np.float32)
    inv = (np.float32(10000.0) ** (-np.arange(16, dtype=np.float32) / np.float32(16))).astype(np.float32)
    ar = row[:, None] * inv
    ac = col[:, None] * inv
    ang = np.concatenate([ar, ar, ac, ac], -1)
    cos = np.cos(ang).astype(np.float32).T
    sin = np.sin(ang).astype(np.float32).T
    cossin = np.zeros((128, 2, 2048), np.float32)
    cossin[:64, 0] = cos
    cossin[64:, 0] = cos
    cossin[:64, 1] = sin
    cossin[64:, 1] = sin
    kc = np.arange(128) % 64
    c = np.arange(64)
    qs = np.clip(c - 8, 0, 48)
    namask = ((kc[:, None] >= qs[None, :]) & (kc[:, None] < qs[None, :] + 16)).astype(np.float32)
    i = np.arange(128)
    lo = (i[None, :] <= i[:, None]).astype(np.float32)
    hi = (i[:, None] <= i[None, :]).astype(np.float32)
    swamask = np.stack([np.tile(lo, (1, 3)), np.tile(hi, (1, 3))], axis=1).astype(np.float32)
    return cmat, cossin, namask, np.ascontiguousarray(swamask)


def _gather_rpb(rpb):
    p = np.arange(128)
    kc = p % 64
    half = p // 64
    c = np.arange(64)
    dc = np.clip(kc[:, None] - c[None, :], -15, 15) + 15
    m = np.arange(14)
    dr = m[None, :, None] + half[:, None, None]
    g = rpb[:, :, dr, dc[:, None, :]]
    g = np.transpose(g, (0, 2, 1, 3, 4)).reshape(2, 128, 8 * 14 * 64)
    return np.ascontiguousarray(g.astype(np.float32))


def _cols(norm_w, qn_a, kn_a, qn_b, kn_b, qa_norm, kva_norm, qn_c, kn_c):
    out = np.ones((2, 128, 34), np.float32)
    for l in range(2):
        out[l, :, 0] = np.tile(qn_a[l], 2)
        out[l, :, 1] = np.tile(kn_a[l], 2)
        out[l, :, 2] = np.tile(qn_b[l], 2)
        out[l, :, 3] = np.tile(kn_b[l], 2)
        out[l, :, 4:10] = qa_norm[l].reshape(6, 128).T
        out[l, :, 10:14] = kva_norm[l].reshape(4, 128).T
        out[l, :, 14] = qn_c[l][:128]
        out[l, :64, 15] = qn_c[l][128:]
        out[l, :, 16] = kn_c[l][:128]
        out[l, :64, 17] = kn_c[l][128:]
        out[l, :, 18:34] = norm_w[l].reshape(16, 128).T
    return out


def make_in_maps(x, c, ctx, c_ctx, norm_w, w_ada, b_ada, w_in, qn_a, kn_a, rpb_a, qn_b, kn_b, sink_b,
                 qa_norm, kva_norm, w_qb, w_kvb, qn_c, kn_c, w_out):
    f = lambda a: np.ascontiguousarray(np.asarray(a, dtype=np.float32))
    x, c, ctx, c_ctx = f(x), f(c), f(ctx), f(c_ctx)
    cmat, cossin, namask, swamask = _consts()
    cols = _cols(f(norm_w), f(qn_a), f(kn_a), f(qn_b), f(kn_b), f(qa_norm), f(kva_norm), f(qn_c), f(kn_c))
    gb = _gather_rpb(f(rpb_a))
    sink = np.ascontiguousarray(np.broadcast_to(f(sink_b)[:, None, :], (2, 64, 12)))
    shared = dict(w_ada=f(w_ada), b_ada=f(b_ada), w_in=f(w_in), w_qb=f(w_qb), w_kvb=f(w_kvb), w_out=f(w_out),
                  cols=cols, sink=sink, gb=gb, cmat=cmat, cossin=cossin, namask=namask, swamask=swamask)
    maps = []
    for b in range(8):
        cc = np.zeros((128, 16, 2), np.float32)
        cc[:, :, 0] = c[b].reshape(16, 128).T
        cc[:, :, 1] = c_ctx.reshape(16, 128).T
        m = dict(shared)
        m["x"] = x[b]
        m["ctx"] = ctx[b]
        m["cc"] = np.ascontiguousarray(cc.reshape(128, 32))
        maps.append(m)
    return maps


def kernel(**inputs):
    maps = make_in_maps(**inputs)
    nc = build(2)
    res = run_bass_kernel_spmd(nc, maps, core_ids=list(range(8)))
    return np.stack([np.asarray(r["out"], dtype=np.float32) for r in res.results], axis=0)
```

```python
import contextlib
import numpy as np
import concourse.bass as bass
import concourse.mybir as mybir
from concourse.bass_utils import run_bass_kernel_spmd

F32 = mybir.dt.float32
BF16 = mybir.dt.bfloat16
AF = mybir.ActivationFunctionType
ALU = mybir.AluOpType

CH = 30000
DCH = 1800
D = 2048
T = 2304
EPS = 1e-6
N_IN = 6208
TB = [(0, 256), (256, 512), (768, 512), (1280, 512), (1792, 512)]


class Buf:
    def __init__(self, t, name, is_dram=False, is_psum=False):
        self.t = t
        self.name = name
        self.is_dram = is_dram
        self.is_psum = is_psum
        self.writers = {}
        self.readers = {}

    def __getitem__(self, k):
        return self.t[k]


class Prog:
    ENG = ["pe", "act", "dve", "pool", "sp"]

    def __init__(self, nc):
        self.nc = nc
        self.ops = {e: [] for e in self.ENG}
        self.cnt = {e: 0 for e in self.ENG}
        self.seen = {e: {} for e in self.ENG}
        self.sems = {}
        self.ndma = {}
        self.latest = {}
        self.stack = None
        self.uid = 0

    def sb(self, name, shape, dtype):
        self.uid += 1
        t = self.stack.enter_context(self.nc.sbuf_tensor("%s_u%d" % (name, self.uid), list(shape), dtype))
        return Buf(t, name)

    def gsb(self, name, shape, dtype):
        return Buf(self.nc.alloc_sbuf_tensor(name, list(shape), dtype), name)

    def ps(self, name, shape, dtype=F32):
        return Buf(self.nc.alloc_psum_tensor(name, list(shape), dtype), name, is_psum=True)

    def dram(self, name, shape, dtype, kind="Internal"):
        return Buf(self.nc.dram_tensor(name, list(shape), dtype, kind=kind), name, is_dram=True)

    def sem(self, key):
        if key not in self.sems:
            self.sems[key] = self.nc.alloc_semaphore("s%d" % len(self.sems))
        return self.sems[key]

    def _collect(self, E, reads, writes, own_keys):
        deps = {}

        def add(k, v):
            if deps.get(k, 0) < v:
                deps[k] = v
        for r in reads:
            for k, v in r.writers.items():
                if E == "pe" and k[:2] == ("eng", "pe"):
                    continue
                add(k, v)
            if r.is_psum:
                for k, v in r.readers.items():
                    if k[:2] not in own_keys:
                        add(k, v)
        for w in writes:
            for k, v in w.writers.items():
                if k[:2] not in own_keys:
                    add(k, v)
            for k, v in w.readers.items():
                if k[:2] not in own_keys:
                    add(k, v)
        waits = []
        for k, v in deps.items():
            if self.seen[E].get(k, 0) >= v:
                continue
            self.seen[E][k] = v
            waits.append((self.sem(k), v))
        return waits

    def _commit(self, ev, reads, writes):
        k, v = ev
        self.latest[k] = v
        for r in reads:
            if r.readers.get(k, 0) < v:
                r.readers[k] = v
        for w in writes:
            if w.writers.get(k, 0) < v:
                w.writers[k] = v
            w.readers = {}

    def op(self, E, fn, reads=(), writes=()):
        waits = self._collect(E, reads, writes, (("eng", E),))
        idx = self.cnt[E]
        self.cnt[E] += 1
        key = ("eng", E, idx // CH)
        val = idx % CH + 1
        self.ops[E].append((waits, fn, self.sem(key), 1))
        self._commit((key, val), reads, writes)

    def dma(self, Q, fn, src, dst):
        side = dst if src.is_dram else src
        n = self.ndma.get(side.name, 0)
        key = ("dma", side.name, n // DCH)
        waits = self._collect(Q, [src], [dst], (("dma", side.name),))
        val = 16 * (n % DCH + 1)
        self.ndma[side.name] = n + 1
        self.ops[Q].append((waits, fn, self.sem(key), 16))
        self._commit((key, val), [src], [dst])

    def barrier(self):
        for E in self.ENG:
            waits = []
            for k, v in self.latest.items():
                if self.seen[E].get(k, 0) >= v:
                    continue
                self.seen[E][k] = v
                waits.append((self.sem(k), v))
            if waits:
                self.ops[E].append((waits, None, None, 0))

    @contextlib.contextmanager
    def scope(self):
        old = self.stack
        with contextlib.ExitStack() as st:
            self.stack = st
            yield
            self.barrier()
        self.stack = old

    def emit(self):
        nc = self.nc
        ops = self.ops

        def run(eng, lst):
            for waits, fn, sem, inc in lst:
                for s, v in waits:
                    eng.wait_ge(s, v)
                if fn is not None:
                    fn(eng).then_inc(sem, inc)

        with nc.Block() as block:
            @block.sync
            def _(e):
                run(e, ops["sp"])

            @block.tensor
            def _(e):
                run(e, ops["pe"])

            @block.scalar
            def _(e):
                run(e, ops["act"])

            @block.vector
            def _(e):
                run(e, ops["dve"])

            @block.gpsimd
            def _(e):
                run(e, ops["pool"])


class Rot:
    def __init__(self, bufs):
        self.bufs = bufs
        self.i = 0

    def next(self):
        b = self.bufs[self.i % len(self.bufs)]
        self.i += 1
        return b


def build(nlayers=2, dbg=(), stop=None):
    nc = bass.Bass("TRN2", target_bir_lowering=False)
    P = Prog(nc)
    EI = "ExternalInput"

    def scr(name, shape, dtype):
        return P.dram(name, shape, dtype, kind=("ExternalOutput" if name in dbg else "Internal"))

    x_d = P.dram("x", [2048, D], F32, EI)
    ctx_d = P.dram("ctx", [256, D], F32, EI)
    cc_d = P.dram("cc", [128, 32], F32, EI)
    w_ada_d = P.dram("w_ada", [2, D, 6144], F32, EI)
    b_ada_d = P.dram("b_ada", [2, 6144], F32, EI)
    w_in_d = P.dram("w_in", [2, D, N_IN], F32, EI)
    w_qb_d = P.dram("w_qb", [2, 768, 1152], F32, EI)
    w_kvb_d = P.dram("w_kvb", [2, 512, 1536], F32, EI)
    w_out_d = P.dram("w_out", [2, D, D], F32, EI)
    cols_d = P.dram("cols", [2, 128, 34], F32, EI)
    sink_d = P.dram("sink", [2, 64, 12], F32, EI)
    gb_d = P.dram("gb", [2, 128, 8 * 896], F32, EI)
    cmat_d = P.dram("cmat", [128, 4, 128], F32, EI)
    cossin_d = P.dram("cossin", [128, 2, 2048], F32, EI)
    namask_d = P.dram("namask", [128, 64], F32, EI)
    swamask_d = P.dram("swamask", [128, 2, 384], F32, EI)
    out_d = P.dram("out", [2048, D], F32, "ExternalOutput")

    WINs = [scr("WIN%d" % i, [D, N_IN], BF16) for i in range(2)]
    WQBs = [scr("WQB%d" % i, [768, 1152], BF16) for i in range(2)]
    WKVBs = [scr("WKVB%d" % i, [512, 1536], BF16) for i in range(2)]
    WOUTs = [scr("WOUT%d" % i, [D, D], BF16) for i in range(2)]
    MODs = [scr("MOD" if i == 0 else "MOD1", [2, 6144], F32) for i in range(2)]
    QAT = scr("QAT", [512, T], BF16)
    KAT = scr("KAT", [512, T], BF16)
    VA = scr("VA", [T, 512], BF16)
    QBT = scr("QBT", [768, T], BF16)
    KBT = scr("KBT", [256, T], BF16)
    VB = scr("VB", [T, 256], BF16)
    CQN = scr("CQN", [768, T], BF16)
    CKVN = scr("CKVN", [512, T], BF16)
    KPE = scr("KPE", [64, T], F32)
    SZ = scr("SZ", [D, T], BF16)
    QCN = scr("QCN", [768, T], BF16)
    QCR = scr("QCR", [384, T], BF16)
    KCN = scr("KCN", [768, T], BF16)
    KCR = scr("KCR", [384, T], BF16)
    VC = scr("VC", [T, 768], BF16)
    YT = scr("YT", [D, T], BF16)
    X1 = scr("X1", [2048, D], F32)
    XC1 = scr("XC1", [256, D], F32)

    psb = [P.ps("ps%d" % i, [128, 512], F32) for i in range(8)]

    ident = P.gsb("ident", [128, 128], F32)
    bd64 = P.gsb("bd64", [128, 128], BF16)
    ones = P.gsb("ones", [128, 128], BF16)
    rotm = P.gsb("rotm", [128, 128], BF16)
    cos_t = P.gsb("cos_t", [128, 2048], F32)
    sin_t = P.gsb("sin_t", [128, 2048], F32)
    namask = P.gsb("namask_s", [128, 64], F32)
    swamask = P.gsb("swamask_s", [128, 2, 384], BF16)
    scT = P.gsb("scT", [128, 32], F32)
    cols = P.gsb("cols_s", [128, 34], F32)
    modcol = P.gsb("modcol", [128, 2, 32], F32)
    gcol = P.gsb("gcol", [128, 2, 16], F32)
    esink = P.gsb("esink", [64, 12], F32)
    ones2 = P.gsb("ones2", [1, 2], F32)
    epsc = P.gsb("epsc", [128, 1], F32)
    esr = P.gsb("esr", [33, 12, 128], BF16)
    selr = P.gsb("selr", [33, 128], BF16)
    onesf = P.gsb("onesf", [64, 128], F32)
    eshi = P.gsb("eshi", [64, 12], BF16)
    eslo = P.gsb("eslo", [64, 12], F32)

    with P.scope():
        cm = P.sb("cm", [128, 4, 128], F32)
        swf = P.sb("swf", [128, 2, 384], F32)
        cct = P.sb("cct", [128, 32], F32)
        P.dma("sp", lambda e: e.dma_start(out=cm[:], in_=cmat_d[:]), cmat_d, cm)
        P.dma("sp", lambda e: e.dma_start(out=swf[:], in_=swamask_d[:]), swamask_d, swf)
        P.dma("sp", lambda e: e.dma_start(out=cct[:], in_=cc_d[:]), cc_d, cct)
        P.dma("sp", lambda e: e.dma_start(out=cos_t[:], in_=cossin_d[:, 0, :]), cossin_d, cos_t)
        P.dma("sp", lambda e: e.dma_start(out=sin_t[:], in_=cossin_d[:, 1, :]), cossin_d, sin_t)
        P.dma("sp", lambda e: e.dma_start(out=namask[:], in_=namask_d[:]), namask_d, namask)
        P.op("dve", lambda e: e.tensor_copy(out=ident[:], in_=cm[:, 0, :]), [cm], [ident])
        P.op("dve", lambda e: e.tensor_copy(out=bd64[:], in_=cm[:, 1, :]), [cm], [bd64])
        P.op("dve", lambda e: e.tensor_copy(out=ones[:], in_=cm[:, 2, :]), [cm], [ones])
        P.op("dve", lambda e: e.tensor_copy(out=rotm[:], in_=cm[:, 3, :]), [cm], [rotm])
        P.op("dve", lambda e: e.tensor_copy(out=swamask[:], in_=swf[:]), [swf], [swamask])
        P.op("dve", lambda e: e.memset(ones2[:], 1.0), [], [ones2])
        P.op("dve", lambda e: e.memset(epsc[:], EPS), [], [epsc])
        P.op("dve", lambda e: e.memset(onesf[:], 1.0), [], [onesf])
        P.op("dve", lambda e: e.memset(selr[:], 0.0), [], [selr])
        P.op("dve", lambda e: e.memset(selr[0:1, 64:128], 1.0), [], [selr])
        P.op("dve", lambda e: e.memset(selr[32:33, 64:128], 1.0), [], [selr])
        P.op("dve", lambda e: e.memset(esr[:], 0.0), [], [esr])
        P.op("act", lambda e: e.activation(out=scT[:], in_=cct[:], func=AF.Silu), [cct], [scT])

    def make_bg(l, ps_bank, CW=3104, store_q="pool", which=("qb", "kvb", "out"), ada=True, extra=(), load_q="sp", casts_first=False):
        stg = [P.sb("stg%d" % i, [128, CW], F32) for i in range(2)]
        bft = [P.sb("bft%d" % i, [128, CW], BF16) for i in range(2)]
        wts = [P.sb("wada%d" % i, [128, 16, 256], F32) for i in range(2)]
        badas = [P.sb("bada%d" % i, [1, 256], F32) for i in range(2)]
        mods = [P.sb("modsb%d" % i, [2, 256], F32) for i in range(2)]
        cast = []
        for (ll, wh) in list(extra) + [(l, which)]:
            for (nm, src, dst, R, C) in [("in", w_in_d, WINs[ll], D, N_IN), ("qb", w_qb_d, WQBs[ll], 768, 1152),
                                         ("kvb", w_kvb_d, WKVBs[ll], 512, 1536), ("out", w_out_d, WOUTs[ll], D, D)]:
                if nm not in wh:
                    continue
                for rc in range(R // 128):
                    for c0 in range(0, C, CW):
                        cast.append((ll, src, dst, rc, c0, min(C, c0 + CW) - c0))

        def cast_stages(k, ll, src, dst, rc, c0, cw):
            st = stg[k % 2]
            bt = bft[k % 2]
            h = cw // 2

            def s0():
                P.dma(load_q if load_q != "alt" else "sp", lambda e: e.dma_start(out=st[:, :cw], in_=src[ll, rc * 128:(rc + 1) * 128, c0:c0 + cw]), src, st)

            def s1():
                P.op("pool", lambda e: e.tensor_copy(out=bt[:, :h], in_=st[:, :h]), [st], [bt])
                P.op("dve", lambda e: e.tensor_copy(out=bt[:, h:cw], in_=st[:, h:cw]), [st], [bt])

            def s2():
                P.dma(store_q, lambda e: e.dma_start(out=dst[rc * 128:(rc + 1) * 128, c0:c0 + cw], in_=bt[:, :cw]), bt, dst)
            return (s0, s1, s2)

        def ada_stages(k):
            w = wts[k % 2]
            bada = badas[k % 2]
            md = mods[k % 2]
            c0 = k * 256

            def s0():
                lq = load_q if load_q != "alt" else ("sp" if k % 2 == 0 else "act")
                P.dma(lq, lambda e: e.dma_start(out=w[:], in_=w_ada_d[l, :, c0:c0 + 256].rearrange("(c p) n -> p c n", p=128)), w_ada_d, w)
                P.dma(lq, lambda e: e.dma_start(out=bada[:], in_=b_ada_d[l:l + 1, c0:c0 + 256]), b_ada_d, bada)

            def s1():
                pm = ps_bank
                for kc in range(16):
                    P.op("pe", lambda e, kc=kc: e.matmul(pm[0:2, 0:256], lhsT=scT[:, 2 * kc:2 * kc + 2], rhs=w[:, kc, :], start=(kc == 0), stop=False), [scT, w], [pm])
                P.op("pe", lambda e: e.matmul(pm[0:2, 0:256], lhsT=ones2[0:1, 0:2], rhs=bada[0:1, :], start=False, stop=True), [ones2, bada], [pm])
                P.op("act", lambda e: e.activation(out=md[0:2, :], in_=pm[0:2, 0:256], func=AF.Copy), [pm], [md])

            def s2():
                P.dma("pool", lambda e: e.dma_start(out=MODs[l][:, c0:c0 + 256], in_=md[0:2, :]), md, MODs[l])
            return (s0, s1, s2)

        def lagged(stages):
            ticks = []
            n = len(stages)
            for t in range(n + 2):
                def tick(t=t):
                    if t < n:
                        stages[t][0]()
                    if 0 <= t - 1 < n:
                        stages[t - 1][1]()
                    if 0 <= t - 2 < n:
                        stages[t - 2][2]()
                ticks.append(tick)
            return ticks
        ct = lagged([cast_stages(k, *c) for k, c in enumerate(cast)])
        at = lagged([ada_stages(k) for k in range(24)]) if ada else []
        jobs = []
        if casts_first:
            jobs = ct + at
        else:
            while ct or at:
                for _ in range(3):
                    if ct:
                        jobs.append(ct.pop(0))
                if at:
                    jobs.append(at.pop(0))
        return jobs

    def ada_tail(l):
        with P.scope():
            nwt = P.sb("nwt", [128, 2, 16], F32)
            snk = P.sb("snk", [64, 12], F32)
            P.dma("sp", lambda e: e.dma_start(out=cols[:], in_=cols_d[l]), cols_d, cols)
            P.dma("sp", lambda e: e.dma_start(out=snk[:], in_=sink_d[l]), sink_d, snk)
            P.op("act", lambda e: e.activation(out=esink[:], in_=snk[:], func=AF.Exp), [snk], [esink])
            P.op("dve", lambda e: e.tensor_copy(out=eshi[:], in_=esink[:]), [esink], [eshi])
            P.op("dve", lambda e: e.tensor_tensor(out=eslo[:], in0=esink[:], in1=eshi[:], op=ALU.subtract), [esink, eshi], [eslo])
            for hh in range(12):
                P.op("dve", lambda e, hh=hh: e.tensor_scalar(out=esr[0:1, hh, :], in0=onesf[0:1, :], scalar1=eshi[0:1, hh:hh + 1], scalar2=None, op0=ALU.mult), [onesf, eshi], [esr])
                P.op("dve", lambda e, hh=hh: e.tensor_scalar(out=esr[32:33, hh, :], in0=onesf[32:33, :], scalar1=eslo[32:33, hh:hh + 1], scalar2=None, op0=ALU.mult), [onesf, eslo], [esr])
            for s_ in range(2):
                P.dma("sp", lambda e, s_=s_: e.dma_start(out=modcol[:, s_, :], in_=MODs[l][s_, 0:4096].rearrange("(c p) -> p c", p=128), allow_slow_non_contiguous=True), MODs[l], modcol)
            for s_ in range(2):
                P.op("dve", lambda e, s_=s_: e.tensor_scalar(out=nwt[:, s_, :], in0=modcol[:, s_, 16:32], scalar1=1.0, scalar2=None, op0=ALU.add), [modcol], [nwt])
                P.op("dve", lambda e, s_=s_: e.tensor_tensor(out=gcol[:, s_, :], in0=nwt[:, s_, :], in1=cols[:, 18:34], op=ALU.mult), [nwt, cols], [gcol])

    def phase_cast_ada(l):
        with P.scope():
            for j in make_bg(l, psb[7], CW=N_IN, store_q="act", load_q="alt", which=(("qb", "kvb") if (l == 0 and bg_next) else ("qb", "kvb", "out"))):
                j()
        ada_tail(l)

    def rope_tail(P_, qn, M, t0, n, obf, f32p, psC):
        pr = psC.next()
        P.op("pe", lambda e: e.matmul(pr[0:M, :n], lhsT=rotm[0:M, 0:M], rhs=qn[0:M, :n], start=True, stop=True), [rotm, qn], [pr])
        t1 = f32p.next()
        t2 = f32p.next()
        c0 = t0 - 256
        e1 = "dve" if M == 128 else "pool"
        P.op(e1, lambda e: e.tensor_tensor(out=t1[0:M, :n], in0=qn[0:M, :n], in1=cos_t[0:M, c0:c0 + n], op=ALU.mult), [qn, cos_t], [t1])
        P.op("dve", lambda e: e.tensor_tensor(out=t2[0:M, :n], in0=pr[0:M, :n], in1=sin_t[0:M, c0:c0 + n], op=ALU.mult), [pr, sin_t], [t2])
        o = obf.next()
        P.op(e1, lambda e: e.tensor_tensor(out=o[0:M, :n], in0=t1[0:M, :n], in1=t2[0:M, :n], op=ALU.add), [t1, t2], [o])
        return o

    def rstd_from(pss, M, n, scale, rsp):
        rs = rsp.next()
        P.op("act", lambda e: e.activation(out=rs[0:M, :n], in_=pss[0:M, :n], func=AF.Ln, scale=scale, bias=epsc[0:M, :]), [pss, epsc], [rs])
        P.op("act", lambda e: e.activation(out=rs[0:M, :n], in_=rs[0:M, :n], func=AF.Exp, scale=-0.5), [rs], [rs])
        return rs

    def phase_B(l):
        with P.scope():
            hT = P.sb("hT", [128, 16, T], BF16)
            hT2 = Buf(hT.t, "hT2")
            with P.scope():
                xts = Rot([P.sb("xt%d" % i, [128, D], F32) for i in range(2)])
                xns = Rot([P.sb("xn%d" % i, [128, D], F32) for i in range(2)])
                junk = P.sb("junk", [128, D], BF16)
                sss = Rot([P.sb("ss%d" % i, [128, 1], F32) for i in range(2)])
                pT = Rot(psb[0:4])
                for tt in range(18):
                    s = 1 if tt < 2 else 0
                    if l == 0:
                        src = ctx_d if tt < 2 else x_d
                    else:
                        src = XC1 if tt < 2 else X1
                    r0 = tt * 128 if tt < 2 else (tt - 2) * 128
                    xt = xts.next()
                    xn = xns.next()
                    ss = sss.next()
                    P.dma("sp", lambda e, xt=xt, src=src, r0=r0: e.dma_start(out=xt[:], in_=src[r0:r0 + 128, :]), src, xt)
                    P.op("act", lambda e, xt=xt, ss=ss: e.activation(out=junk[:], in_=xt[:], func=AF.Square, accum_out=ss[:]), [xt], [junk, ss])
                    P.op("act", lambda e, ss=ss: e.activation(out=ss[:], in_=ss[:], func=AF.Sqrt, scale=1.0 / D, bias=EPS), [ss], [ss])
                    P.op("dve", lambda e, ss=ss: e.reciprocal(out=ss[:], in_=ss[:]), [ss], [ss])
                    P.op("dve", lambda e, xt=xt, xn=xn, ss=ss: e.tensor_scalar(out=xn[:], in0=xt[:], scalar1=ss[:], scalar2=None, op0=ALU.mult), [xt, ss], [xn])
                    for g4 in range(4):
                        pb = pT.next()
                        for c4 in range(4):
                            c = g4 * 4 + c4
                            P.op("pe", lambda e, pb=pb, xn=xn, c=c, c4=c4: e.transpose(out=pb[:, c4 * 128:(c4 + 1) * 128], in_=xn[:, c * 128:(c + 1) * 128], identity=ident[:]), [xn, ident], [pb])
                        for c4 in range(4):
                            c = g4 * 4 + c4
                            if g4 % 2 == 0:
                                P.op("act", lambda e, pb=pb, c=c, c4=c4, s=s, tt=tt: e.activation(out=hT[:, c, tt * 128:(tt + 1) * 128], in_=pb[:, c4 * 128:(c4 + 1) * 128], func=AF.Identity, scale=gcol[:, s, c:c + 1], bias=modcol[:, s, c:c + 1]), [pb, gcol, modcol], [hT])
                            else:
                                P.op("dve", lambda e, pb=pb, c=c, c4=c4, s=s, tt=tt: e.tensor_scalar(out=hT[:, c, tt * 128:(tt + 1) * 128], in0=pb[:, c4 * 128:(c4 + 1) * 128], scalar1=gcol[:, s, c:c + 1], scalar2=modcol[:, s, c:c + 1], op0=ALU.mult, op1=ALU.add), [pb, gcol, modcol], [hT2])
            if stop == "B1":
                return
            with P.scope():
                wts = Rot([P.sb("wt%d" % i, [128, 16, 768], BF16) for i in range(2)])
                sqp = Rot([P.sb("sq%d" % i, [128, 512], BF16) for i in range(3)])
                rsp = Rot([P.sb("rs%d" % i, [128, 512], F32) for i in range(3)])
                obf = Rot([P.sb("ob%d" % i, [128, 512], BF16) for i in range(6)])
                f32p = Rot([P.sb("f32_%d" % i, [128, 512], F32) for i in range(4)])
                raw = P.sb("raw", [128, 6, 512], F32)
                psA = Rot(psb[0:3])
                psB = Rot(psb[3:5])
                psC = Rot(psb[5:7])

                wstg = Rot([P.sb("wstg%d" % i, [128, 16, 128], F32) for i in range(3)])
                wgroups = [(0, 512), (512, 512), (1024, 512), (1536, 768), (2304, 256), (2560, 256), (2816, 768),
                           (3584, 512), (4096, 64)] + [(4160 + i * 512, 512) for i in range(4)]
                wcache = {}
                wticks = []

                def issue_w(idx, spread):
                    col0, ncols = wgroups[idx]
                    wt = wts.next()
                    subs = [(c, min(128, ncols - c)) for c in range(0, ncols, 128)]
                    sts = {}

                    def dma(k):
                        c, cw = subs[k]
                        st = wstg.next()
                        sts[k] = st
                        P.dma("sp", lambda e: e.dma_start(out=st[:, :, :cw], in_=w_in_d[l, :, col0 + c:col0 + c + cw].rearrange("(c p) n -> p c n", p=128)), w_in_d, st)

                    def cast(k):
                        c, cw = subs[k]
                        st = sts[k]
                        P.op("dve", lambda e: e.tensor_copy(out=wt[:, :, c:c + cw], in_=st[:, :, :cw]), [st], [wt])

                    n = len(subs)
                    for t in range(n + 3):
                        def tick(t=t):
                            if t - 3 >= 0:
                                cast(t - 3)
                            if t < n:
                                dma(t)
                        if spread:
                            wticks.append(tick)
                        else:
                            tick()
                    wcache[idx] = wt

                def wtick():
                    if wticks:
                        wticks.pop(0)()

                def load_w(col0, ncols):
                    idx = [g[0] for g in wgroups].index(col0)
                    assert wgroups[idx][1] == ncols
                    while wticks:
                        wticks.pop(0)()
                    if idx not in wcache:
                        issue_w(idx, False)
                    wt = wcache[idx]
                    if idx + 1 < len(wgroups) and (idx + 1) not in wcache:
                        issue_w(idx + 1, True)
                    return wt

                def main_mm(wt, j, M, t0, n):
                    wtick()
                    pu = psA.next()
                    for kc in range(16):
                        P.op("pe", lambda e, kc=kc: e.matmul(pu[0:M, :n], lhsT=wt[:, kc, j * 128:j * 128 + M], rhs=hT[:, kc, t0:t0 + n], start=(kc == 0), stop=(kc == 15)), [wt, hT, hT2], [pu])
                    return pu

                def headnorm_group(col0, nch, gi, dst, do_rope, tbs=TB):
                    wt = load_w(col0, nch * 128)
                    q1 = []
                    q2 = []

                    def stage1(pu, sq, j, t0, n):
                        pss = psB.next()
                        P.op("pe", lambda e: e.matmul(pss[:, :n], lhsT=bd64[:], rhs=sq[:, :n], start=True, stop=True), [bd64, sq], [pss])
                        rs = rstd_from(pss, 128, n, 1.0 / 64, rsp)
                        o1 = obf.next()
                        P.op("dve", lambda e: e.scalar_tensor_tensor(out=o1[:, :n], in0=pu[:, :n], scalar=cols[:, gi:gi + 1], in1=rs[:, :n], op0=ALU.mult, op1=ALU.mult), [pu, cols, rs], [o1])
                        q2.append((o1, j, t0, n))

                    def stage2(o1, j, t0, n):
                        if do_rope and t0 >= 256:
                            o2 = rope_tail(P, o1, 128, t0, n, obf, f32p, psC)
                        else:
                            o2 = o1
                        P.dma("pool", lambda e: e.dma_start(out=dst[j * 128:(j + 1) * 128, t0:t0 + n], in_=o2[:, :n]), o2, dst)

                    for j in range(nch):
                        for (t0, n) in tbs:
                            pu = main_mm(wt, j, 128, t0, n)
                            sq = sqp.next()
                            P.op("act", lambda e, pu=pu, sq=sq, n=n: e.activation(out=sq[:, :n], in_=pu[:, :n], func=AF.Square), [pu], [sq])
                            if q2:
                                stage2(*q2.pop(0))
                            if q1:
                                stage1(*q1.pop(0))
                            q1.append((pu, sq, j, t0, n))
                    while q1 or q2:
                        if q2:
                            stage2(*q2.pop(0))
                        if q1:
                            stage1(*q1.pop(0))

                def allnorm_group(col0, nch, gi, dst, tbs=TB):
                    wt = load_w(col0, nch * 128)
                    for (t0, n) in tbs:
                        pss = psB.next()
                        pend = None
                        for j in range(nch):
                            pu = main_mm(wt, j, 128, t0, n)
                            sq = sqp.next()
                            P.op("act", lambda e, pu=pu, j=j, n=n: e.activation(out=raw[:, j, :n], in_=pu[:, :n], func=AF.Copy), [pu], [raw])
                            P.op("act", lambda e, pu=pu, sq=sq, n=n: e.activation(out=sq[:, :n], in_=pu[:, :n], func=AF.Square), [pu], [sq])
                            if pend is not None:
                                pj, psq = pend
                                P.op("pe", lambda e, pj=pj, psq=psq, n=n, pss=pss: e.matmul(pss[:, :n], lhsT=ones[:], rhs=psq[:, :n], start=(pj == 0), stop=False), [ones, psq], [pss])
                            pend = (j, sq)
                        pj, psq = pend
                        P.op("pe", lambda e, pj=pj, psq=psq, n=n, pss=pss: e.matmul(pss[:, :n], lhsT=ones[:], rhs=psq[:, :n], start=(pj == 0), stop=True), [ones, psq], [pss])
                        rs = rstd_from(pss, 128, n, 1.0 / (nch * 128), rsp)
                        for j in range(nch):
                            o = obf.next()
                            P.op("dve", lambda e, o=o, j=j, n=n, rs=rs: e.scalar_tensor_tensor(out=o[:, :n], in0=raw[:, j, :n], scalar=cols[:, gi + j:gi + j + 1], in1=rs[:, :n], op0=ALU.mult, op1=ALU.mult), [raw, cols, rs], [o])
                            P.dma("pool", lambda e, o=o, j=j, t0=t0, n=n: e.dma_start(out=dst[j * 128:(j + 1) * 128, t0:t0 + n], in_=o[:, :n]), o, dst)

                def v_group(col0, ncols, dst):
                    wt = load_w(col0, ncols)
                    for tt in range(18):
                        wtick()
                        pv = psA.next()
                        for kc in range(16):
                            P.op("pe", lambda e, kc=kc, tt=tt, pv=pv: e.matmul(pv[:, :ncols], lhsT=hT[:, kc, tt * 128:(tt + 1) * 128], rhs=wt[:, kc, :ncols], start=(kc == 0), stop=(kc == 15)), [hT, hT2, wt], [pv])
                        o = obf.next()
                        if tt % 2 == 0:
                            P.op("act", lambda e, o=o, pv=pv: e.activation(out=o[:, :ncols], in_=pv[:, :ncols], func=AF.Copy), [pv], [o])
                        else:
                            P.op("dve", lambda e, o=o, pv=pv: e.tensor_copy(out=o[:, :ncols], in_=pv[:, :ncols]), [pv], [o])
                        P.dma("pool", lambda e, o=o, tt=tt: e.dma_start(out=dst[tt * 128:(tt + 1) * 128, :], in_=o[:, :ncols]), o, dst)

                def kpe_group():
                    wt = load_w(4096, 64)
                    for (t0, n) in TB:
                        pu = main_mm(wt, 0, 64, t0, n)
                        o = f32p.next()
                        P.op("act", lambda e, o=o, pu=pu, n=n: e.activation(out=o[0:64, :n], in_=pu[0:64, :n], func=AF.Copy), [pu], [o])
                        P.dma("pool", lambda e, o=o, t0=t0, n=n: e.dma_start(out=KPE[:, t0:t0 + n], in_=o[0:64, :n]), o, KPE)

                def z_group():
                    for half in range(4):
                        wt = load_w(4160 + half * 512, 512)
                        for j in range(4):
                            for (t0, n) in qtb:
                                pu = main_mm(wt, j, 128, t0, n)
                                o = obf.next()
                                P.op("act", lambda e, o=o, pu=pu, n=n: e.activation(out=o[:, :n], in_=pu[:, :n], func=AF.Silu), [pu], [o])
                                r = (half * 4 + j) * 128
                                P.dma("pool", lambda e, o=o, r=r, t0=t0, n=n: e.dma_start(out=SZ[r:r + 128, t0:t0 + n], in_=o[:, :n]), o, SZ)

                qtb = TB[1:] if (l == nlayers - 1 and nlayers == 2) else TB
                headnorm_group(0, 4, 0, QAT, False, qtb)
                headnorm_group(512, 4, 1, KAT, False)
                v_group(1024, 512, VA)
                headnorm_group(1536, 6, 2, QBT, True, qtb)
                headnorm_group(2304, 2, 3, KBT, True)
                v_group(2560, 256, VB)
                allnorm_group(2816, 6, 4, CQN, qtb)
                allnorm_group(3584, 4, 10, CKVN)
                kpe_group()
                z_group()

    def phase_B3(l):
        with P.scope():
            wqb = P.sb("wqb", [128, 6, 1152], BF16)
            wkvb = P.sb("wkvb", [128, 4, 1536], BF16)
            P.dma("sp", lambda e: e.dma_start(out=wqb[:], in_=WQBs[l][:].rearrange("(c p) n -> p c n", p=128)), WQBs[l], wqb)
            P.dma("sp", lambda e: e.dma_start(out=wkvb[:], in_=WKVBs[l][:].rearrange("(c p) n -> p c n", p=128)), WKVBs[l], wkvb)
            cqs = Rot([P.sb("cq%d" % i, [128, 6, 512], BF16) for i in range(2)])
            ckvs = Rot([P.sb("ckv%d" % i, [128, 4, 512], BF16) for i in range(2)])
            kpes = Rot([P.sb("kpe%d" % i, [64, 512], F32) for i in range(2)])
            sqks = Rot([P.sb("sqk%d" % i, [64, 512], BF16) for i in range(2)])
            sqp = Rot([P.sb("sq%d" % i, [128, 512], BF16) for i in range(9)])
            rawp = Rot([P.sb("raw%d" % i, [128, 512], F32) for i in range(9)])
            rsp = Rot([P.sb("rs%d" % i, [128, 512], F32) for i in range(4)])
            obf = Rot([P.sb("ob%d" % i, [128, 512], BF16) for i in range(12)])
            f32p = Rot([P.sb("f32_%d" % i, [128, 512], F32) for i in range(4)])
            ovs = Rot([P.sb("ov%d" % i, [128, 768], BF16) for i in range(2)])
            psA = Rot(psb[0:4])
            psB = Rot(psb[4:6])
            psC = Rot(psb[6:8])
            qB = []
            qC = []

            def stageB(h, t0, n, kpe, sqk, rN, rR, rK, sqN, sqR, sqK):
                pq = psB.next()
                P.op("pe", lambda e: e.matmul(pq[:, :n], lhsT=ones[:], rhs=sqN[:, :n], start=True, stop=False), [ones, sqN], [pq])
                P.op("pe", lambda e: e.matmul(pq[:, :n], lhsT=ones[0:64, :], rhs=sqR[0:64, :n], start=False, stop=True), [ones, sqR], [pq])
                pk = psB.next()
                P.op("pe", lambda e: e.matmul(pk[:, :n], lhsT=ones[:], rhs=sqK[:, :n], start=True, stop=False), [ones, sqK], [pk])
                P.op("pe", lambda e: e.matmul(pk[:, :n], lhsT=ones[0:64, :], rhs=sqk[0:64, :n], start=False, stop=True), [ones, sqk], [pk])
                rq = rstd_from(pq, 128, n, 1.0 / 192, rsp)
                rk = rstd_from(pk, 128, n, 1.0 / 192, rsp)
                oN = obf.next(); oR = obf.next(); oK = obf.next(); oKR = obf.next()
                P.op("dve", lambda e: e.scalar_tensor_tensor(out=oN[:, :n], in0=rN[:, :n], scalar=cols[:, 14:15], in1=rq[:, :n], op0=ALU.mult, op1=ALU.mult), [rN, cols, rq], [oN])
                P.op("dve", lambda e: e.scalar_tensor_tensor(out=oR[0:64, :n], in0=rR[0:64, :n], scalar=cols[0:64, 15:16], in1=rq[0:64, :n], op0=ALU.mult, op1=ALU.mult), [rR, cols, rq], [oR])
                P.op("dve", lambda e: e.scalar_tensor_tensor(out=oK[:, :n], in0=rK[:, :n], scalar=cols[:, 16:17], in1=rk[:, :n], op0=ALU.mult, op1=ALU.mult), [rK, cols, rk], [oK])
                P.op("dve", lambda e: e.scalar_tensor_tensor(out=oKR[0:64, :n], in0=kpe[0:64, :n], scalar=cols[0:64, 17:18], in1=rk[0:64, :n], op0=ALU.mult, op1=ALU.mult), [kpe, cols, rk], [oKR])
                P.dma("act", lambda e: e.dma_start(out=QCN[h * 128:(h + 1) * 128, t0:t0 + n], in_=oN[:, :n]), oN, QCN)
                P.dma("act", lambda e: e.dma_start(out=KCN[h * 128:(h + 1) * 128, t0:t0 + n], in_=oK[:, :n]), oK, KCN)
                qC.append((h, t0, n, oR, oKR))

            def stageC(h, t0, n, oR, oKR):
                if t0 >= 256:
                    oR = rope_tail(P, oR, 64, t0, n, obf, f32p, psC)
                    oKR = rope_tail(P, oKR, 64, t0, n, obf, f32p, psC)
                P.dma("sp", lambda e: e.dma_start(out=QCR[h * 64:(h + 1) * 64, t0:t0 + n], in_=oR[0:64, :n]), oR, QCR)
                P.dma("sp", lambda e: e.dma_start(out=KCR[h * 64:(h + 1) * 64, t0:t0 + n], in_=oKR[0:64, :n]), oKR, KCR)

            def drain_one():
                if qC:
                    stageC(*qC.pop(0))
                if qB:
                    stageB(*qB.pop(0))

            for (t0, n) in TB:
                cq = cqs.next()
                ckv = ckvs.next()
                kpe = kpes.next()
                sqk = sqks.next()
                P.dma("sp", lambda e, cq=cq, t0=t0, n=n: e.dma_start(out=cq[:, :, :n], in_=CQN[:, t0:t0 + n].rearrange("(c p) t -> p c t", p=128)), CQN, cq)
                P.dma("sp", lambda e, ckv=ckv, t0=t0, n=n: e.dma_start(out=ckv[:, :, :n], in_=CKVN[:, t0:t0 + n].rearrange("(c p) t -> p c t", p=128)), CKVN, ckv)
                P.dma("sp", lambda e, kpe=kpe, t0=t0, n=n: e.dma_start(out=kpe[:, :n], in_=KPE[:, t0:t0 + n]), KPE, kpe)
                P.op("act", lambda e, kpe=kpe, sqk=sqk, n=n: e.activation(out=sqk[:, :n], in_=kpe[:, :n], func=AF.Square), [kpe], [sqk])
                for h in range(6):
                    pN = psA.next()
                    for kc in range(6):
                        P.op("pe", lambda e, kc=kc, pN=pN, h=h, cq=cq, n=n: e.matmul(pN[:, :n], lhsT=wqb[:, kc, h * 192:h * 192 + 128], rhs=cq[:, kc, :n], start=(kc == 0), stop=(kc == 5)), [wqb, cq], [pN])
                    pR = psA.next()
                    for kc in range(6):
                        P.op("pe", lambda e, kc=kc, pR=pR, h=h, cq=cq, n=n: e.matmul(pR[0:64, :n], lhsT=wqb[:, kc, h * 192 + 128:h * 192 + 192], rhs=cq[:, kc, :n], start=(kc == 0), stop=(kc == 5)), [wqb, cq], [pR])
                    pK = psA.next()
                    for kc in range(4):
                        P.op("pe", lambda e, kc=kc, pK=pK, h=h, ckv=ckv, n=n: e.matmul(pK[:, :n], lhsT=wkvb[:, kc, h * 256:h * 256 + 128], rhs=ckv[:, kc, :n], start=(kc == 0), stop=(kc == 3)), [wkvb, ckv], [pK])
                    sqN = sqp.next(); sqR = sqp.next(); sqK = sqp.next()
                    rN = rawp.next(); rR = rawp.next(); rK = rawp.next()
                    P.op("act", lambda e, sqN=sqN, pN=pN, n=n: e.activation(out=sqN[:, :n], in_=pN[:, :n], func=AF.Square), [pN], [sqN])
                    P.op("act", lambda e, rN=rN, pN=pN, n=n: e.activation(out=rN[:, :n], in_=pN[:, :n], func=AF.Copy), [pN], [rN])
                    P.op("act", lambda e, sqR=sqR, pR=pR, n=n: e.activation(out=sqR[0:64, :n], in_=pR[0:64, :n], func=AF.Square), [pR], [sqR])
                    P.op("act", lambda e, rR=rR, pR=pR, n=n: e.activation(out=rR[0:64, :n], in_=pR[0:64, :n], func=AF.Copy), [pR], [rR])
                    P.op("act", lambda e, sqK=sqK, pK=pK, n=n: e.activation(out=sqK[:, :n], in_=pK[:, :n], func=AF.Square), [pK], [sqK])
                    P.op("act", lambda e, rK=rK, pK=pK, n=n: e.activation(out=rK[:, :n], in_=pK[:, :n], func=AF.Copy), [pK], [rK])
                    drain_one()
                    qB.append((h, t0, n, kpe, sqk, rN, rR, rK, sqN, sqR, sqK))
                for ti in range(n // 128):
                    ov = ovs.next()
                    for half in range(2):
                        pV = psA.next()
                        for kc in range(4):
                            P.op("pe", lambda e, kc=kc, pV=pV, ckv=ckv, ti=ti, half=half: e.matmul(
                                pV[:, 0:384].rearrange("p (h x) -> p h x", x=128),
                                lhsT=ckv[:, kc, ti * 128:(ti + 1) * 128],
                                rhs=wkvb[:, kc, :].rearrange("p (h x) -> p h x", x=256)[:, 3 * half:3 * half + 3, 128:256],
                                start=(kc == 0), stop=(kc == 3)), [ckv, wkvb], [pV])
                        if half == 0:
                            P.op("act", lambda e, ov=ov, pV=pV: e.activation(out=ov[:, 0:384], in_=pV[:, 0:384], func=AF.Copy), [pV], [ov])
                        else:
                            P.op("dve", lambda e, ov=ov, pV=pV: e.tensor_copy(out=ov[:, 384:768], in_=pV[:, 0:384]), [pV], [ov])
                    P.dma("pool", lambda e, ov=ov, r=t0 + ti * 128: e.dma_start(out=VC[r:r + 128, :], in_=ov[:]), ov, VC)
            while qB or qC:
                drain_one()

    def phase_C(l, last):
        with P.scope():
            psS = Rot(psb[0:3])
            psO = Rot(psb[3:5])
            psD = Rot(psb[5:7])
            ptp = Rot([P.sb("pt%d" % i, [128, 512], BF16) for i in range(6)])
            rdp = Rot([P.sb("rd%d" % i, [128, 512], F32) for i in range(2)])
            tmp = Rot([P.sb("tm%d" % i, [128, 512], F32) for i in range(2)])
            szp = Rot([P.sb("sz%d" % i, [128, 512], BF16) for i in range(2)])
            yp = Rot([P.sb("y%d" % i, [128, 512], BF16) for i in range(2)])
            bg_jobs = make_bg(l + 1, psb[7], extra=[(l, ("out",))], load_q="act", casts_first=True) if (bg_next and not last) else []
            pipe = []
            nstep = [0]

            def push(fn):
                pipe.append(fn)
                if len(pipe) > 2:
                    pipe.pop(0)()
                nstep[0] += 1
                if bg_jobs and nstep[0] % 10 == 0:
                    bg_jobs.pop(0)()

            def flush():
                while pipe:
                    pipe.pop(0)()

            def finalize(pO, pD, M, n, row0, t0, sink_heads=None, three=False, fold=False):
                rd = rdp.next()
                if fold:
                    P.op("act", lambda e: e.activation(out=rd[0:64, :n], in_=pO[64:128, :n], func=AF.Ln), [pO], [rd])
                elif sink_heads is not None:
                    for g, hh in enumerate(sink_heads):
                        P.op("dve", lambda e, g=g, hh=hh: e.tensor_scalar(out=rd[0:M, g * 128:(g + 1) * 128], in0=pD[0:M, g * 128:(g + 1) * 128], scalar1=esink[0:M, hh:hh + 1], scalar2=None, op0=ALU.add), [pD, esink], [rd])
                    P.op("act", lambda e: e.activation(out=rd[0:M, :n], in_=rd[0:M, :n], func=AF.Ln), [rd], [rd])
                else:
                    P.op("act", lambda e: e.activation(out=rd[0:M, :n], in_=pD[0:M, :n], func=AF.Ln), [pD], [rd])
                P.op("act", lambda e: e.activation(out=rd[0:M, :n], in_=rd[0:M, :n], func=AF.Exp, scale=-1.0), [rd], [rd])
                szt = szp.next()
                if three:
                    P.dma("sp", lambda e: e.dma_start(out=szt[0:64, 0:384].rearrange("p (g t) -> p g t", g=3), in_=SZ[row0:row0 + 192, t0:t0 + 128].rearrange("(g d) t -> d g t", g=3)), SZ, szt)
                else:
                    P.dma("sp", lambda e: e.dma_start(out=szt[0:M, :n], in_=SZ[row0:row0 + M, t0:t0 + n]), SZ, szt)
                tm = tmp.next()
                P.op("dve", lambda e: e.tensor_tensor(out=tm[0:M, :n], in0=pO[0:M, :n], in1=rd[0:M, :n], op=ALU.mult), [pO, rd], [tm])
                y = yp.next()
                P.op("pool", lambda e: e.tensor_tensor(out=y[0:M, :n], in0=tm[0:M, :n], in1=szt[0:M, :n], op=ALU.mult), [tm, szt], [y])
                if three:
                    P.dma("pool", lambda e: e.dma_start(out=YT[row0:row0 + 192, t0:t0 + 128].rearrange("(g d) t -> d g t", g=3), in_=y[0:64, 0:384].rearrange("p (g t) -> p g t", g=3)), y, YT)
                else:
                    P.dma("pool", lambda e: e.dma_start(out=YT[row0:row0 + M, t0:t0 + n], in_=y[0:M, :n]), y, YT)

            def attend(keys, s_mm, v_of, M, n, scale, fin, fold=False, sink=None):
                pO = psO.next()
                pD = None if fold else psD.next()
                nk = len(keys)
                MM = 128 if fold else M

                def pv(i, pt):
                    kt = keys[i][0]
                    vb, vap = v_of(kt)
                    first = (i == 0)
                    if first and sink is not None:
                        P.op("pe", lambda e: e.matmul(pO[0:128, 0:384].rearrange("p (g t) -> p g t", g=3), lhsT=selr[0:33, :], rhs=esr[0:33, sink * 3:sink * 3 + 3, :], start=True, stop=False), [selr, esr], [pO])
                        first = False
                    P.op("pe", lambda e: e.matmul(pO[0:MM, :n], lhsT=vap, rhs=pt[:, :n], start=first, stop=(i == nk - 1)), [vb, pt], [pO])
                    if not fold:
                        P.op("pe", lambda e: e.matmul(pD[0:M, :n], lhsT=ones[:, 0:M], rhs=pt[:, :n], start=(i == 0), stop=(i == nk - 1)), [ones, pt], [pD])
                    if i == nk - 1:
                        fin(pO, pD)

                for i, (kt, mk) in enumerate(keys):
                    pS = psS.next()
                    s_mm(pS, kt)
                    pt = ptp.next()
                    P.op("act", lambda e, pS=pS, pt=pt: e.activation(out=pt[:, :n], in_=pS[:, :n], func=AF.Exp, scale=scale), [pS], [pt])
                    if mk is not None:
                        P.op("dve", lambda e, pt=pt, mk=mk: e.tensor_tensor(out=pt[:, :n], in0=pt[:, :n], in1=swamask[:, mk, :], op=ALU.mult), [pt, swamask], [pt])
                    push(lambda i=i, pt=pt: pv(i, pt))

            with P.scope():
                kNs = Rot([P.sb("kN%d" % i, [128, T], BF16) for i in range(2)])
                kRs = Rot([P.sb("kR%d" % i, [128, T], BF16) for i in range(2)])
                qNs = Rot([P.sb("qN%d" % i, [128, T], BF16) for i in range(2)])
                qRs = Rot([P.sb("qR%d" % i, [128, T], BF16) for i in range(2)])
                for bb in kRs.bufs + qRs.bufs:
                    P.op("pool", lambda e, bb=bb: e.memset(bb[64:128, :], 0.0), [], [bb])
                vs = Rot([P.sb("vc%d" % i, [128, 18, 128], BF16) for i in range(2)])
                for h in range(6):
                    kN = kNs.next(); kR = kRs.next(); qN = qNs.next(); qR = qRs.next(); v = vs.next()
                    P.dma("sp", lambda e, kN=kN, h=h: e.dma_start(out=kN[:], in_=KCN[h * 128:(h + 1) * 128, :]), KCN, kN)
                    P.dma("sp", lambda e, kR=kR, h=h: e.dma_start(out=kR[0:64, :], in_=KCR[h * 64:(h + 1) * 64, :]), KCR, kR)
                    P.dma("sp", lambda e, qN=qN, h=h: e.dma_start(out=qN[:], in_=QCN[h * 128:(h + 1) * 128, :]), QCN, qN)
                    P.dma("sp", lambda e, qR=qR, h=h: e.dma_start(out=qR[0:64, :], in_=QCR[h * 64:(h + 1) * 64, :]), QCR, qR)
                    P.dma("sp", lambda e, v=v, h=h: e.dma_start(out=v[:], in_=VC[:, h * 128:(h + 1) * 128].rearrange("(t p) d -> p t d", p=128)), VC, v)
                    blocks = [(t0, n, list(range(18))) for (t0, n) in TB[1:]]
                    if not last:
                        blocks.append((0, 256, [0, 1]))
                    for (t0, n, kts) in blocks:
                        def s_mm(pS, kt, t0=t0, n=n, kN=kN, kR=kR, qN=qN, qR=qR):
                            P.op("pe", lambda e: e.matmul(pS[:, :n], lhsT=kN[:, kt * 128:(kt + 1) * 128], rhs=qN[:, t0:t0 + n], start=True, stop=False), [kN, qN], [pS])
                            P.op("pe", lambda e: e.matmul(pS[:, :n], lhsT=kR[:, kt * 128:(kt + 1) * 128], rhs=qR[:, t0:t0 + n], start=False, stop=True), [kR, qR], [pS])
                        attend([(kt, None) for kt in kts], s_mm, lambda kt, v=v: (v, v[:, kt, :]), 128, n, 192 ** -0.5,
                               lambda pO, pD, n=n, h=h, t0=t0: finalize(pO, pD, 128, n, 1280 + h * 128, t0))
                flush()

            with P.scope():
                kTs = Rot([P.sb("kb%d" % i, [128, T], BF16) for i in range(2)])
                qs = Rot([P.sb("qb%d" % i, [128, 3, T], BF16) for i in range(2)])
                for bb in kTs.bufs:
                    P.op("pool", lambda e, bb=bb: e.memset(bb[64:128, :], 0.0), [], [bb])
                for bb in qs.bufs:
                    P.op("pool", lambda e, bb=bb: e.memset(bb[64:128, :, :], 0.0), [], [bb])
                vs = Rot([P.sb("vb%d" % i, [128, 18, 128], BF16) for i in range(2)])
                for vv in vs.bufs:
                    P.op("pool", lambda e, vv=vv: e.memset(vv[:, :, 64:128], 1.0), [], [vv])
                for kvh in range(4):
                    kT = kTs.next(); q = qs.next(); v = vs.next()
                    P.dma("sp", lambda e, kT=kT, kvh=kvh: e.dma_start(out=kT[0:64, :], in_=KBT[kvh * 64:(kvh + 1) * 64, :]), KBT, kT)
                    P.dma("sp", lambda e, q=q, kvh=kvh: e.dma_start(out=q[0:64, :, :], in_=QBT[kvh * 192:(kvh + 1) * 192, :].rearrange("(g d) t -> d g t", g=3)), QBT, q)
                    P.dma("sp", lambda e, v=v, kvh=kvh: e.dma_start(out=v[:, :, 0:64], in_=VB[:, kvh * 64:(kvh + 1) * 64].rearrange("(t p) d -> p t d", p=128)), VB, v)
                    qtiles = []
                    for qt in range(16):
                        keys = [(0, None), (1, None)]
                        if qt > 0:
                            keys.append((2 + qt - 1, 0))
                        keys.append((2 + qt, None))
                        if qt < 15:
                            keys.append((2 + qt + 1, 1))
                        qtiles.append((256 + qt * 128, keys))
                    if not last:
                        qtiles.append((0, [(0, None), (1, None)]))
                        qtiles.append((128, [(0, None), (1, None)]))
                    for (tok0, keys) in qtiles:
                        def s_mm(pS, kt, tok0=tok0, kT=kT, q=q):
                            P.op("pe", lambda e: e.matmul(pS[:, 0:384].rearrange("p (g t) -> p g t", g=3), lhsT=kT[:, kt * 128:(kt + 1) * 128], rhs=q[:, :, tok0:tok0 + 128], start=True, stop=True), [kT, q], [pS])
                        attend(keys, s_mm, lambda kt, v=v: (v, v[:, kt, :]), 64, 384, 0.125,
                               lambda pO, pD, kvh=kvh, tok0=tok0: finalize(pO, pD, 64, 384, 512 + kvh * 192, tok0, three=True, fold=True), fold=True, sink=kvh)
                flush()

            with P.scope():
                kTs = Rot([P.sb("ka%d" % i, [64, T], BF16) for i in range(2)])
                qs = Rot([P.sb("qa%d" % i, [64, T], BF16) for i in range(2)])
                v0s = Rot([P.sb("va0_%d" % i, [128, 18, 128], BF16) for i in range(2)])
                v1s = Rot([P.sb("va1_%d" % i, [128, 17, 128], BF16) for i in range(2)])
                for vv in v0s.bufs + v1s.bufs:
                    P.op("pool", lambda e, vv=vv: e.memset(vv[:, :, 64:128], 1.0), [], [vv])
                gfs = Rot([P.sb("gf%d" % i, [128, 896], F32) for i in range(2)])
                Gs = Rot([P.sb("G%d" % i, [128, 16, 64], BF16) for i in range(2)])
                for h in range(8):
                    kT = kTs.next(); q = qs.next(); v0 = v0s.next(); v1 = v1s.next(); gf = gfs.next(); G = Gs.next()
                    P.dma("sp", lambda e, kT=kT, h=h: e.dma_start(out=kT[:], in_=KAT[h * 64:(h + 1) * 64, :]), KAT, kT)
                    P.dma("sp", lambda e, q=q, h=h: e.dma_start(out=q[:], in_=QAT[h * 64:(h + 1) * 64, :]), QAT, q)
                    P.dma("sp", lambda e, v0=v0, h=h: e.dma_start(out=v0[:, :, 0:64], in_=VA[:, h * 64:(h + 1) * 64].rearrange("(t p) d -> p t d", p=128)), VA, v0)
                    P.dma("sp", lambda e, v1=v1, h=h: e.dma_start(out=v1[:, :, 0:64], in_=VA[64:64 + 17 * 128, h * 64:(h + 1) * 64].rearrange("(t p) d -> p t d", p=128)), VA, v1)
                    P.dma("sp", lambda e, gf=gf, h=h: e.dma_start(out=gf[:], in_=gb_d[l, :, h * 896:(h + 1) * 896]), gb_d, gf)
                    P.op("act", lambda e, gf=gf: e.activation(out=gf[:], in_=gf[:], func=AF.Exp), [gf], [gf])
                    for m in range(14):
                        P.op("dve", lambda e, gf=gf, G=G, m=m: e.tensor_tensor(out=G[:, m, :], in0=gf[:, m * 64:(m + 1) * 64], in1=namask[:], op=ALU.mult), [gf, namask], [G])
                    def pv_stage(pt, ri, rs_, pO, pD, fin, h=h, v0=v0, v1=v1):
                        for i in range(6):
                            if i < 2:
                                vb, vap = v0, v0[:, i, :]
                            elif rs_ % 2 == 0:
                                vb, vap = v0, v0[:, 2 + rs_ // 2 + (i - 2), :]
                            else:
                                vb, vap = v1, v1[:, (3 + rs_) // 2 + (i - 2), :]
                            P.op("pe", lambda e, i=i, vap=vap: e.matmul(pO[0:128, ri * 64:(ri + 1) * 64], lhsT=vap, rhs=pt[:, i * 64:(i + 1) * 64], start=(i == 0), stop=(i == 5)), [vb, pt], [pO])
                        if fin is not None:
                            finalize(pO, None, 64, 512, h * 64, fin, fold=True)

                    for rg in range(4):
                        pO = psO.next()
                        pD = None
                        for ri in range(8):
                            r = rg * 8 + ri
                            rs_ = min(max(r - 4, 0), 24)
                            m0 = rs_ - r + 7
                            tq = 256 + r * 64
                            ks = 256 + rs_ * 64
                            pS = psS.next()
                            for i in range(6):
                                k0 = i * 128 if i < 2 else ks + (i - 2) * 128
                                P.op("pe", lambda e, i=i, k0=k0, pS=pS, tq=tq, kT=kT, q=q: e.matmul(pS[:, i * 64:(i + 1) * 64], lhsT=kT[0:64, k0:k0 + 128], rhs=q[0:64, tq:tq + 64], start=True, stop=True), [kT, q], [pS])
                            pt = ptp.next()
                            P.op("act", lambda e, pS=pS, pt=pt: e.activation(out=pt[:, 0:384], in_=pS[:, 0:384], func=AF.Exp, scale=0.125), [pS], [pt])
                            P.op("dve", lambda e, pt=pt, G=G, m0=m0: e.tensor_tensor(out=pt[:, 128:384].rearrange("p (j c) -> p j c", c=64), in0=pt[:, 128:384].rearrange("p (j c) -> p j c", c=64), in1=G[:, m0:m0 + 8, :].rearrange("p (j two) c -> p j two c", two=2)[:, :, 0, :], op=ALU.mult), [pt, G], [pt])
                            push(lambda a=(pt, ri, rs_, pO, pD, (256 + rg * 512) if ri == 7 else None), f=pv_stage: f(*a))
                    if not last:
                        def s_mm(pS, kt, kT=kT, q=q):
                            P.op("pe", lambda e: e.matmul(pS[:, 0:256], lhsT=kT[0:64, kt * 128:(kt + 1) * 128], rhs=q[0:64, 0:256], start=True, stop=True), [kT, q], [pS])
                        attend([(0, None), (1, None)], s_mm, lambda kt, v0=v0: (v0, v0[:, kt, :]), 64, 256, 0.125,
                               lambda pO, pD, h=h: finalize(pO, pD, 64, 256, h * 64, 0, fold=True), fold=True)
                flush()
            while bg_jobs:
                bg_jobs.pop(0)()

    def phase_D(l, last):
        with P.scope():
            wout = P.sb("wout", [128, 16, D], BF16)
            gate_row = [P.sb("gate_l", [128, D], F32), P.sb("gate_c", [128, D], F32)]
            for s in range(2):
                P.dma("sp", lambda e, s=s: e.dma_start(out=gate_row[s][:], in_=MODs[l][s, 4096:6144].partition_broadcast(128)), MODs[l], gate_row[s])
            for cb in range(4):
                P.dma("sp", lambda e, cb=cb: e.dma_start(out=wout[:, :, cb * 512:(cb + 1) * 512], in_=WOUTs[l][:, cb * 512:(cb + 1) * 512].rearrange("(c p) n -> p c n", p=128)), WOUTs[l], wout)
            yts = Rot([P.sb("yT%d" % i, [128, 16, 512], BF16) for i in range(2)])
            xts = Rot([P.sb("xt%d" % i, [128, D], F32) for i in range(2)])
            ots = Rot([P.sb("ot%d" % i, [128, D], F32) for i in range(2)])
            tmp = Rot([P.sb("tm%d" % i, [128, 512], F32) for i in range(2)])
            psA = Rot(psb[0:4])
            blocks = list(TB[1:])
            if not last:
                blocks.append(TB[0])
            for (t0, n) in blocks:
                yt = yts.next()
                P.dma("sp", lambda e, yt=yt, t0=t0, n=n: e.dma_start(out=yt[:, :, :n], in_=YT[:, t0:t0 + n].rearrange("(c p) t -> p c t", p=128)), YT, yt)
                for ti in range(n // 128):
                    isctx = t0 < 256
                    r0 = (t0 + ti * 128) if isctx else (t0 - 256 + ti * 128)
                    if l == 0:
                        src = ctx_d if isctx else x_d
                    else:
                        src = XC1 if isctx else X1
                    if last:
                        dst = out_d
                    else:
                        dst = XC1 if isctx else X1
                    g = gate_row[1 if isctx else 0]
                    xt = xts.next()
                    ot = ots.next()
                    P.dma("sp", lambda e, xt=xt, src=src, r0=r0: e.dma_start(out=xt[:], in_=src[r0:r0 + 128, :]), src, xt)
                    for cb in range(4):
                        po = psA.next()
                        for kc in range(16):
                            P.op("pe", lambda e, kc=kc, po=po, yt=yt, ti=ti, cb=cb: e.matmul(po[:, :], lhsT=yt[:, kc, ti * 128:(ti + 1) * 128], rhs=wout[:, kc, cb * 512:(cb + 1) * 512], start=(kc == 0), stop=(kc == 15)), [yt, wout], [po])
                        tm = tmp.next()
                        P.op("dve", lambda e, tm=tm, po=po, g=g, cb=cb: e.tensor_tensor(out=tm[:], in0=po[:], in1=g[:, cb * 512:(cb + 1) * 512], op=ALU.mult), [po, g], [tm])
                        P.op("pool", lambda e, tm=tm, ot=ot, xt=xt, cb=cb: e.tensor_tensor(out=ot[:, cb * 512:(cb + 1) * 512], in0=tm[:], in1=xt[:, cb * 512:(cb + 1) * 512], op=ALU.add), [tm, xt], [ot])
                    P.dma("pool", lambda e, ot=ot, dst=dst, r0=r0: e.dma_start(out=dst[r0:r0 + 128, :], in_=ot[:]), ot, dst)

    bg_next = (nlayers == 2)
    for l in range(nlayers):
        last = (l == nlayers - 1) and nlayers == 2
        if l == 0 or not bg_next:
            phase_cast_ada(l)
        else:
            ada_tail(l)
        if stop == "ada":
            break
        phase_B(l)
        if stop in ("B1", "B"):
            break
        phase_B3(l)
        if stop == "B3":
            break
        phase_C(l, last)
        if stop == "C":
            break
        phase_D(l, last)
    P.barrier()
    P.emit()
    return nc


def _consts():
    ident = np.eye(128, dtype=np.float32)
    bd = np.zeros((128, 128), np.float32)
    bd[:64, :64] = 1
    bd[64:, 64:] = 1
    on = np.ones((128, 128), np.float32)
    r64 = np.zeros((64, 64), np.float32)
    for i in range(16):
        r64[i + 16, i] = -1
        r64[i, i + 16] = 1
        r64[i + 48, i + 32] = -1
        r64[i + 32, i + 48] = 1
    rot = np.zeros((128, 128), np.float32)
    rot[:64, :64] = r64
    rot[64:, 64:] = r64
    cmat = np.ascontiguousarray(np.stack([ident, bd, on, rot], axis=1))
    t = np.arange(2048)
    row = (t // 64).astype(np.float32)
    col = (t % 64).astype(np.float32)
    inv = (np.float32(10000.0) ** (-np.arange(16, dtype=np.float32) / np.float32(16))).astype(np.float32)
    ar = row[:, None] * inv
    ac = col[:, None] * inv
    ang = np.concatenate([ar, ar, ac, ac], -1)
    cos = np.cos(ang).astype(np.float32).T
    sin = np.sin(ang).astype(np.float32).T
    cossin = np.zeros((128, 2, 2048), np.float32)
    cossin[:64, 0] = cos
    cossin[64:, 0] = cos
    cossin[:64, 1] = sin
    cossin[64:, 1] = sin
    kc = np.arange(128) % 64
    c = np.arange(64)
    qs = np.clip(c - 8, 0, 48)
    namask = ((kc[:, None] >= qs[None, :]) & (kc[:, None] < qs[None, :] + 16)).astype(np.float32)
    i = np.arange(128)
    lo = (i[None, :] <= i[:, None]).astype(np.float32)
    hi = (i[:, None] <= i[None, :]).astype(np.float32)
    swamask = np.stack([np.tile(lo, (1, 3)), np.tile(hi, (1, 3))], axis=1).astype(np.float32)
    return cmat, cossin, namask, np.ascontiguousarray(swamask)


def _gather_rpb(rpb):
    p = np.arange(128)
    kc = p % 64
    half = p // 64
    c = np.arange(64)
    dc = np.clip(kc[:, None] - c[None, :], -15, 15) + 15
    m = np.arange(14)
    dr = m[None, :, None] + half[:, None, None]
    g = rpb[:, :, dr, dc[:, None, :]]
    g = np.transpose(g, (0, 2, 1, 3, 4)).reshape(2, 128, 8 * 14 * 64)
    return np.ascontiguousarray(g.astype(np.float32))


def _cols(norm_w, qn_a, kn_a, qn_b, kn_b, qa_norm, kva_norm, qn_c, kn_c):
    out = np.ones((2, 128, 34), np.float32)
    for l in range(2):
        out[l, :, 0] = np.tile(qn_a[l], 2)
        out[l, :, 1] = np.tile(kn_a[l], 2)
        out[l, :, 2] = np.tile(qn_b[l], 2)
        out[l, :, 3] = np.tile(kn_b[l], 2)
        out[l, :, 4:10] = qa_norm[l].reshape(6, 128).T
        out[l, :, 10:14] = kva_norm[l].reshape(4, 128).T
        out[l, :, 14] = qn_c[l][:128]
        out[l, :64, 15] = qn_c[l][128:]
        out[l, :, 16] = kn_c[l][:128]
        out[l, :64, 17] = kn_c[l][128:]
        out[l, :, 18:34] = norm_w[l].reshape(16, 128).T
    return out


def make_in_maps(x, c, ctx, c_ctx, norm_w, w_ada, b_ada, w_in, qn_a, kn_a, rpb_a, qn_b, kn_b, sink_b,
                 qa_norm, kva_norm, w_qb, w_kvb, qn_c, kn_c, w_out):
    f = lambda a: np.ascontiguousarray(np.asarray(a, dtype=np.float32))
    x, c, ctx, c_ctx = f(x), f(c), f(ctx), f(c_ctx)
    cmat, cossin, namask, swamask = _consts()
    cols = _cols(f(norm_w), f(qn_a), f(kn_a), f(qn_b), f(kn_b), f(qa_norm), f(kva_norm), f(qn_c), f(kn_c))
    gb = _gather_rpb(f(rpb_a))
    sink = np.ascontiguousarray(np.broadcast_to(f(sink_b)[:, None, :], (2, 64, 12)))
    shared = dict(w_ada=f(w_ada), b_ada=f(b_ada), w_in=f(w_in), w_qb=f(w_qb), w_kvb=f(w_kvb), w_out=f(w_out),
                  cols=cols, sink=sink, gb=gb, cmat=cmat, cossin=cossin, namask=namask, swamask=swamask)
    maps = []
    for b in range(8):
        cc = np.zeros((128, 16, 2), np.float32)
        cc[:, :, 0] = c[b].reshape(16, 128).T
        cc[:, :, 1] = c_ctx.reshape(16, 128).T
        m = dict(shared)
        m["x"] = x[b]
        m["ctx"] = ctx[b]
        m["cc"] = np.ascontiguousarray(cc.reshape(128, 32))
        maps.append(m)
    return maps


def kernel(**inputs):
    maps = make_in_maps(**inputs)
    nc = build(2)
    res = run_bass_kernel_spmd(nc, maps, core_ids=list(range(8)))
    return np.stack([np.asarray(r["out"], dtype=np.float32) for r in res.results], axis=0)
```
